# Optimizing a Trainium2 kernel written in Bass

```python
import jax, jax.numpy as jnp
from jax import lax
import numpy as np

D_MODEL = 1024
BATCH = 8
SEQ = 4096
DEPTH = 4
DEC_BATCH = 16
DEC_SEQ = 32
PAST_LEN = 2048

CHUNK = 64
N_MIXERS = 4
LAYERS_PER_MIXER = DEPTH // N_MIXERS
Q_BLOCK = 128
HEAD_DIM = 64
ROPE_THETA = 500000.0
NORM_EPS = 1e-6
A_HEADS = 16
A_Q_LORA = 384
A_KV_LORA = 256
A_NOPE = 64
A_ROPE = 32
A_V = 64
B_HEADS = 16
C_HEADS = 16
C_KV_HEADS = 4
C_WINDOW = 128
C_PREV_CHUNKS = C_WINDOW // CHUNK
C_ROT = HEAD_DIM // 4
D_HEADS = 16
D_PREV_CHUNKS = 8
D_PAST = D_PREV_CHUNKS * CHUNK
D_REL_CLIP = 128
FFN_HIDDEN = ((8 * D_MODEL // 3 + 255) // 256) * 256

kernel_name = 'hybrid_streaming_encoder_step'


def rmsnorm(x, g):
    x32 = x.astype(jnp.float32)
    y = x32 * lax.rsqrt(jnp.mean(x32 * x32, axis=-1, keepdims=True) + NORM_EPS)
    return (y * g.astype(jnp.float32)).astype(x.dtype)


def rope(x, pos, n_rot):
    half = n_rot // 2
    inv = ROPE_THETA ** (-jnp.arange(half, dtype=jnp.float32) * 2.0 / n_rot)
    ang = pos.astype(jnp.float32)[:, None] * inv[None, :]
    cos = jnp.cos(ang)[:, None, :]
    sin = jnp.sin(ang)[:, None, :]
    x32 = x.astype(jnp.float32)
    x1 = x32[..., :half]
    x2 = x32[..., half:n_rot]
    out = jnp.concatenate([x1 * cos - x2 * sin, x2 * cos + x1 * sin, x32[..., n_rot:]], axis=-1)
    return out.astype(x.dtype)


def adaln(c, w, b):
    return jnp.split(jax.nn.silu(c) @ w + b, 6, axis=-1)


def modulate(x, g, shift, scale):
    return rmsnorm(x, g) * (1 + scale[:, None, :]) + shift[:, None, :]


def flat_heads(o):
    return o.reshape(o.shape[:2] + (-1,))


def chunk_mask(q_pos, k_pos, n_prev):
    qc = q_pos[:, None] // CHUNK
    kc = k_pos[None, :] // CHUNK
    m = (kc <= qc) & (k_pos[None, :] >= 0)
    if n_prev is not None:
        m = m & (kc >= qc - n_prev)
    return m


def map_blocks(fn, xs, block):
    B, S = xs[0].shape[:2]
    nb = S // block
    blk = tuple(jnp.moveaxis(a.reshape((B, nb, block) + a.shape[2:]), 1, 0) for a in xs)
    out = lax.map(lambda args: fn(args[0], *args[1:]), (jnp.arange(nb),) + blk)
    return jnp.moveaxis(out, 0, 1).reshape((B, S) + out.shape[3:])


def mla_project(h, pos, w_down, g_q, g_kv, w_uq, w_uk):
    B, S, _ = h.shape
    cq, ckv, kr = jnp.split(h @ w_down, [A_Q_LORA, A_Q_LORA + A_KV_LORA], axis=-1)
    cq = rmsnorm(cq, g_q)
    ckv = rmsnorm(ckv, g_kv)
    q = (cq @ w_uq).reshape(B, S, A_HEADS, A_NOPE + A_ROPE)
    q_rope = rope(q[..., A_NOPE:], pos, A_ROPE)
    q_lat = jnp.einsum('bshn,chn->bshc', q[..., :A_NOPE], w_uk)
    k_rope = rope(kr[:, :, None, :], pos, A_ROPE)[:, :, 0, :]
    return q_lat, q_rope, ckv, k_rope


def mla_attend(q_lat, q_rope, ckv, k_rope, q_pos, k_pos, w_uv):
    s = jnp.einsum('bqhc,bkc->bhqk', q_lat, ckv) + jnp.einsum('bqhr,bkr->bhqk', q_rope, k_rope)
    s = s.astype(jnp.float32) * (A_NOPE + A_ROPE) ** -0.5
    s = jnp.where(chunk_mask(q_pos, k_pos, None), s, -jnp.inf)
    p = jax.nn.softmax(s, axis=-1).astype(ckv.dtype)
    o_lat = jnp.einsum('bhqk,bkc->bqhc', p, ckv)
    o = jnp.einsum('bqhc,chv->bqhv', o_lat, w_uv)
    return flat_heads(o)


def mixer_a(hp, hs, cache_ckv, cache_kr, past_len, w_down, g_q, g_kv, w_uq, w_uk, w_uv, w_o):
    S, T = hp.shape[1], hs.shape[1]
    pos_p = jnp.arange(S)
    ql, qr, ckv_p, kr_p = mla_project(hp, pos_p, w_down, g_q, g_kv, w_uq, w_uk)

    def blk(i, ql_b, qr_b):
        return mla_attend(ql_b, qr_b, ckv_p, kr_p, i * Q_BLOCK + jnp.arange(Q_BLOCK), pos_p, w_uv)

    op = map_blocks(blk, (ql, qr), Q_BLOCK)
    pos_s = past_len + jnp.arange(T)
    qls, qrs, ckv_s, kr_s = mla_project(hs, pos_s, w_down, g_q, g_kv, w_uq, w_uk)
    os_ = mla_attend(qls, qrs, jnp.concatenate([cache_ckv, ckv_s], axis=1),
                     jnp.concatenate([cache_kr, kr_s], axis=1), pos_s, jnp.arange(past_len + T), w_uv)
    return op @ w_o, os_ @ w_o, (ckv_p, kr_p, ckv_s, kr_s)


def stick_breaking(q, k, v, q_pos, k_pos):
    z = jnp.einsum('bqhd,bkhd->bhqk', q, k).astype(jnp.float32) * q.shape[-1] ** -0.5
    causal = k_pos[None, :] < q_pos[:, None]
    log_stay = jnp.where(causal, jax.nn.log_sigmoid(-z), 0.0)
    after = lax.cumsum(log_stay, axis=3, reverse=True) - log_stay
    a = jnp.where(causal, jnp.exp(jax.nn.log_sigmoid(z) + after), 0.0)
    return jnp.einsum('bhqk,bkhd->bqhd', a.astype(v.dtype), v)


def mixer_b(hp, hs, cache_k, cache_v, past_len, w_qkv, w_o):
    def proj(h):
        B, S, _ = h.shape
        return jnp.split((h @ w_qkv).reshape(B, S, 3 * B_HEADS, HEAD_DIM), 3, axis=2)

    S, T = hp.shape[1], hs.shape[1]
    qp, kp, vp = proj(hp)
    pos_p = jnp.arange(S)
    op = map_blocks(lambda i, qb: stick_breaking(qb, kp, vp, i * Q_BLOCK + jnp.arange(Q_BLOCK), pos_p),
                    (qp,), Q_BLOCK)
    qs, ks_, vs = proj(hs)
    os_ = stick_breaking(qs, jnp.concatenate([cache_k, ks_], axis=1), jnp.concatenate([cache_v, vs], axis=1),
                         past_len + jnp.arange(T), jnp.arange(past_len + T))
    return flat_heads(op) @ w_o, flat_heads(os_) @ w_o, (kp, vp, ks_, vs)


def softmax_attend(q, k, v, q_pos, k_pos, n_prev, rel_bias, sink):
    B, Q, H, d = q.shape
    K, G = k.shape[1], k.shape[2]
    R = H // G
    s = jnp.einsum('bqgrd,bkgd->bgrqk', q.reshape(B, Q, G, R, d), k)
    s = s.reshape(B, H, Q, K).astype(jnp.float32) * d ** -0.5
    if rel_bias is not None:
        rel = jnp.clip(q_pos[:, None] - k_pos[None, :], -D_REL_CLIP, D_REL_CLIP) + D_REL_CLIP
        s = s + rel_bias.astype(jnp.float32)[:, rel][None]
    s = jnp.where(chunk_mask(q_pos, k_pos, n_prev), s, -jnp.inf)
    if sink is None:
        p = jax.nn.softmax(s, axis=-1)
    else:
        sk = sink.astype(jnp.float32)[None, :, None, None]
        m = jnp.maximum(jnp.max(s, axis=-1, keepdims=True), sk)
        e = jnp.exp(s - m)
        p = e / (jnp.sum(e, axis=-1, keepdims=True) + jnp.exp(sk - m))
    o = jnp.einsum('bgrqk,bkgd->bqgrd', p.reshape(B, G, R, Q, K).astype(v.dtype), v)
    return o.reshape(B, Q, H, d)


def band_mixer(qp, kp, vp, qs, ks_, vs, cache_k, cache_v, past_len, n_prev, rel_bias, sink):
    def attend(q, k, v, q_pos, k_pos):
        return softmax_attend(q, k, v, q_pos, k_pos, n_prev, rel_bias, sink)

    S, T, Lc = qp.shape[1], qs.shape[1], cache_k.shape[1]
    pad = n_prev * CHUNK
    band = pad + CHUNK
    kpad = jnp.pad(kp, ((0, 0), (pad, 0), (0, 0), (0, 0)))
    vpad = jnp.pad(vp, ((0, 0), (pad, 0), (0, 0), (0, 0)))

    def blk(i, qb):
        start = i * CHUNK
        kb = lax.dynamic_slice_in_dim(kpad, start, band, axis=1)
        vb = lax.dynamic_slice_in_dim(vpad, start, band, axis=1)
        return attend(qb, kb, vb, start + jnp.arange(CHUNK), start - pad + jnp.arange(band))

    op = map_blocks(blk, (qp,), CHUNK)
    k_all = jnp.concatenate([cache_k, ks_], axis=1)
    v_all = jnp.concatenate([cache_v, vs], axis=1)
    os_ = attend(qs, k_all, v_all, past_len + jnp.arange(T), past_len - Lc + jnp.arange(Lc + T))
    keep = min(pad, S)
    return op, os_, (kp[:, S - keep:], vp[:, S - keep:], k_all[:, T:], v_all[:, T:])


def mixer_c(hp, hs, cache_k, cache_v, past_len, w_qkv, b_qkv, sink, w_o):
    def proj(h, pos):
        B, S, _ = h.shape
        qkv = (h @ w_qkv + b_qkv).reshape(B, S, C_HEADS + 2 * C_KV_HEADS, HEAD_DIM)
        q = rope(qkv[:, :, :C_HEADS], pos, C_ROT)
        k = rope(qkv[:, :, C_HEADS:C_HEADS + C_KV_HEADS], pos, C_ROT)
        return q, k, qkv[:, :, C_HEADS + C_KV_HEADS:]

    qp, kp, vp = proj(hp, jnp.arange(hp.shape[1]))
    qs, ks_, vs = proj(hs, past_len + jnp.arange(hs.shape[1]))
    op, os_, st = band_mixer(qp, kp, vp, qs, ks_, vs, cache_k, cache_v, past_len, C_PREV_CHUNKS, None, sink)
    return flat_heads(op) @ w_o, flat_heads(os_) @ w_o, st


def mixer_d(hp, hs, cache_k, cache_v, past_len, w_qkv, rel_bias, w_o):
    def proj(h):
        B, S, _ = h.shape
        return jnp.split((h @ w_qkv).reshape(B, S, 3 * D_HEADS, HEAD_DIM), 3, axis=2)

    qp, kp, vp = proj(hp)
    qs, ks_, vs = proj(hs)
    op, os_, st = band_mixer(qp, kp, vp, qs, ks_, vs, cache_k, cache_v, past_len, D_PREV_CHUNKS, rel_bias, None)
    return flat_heads(op) @ w_o, flat_heads(os_) @ w_o, st


def swiglu(h, w_in, w_out):
    g, u = jnp.split(h @ w_in, 2, axis=-1)
    return (jax.nn.silu(g) * u) @ w_out


def stack_states(entries):
    return tuple(jnp.stack(parts, axis=0) for parts in zip(*entries))


def setup_inputs(seed: int = 0) -> dict:
    key = jax.random.key(seed)
    ks = iter(jax.random.split(key, 48))

    def nrm(shape, scale=1.0):
        return jax.random.normal(next(ks), shape, jnp.float32) * scale

    L = LAYERS_PER_MIXER
    D = D_MODEL
    c_len = min(C_WINDOW, PAST_LEN)
    d_len = min(D_PAST, PAST_LEN)
    a_down = A_Q_LORA + A_KV_LORA + A_ROPE
    c_qkv = (C_HEADS + 2 * C_KV_HEADS) * HEAD_DIM
    return {
        'x_prompt': nrm((BATCH, SEQ, D)),
        'x_sample': nrm((DEC_BATCH, DEC_SEQ, D)),
        'c_prompt': nrm((BATCH, D)),
        'c_sample': nrm((DEC_BATCH, D)),
        'cache_a_ckv': nrm((L, DEC_BATCH, PAST_LEN, A_KV_LORA)),
        'cache_a_krope': nrm((L, DEC_BATCH, PAST_LEN, A_ROPE)),
        'cache_b_k': nrm((L, DEC_BATCH, PAST_LEN, B_HEADS, HEAD_DIM)),
        'cache_b_v': nrm((L, DEC_BATCH, PAST_LEN, B_HEADS, HEAD_DIM)),
        'cache_c_k': nrm((L, DEC_BATCH, c_len, C_KV_HEADS, HEAD_DIM)),
        'cache_c_v': nrm((L, DEC_BATCH, c_len, C_KV_HEADS, HEAD_DIM)),
        'cache_d_k': nrm((L, DEC_BATCH, d_len, D_HEADS, HEAD_DIM)),
        'cache_d_v': nrm((L, DEC_BATCH, d_len, D_HEADS, HEAD_DIM)),
        'w_mod': nrm((DEPTH, D, 6 * D), 0.5 * D ** -0.5),
        'b_mod': nrm((DEPTH, 6 * D), 0.02),
        'g_mix': 1.0 + nrm((DEPTH, D), 0.05),
        'g_ffn': 1.0 + nrm((DEPTH, D), 0.05),
        'w_ffn_in': nrm((DEPTH, D, 2 * FFN_HIDDEN), D ** -0.5),
        'w_ffn_out': nrm((DEPTH, FFN_HIDDEN, D), FFN_HIDDEN ** -0.5),
        'w_a_down': nrm((L, D, a_down), D ** -0.5),
        'g_a_q': 1.0 + nrm((L, A_Q_LORA), 0.05),
        'g_a_kv': 1.0 + nrm((L, A_KV_LORA), 0.05),
        'w_a_uq': nrm((L, A_Q_LORA, A_HEADS * (A_NOPE + A_ROPE)), A_Q_LORA ** -0.5),
        'w_a_uk': nrm((L, A_KV_LORA, A_HEADS, A_NOPE), A_KV_LORA ** -0.5),
        'w_a_uv': nrm((L, A_KV_LORA, A_HEADS, A_V), A_KV_LORA ** -0.5),
        'w_a_o': nrm((L, A_HEADS * A_V, D), (A_HEADS * A_V) ** -0.5),
        'w_b_qkv': nrm((L, D, 3 * B_HEADS * HEAD_DIM), D ** -0.5),
        'w_b_o': nrm((L, B_HEADS * HEAD_DIM, D), (B_HEADS * HEAD_DIM) ** -0.5),
        'w_c_qkv': nrm((L, D, c_qkv), D ** -0.5),
        'b_c_qkv': nrm((L, c_qkv), 0.02),
        'sink_c': nrm((L, C_HEADS)),
        'w_c_o': nrm((L, C_HEADS * HEAD_DIM, D), (C_HEADS * HEAD_DIM) ** -0.5),
        'w_d_qkv': nrm((L, D, 3 * D_HEADS * HEAD_DIM), D ** -0.5),
        'rel_bias_d': nrm((L, D_HEADS, 2 * D_REL_CLIP + 1), 0.5),
        'w_d_o': nrm((L, D_HEADS * HEAD_DIM, D), (D_HEADS * HEAD_DIM) ** -0.5),
        'g_final': 1.0 + nrm((D,), 0.05),
    }


def reference(x_prompt, x_sample, c_prompt, c_sample, cache_a_ckv, cache_a_krope, cache_b_k, cache_b_v,
              cache_c_k, cache_c_v, cache_d_k, cache_d_v, w_mod, b_mod, g_mix, g_ffn, w_ffn_in, w_ffn_out,
              w_a_down, g_a_q, g_a_kv, w_a_uq, w_a_uk, w_a_uv, w_a_o, w_b_qkv, w_b_o,
              w_c_qkv, b_c_qkv, sink_c, w_c_o, w_d_qkv, rel_bias_d, w_d_o, g_final):
    past_len = cache_a_ckv.shape[2]
    xp, xs = x_prompt, x_sample
    states = [[], [], [], []]
    for i in range(DEPTH):
        m, j = i % N_MIXERS, i // N_MIXERS
        mp = adaln(c_prompt, w_mod[i], b_mod[i])
        ms = adaln(c_sample, w_mod[i], b_mod[i])
        hp = modulate(xp, g_mix[i], mp[0], mp[1])
        hs = modulate(xs, g_mix[i], ms[0], ms[1])
        if m == 0:
            op, os_, st = mixer_a(hp, hs, cache_a_ckv[j], cache_a_krope[j], past_len, w_a_down[j], g_a_q[j],
                                  g_a_kv[j], w_a_uq[j], w_a_uk[j], w_a_uv[j], w_a_o[j])
        elif m == 1:
            op, os_, st = mixer_b(hp, hs, cache_b_k[j], cache_b_v[j], past_len, w_b_qkv[j], w_b_o[j])
        elif m == 2:
            op, os_, st = mixer_c(hp, hs, cache_c_k[j], cache_c_v[j], past_len, w_c_qkv[j], b_c_qkv[j],
                                  sink_c[j], w_c_o[j])
        else:
            op, os_, st = mixer_d(hp, hs, cache_d_k[j], cache_d_v[j], past_len, w_d_qkv[j], rel_bias_d[j], w_d_o[j])
        states[m].append(st)
        xp = xp + mp[2][:, None, :] * op
        xs = xs + ms[2][:, None, :] * os_
        hp = modulate(xp, g_ffn[i], mp[3], mp[4])
        hs = modulate(xs, g_ffn[i], ms[3], ms[4])
        xp = xp + mp[5][:, None, :] * swiglu(hp, w_ffn_in[i], w_ffn_out[i])
        xs = xs + ms[5][:, None, :] * swiglu(hs, w_ffn_in[i], w_ffn_out[i])
    y_prompt = rmsnorm(xp, g_final)
    y_sample = rmsnorm(xs, g_final)
    a_ckv_p, a_kr_p, a_ckv_s, a_kr_s = stack_states(states[0])
    b_k_p, b_v_p, b_k_s, b_v_s = stack_states(states[1])
    c_k_p, c_v_p, c_k_s, c_v_s = stack_states(states[2])
    d_k_p, d_v_p, d_k_s, d_v_s = stack_states(states[3])
    return (y_prompt, y_sample, a_ckv_p, a_kr_p, a_ckv_s, a_kr_s, b_k_p, b_v_p, b_k_s, b_v_s,
            c_k_p, c_v_p, c_k_s, c_v_s, d_k_p, d_v_p, d_k_s, d_v_s)
```

```python
import numpy as np
import concourse.bass as bass
import concourse.mybir as mybir
from concourse.bass_utils import run_bass_kernel_spmd

F32 = mybir.dt.float32
BF16 = mybir.dt.bfloat16
AF = mybir.ActivationFunctionType
ALU = mybir.AluOpType

NDS = 40


class Dep:
    __slots__ = ("w", "r")

    def __init__(self):
        self.w = None
        self.r = {}


class Eng:
    def __init__(self, nc, eng, name):
        self.e = eng
        self.key = name
        self.sem = nc.alloc_semaphore(name)
        self.n = 0
        self.seen = {}


class K:
    def __init__(self, nc):
        self.nc = nc
        self.pe = Eng(nc, nc.tensor, "s_pe")
        self.act = Eng(nc, nc.scalar, "s_act")
        self.dve = Eng(nc, nc.vector, "s_dve")
        self.pool = Eng(nc, nc.gpsimd, "s_pool")
        self.sp = Eng(nc, nc.sync, "s_sp")
        self.engs = [self.pe, self.act, self.dve, self.pool, self.sp]
        self.dsems = [nc.alloc_semaphore("s_d%d" % i) for i in range(NDS)]
        self.dval = [0] * NDS
        self.dnext = 0
        self.ninst = 0

    def _wait(self, E, ev):
        k, s, v = ev
        if E.seen.get(k, 0) >= v:
            return
        E.e.wait_ge(s, v)
        E.seen[k] = v

    def _deps(self, E, reads, writes, skip_self=False):
        for b in reads:
            if b.w is not None and not (skip_self and b.w[0] == E.key):
                self._wait(E, b.w)
        for b in writes:
            if b.w is not None and not (skip_self and b.w[0] == E.key):
                self._wait(E, b.w)
            for k, (s, v) in b.r.items():
                if skip_self and k == E.key:
                    continue
                self._wait(E, (k, s, v))

    def _mark(self, ev, reads, writes):
        k, s, v = ev
        for b in reads:
            o = b.r.get(k)
            if o is None or o[1] < v:
                b.r[k] = (s, v)
        for b in writes:
            b.w = ev
            b.r = {}

    def op(self, E, fn, reads=(), writes=(), inc=True, skip_self=False):
        self._deps(E, reads, writes, skip_self)
        ins = fn()
        self.ninst += 1
        if inc:
            E.n += 1
            ins.then_inc(E.sem, 1)
            ev = (E.key, E.sem, E.n)
        else:
            ev = (E.key, E.sem, E.n + 1)
        self._mark(ev, reads, writes)
        return ins

    def dma(self, Q, out, in_, reads=(), writes=(), **kw):
        i = self.dnext
        self.dnext = (i + 1) % NDS
        key = "d%d" % i
        if self.dval[i] > 0:
            self._wait(Q, (key, self.dsems[i], self.dval[i]))
        self._deps(Q, reads, writes)
        self.dval[i] += 16
        Q.e.dma_start(out=out, in_=in_, **kw).then_inc(self.dsems[i], 16)
        self.ninst += 1
        self._mark((key, self.dsems[i], self.dval[i]), reads, writes)

    def barrier(self):
        sp = self.sp
        for i in range(NDS):
            if self.dval[i] > 0:
                self._wait(sp, ("d%d" % i, self.dsems[i], self.dval[i]))
        for E in self.engs:
            if E is not sp and E.n > 0:
                self._wait(sp, (E.key, E.sem, E.n))
        sp.n += 1
        sp.e.sem_inc(sp.sem, 1)
        for E in self.engs:
            if E is not sp:
                self._wait(E, (sp.key, sp.sem, sp.n))
                for F in self.engs:
                    E.seen[F.key] = max(E.seen.get(F.key, 0), F.n)
                for i in range(NDS):
                    E.seen["d%d" % i] = self.dval[i]

from contextlib import ExitStack

D = 1024
NCH = 8
T_S = 32
HID = 2816
HCH = 22
EPS = 1e-6


def build(S, P, depth=4):
    nc = bass.Bass("TRN2", target_bir_lowering=False)
    k = K(nc)
    TT = S + 2 * T_S
    NPT = S // 512
    tiles = [(t * 512, 512, 0) for t in range(NPT)] + [(S, T_S, 1), (S + T_S, T_S, 2)]
    LC = [P, P, min(128, P), min(512, P)]
    NKV = [16, 16, 4, 16]

    def din(name, shape, dt=F32):
        return nc.dram_tensor(name, list(shape), dt, kind="ExternalInput").ap()

    def dout(name, shape, dt=F32):
        return nc.dram_tensor(name, list(shape), dt, kind="ExternalOutput").ap()

    def dscr(name, shape, dt):
        return nc.dram_tensor(name, list(shape), dt, kind="Internal").ap()

    I = {}
    for nm, shp in [("xT", [D, TT]), ("cT", [128, NCH, 3]), ("w_mod", [4, D, 6 * D]), ("bmodT", [128, 4, 48]),
                    ("gmixT", [128, 4, NCH]), ("gffnT", [128, 4, NCH]), ("gfinT", [128, NCH]),
                    ("w_ffn_in", [4, D, 2 * HID]), ("w_ffn_out", [4, HID, D]), ("ident", [128, 128]),
                    ("tri", [128, 128]), ("antiI", [128, 128]),
                    ("w_a_down", [D, 672]), ("gaqT", [128, 3]), ("gakvT", [128, 2]), ("w_a_uq", [384, 1536]),
                    ("w_a_uk", [256, 1024]), ("w_a_uv", [256, 1024]), ("w_a_o", [D, D]),
                    ("cache_a_ckv", [2, P, 256]), ("cache_a_kr", [2, P, 32]),
                    ("ropeA_cos", [128, TT]), ("ropeA_sin", [128, TT]), ("maskA", [4, 128, 512]),
                    ("w_b_qkv", [D, 3 * D]), ("w_b_o", [D, D]), ("cache_b_k", [2, P, D]), ("cache_b_v", [2, P, D]),
                    ("maskB", [4, 128, 512]), ("maskBp", [4, 128, 512]), ("maskBs", [32, 512]), ("maskBsp", [32, 512]),
                    ("w_c_qkv", [D, 1536]), ("bcT", [128, 12]), ("bcswT", [128, 12]), ("bc_row", [1, 1536]), ("sink_c", [1, 16]),
                    ("w_c_o", [D, D]),
                    ("cache_c_k", [2, LC[2], 256]), ("cache_c_v", [2, LC[2], 256]),
                    ("ropeC_cos", [128, TT]), ("ropeC_sin", [128, TT]), ("maskC", [5, 128, 512]),
                    ("w_d_qkv", [D, 3 * D]), ("rel_bias_d", [16, 257]), ("w_d_o", [D, D]),
                    ("cache_d_k", [2, LC[3], D]), ("cache_d_v", [2, LC[3], D]), ("maskD", [8, 128, 512])]:
        I[nm] = din(nm, shp)
    O = {}
    for nm, shp in [("yT", [D, TT]), ("a_ckv", [TT, 256]), ("a_kr", [TT, 32]), ("b_k", [TT, D]), ("b_v", [TT, D]),
                    ("c_k_p", [min(128, S), 256]), ("c_v_p", [min(128, S), 256]),
                    ("c_k_s", [2, LC[2], 256]), ("c_v_s", [2, LC[2], 256]),
                    ("d_k_p", [min(512, S), D]), ("d_v_p", [min(512, S), D]),
                    ("d_k_s", [2, LC[3], D]), ("d_v_s", [2, LC[3], D])]:
        O[nm] = dout(nm, shp)

    xres = dscr("xres", [D, TT], F32)
    dxres = [Dep() for _ in tiles]
    attnT = dscr("attnT", [D, TT], BF16)
    dattn = Dep()
    TTX = TT + 2 * P
    qT_s = dscr("qT_s", [16 * 96, TT], BF16)
    kT_s = dscr("kT_s", [D + 32, TTX], BF16)
    v_s = dscr("v_s", [TTX, D], BF16)
    dq = Dep(); dkk = Dep(); dv = Dep()
    ext_s = dscr("ext_s", [16, 1536], F32)

    cur = [None]
    uid = [0]

    def sb(name, shape, dt):
        uid[0] += 1
        if cur[0] is None:
            return nc.alloc_sbuf_tensor("%s_%d" % (name, uid[0]), list(shape), dt)
        return cur[0].enter_context(nc.sbuf_tensor("%s_%d" % (name, uid[0]), list(shape), dt))

    ones_bf = sb("ones_bf", [128, 128], BF16); d_const = Dep()
    ident = sb("ident_s", [128, 128], F32)
    mod = sb("mod", [128, 4, 48, 3], F32); d_mod = Dep()
    amix = sb("amix", [128, 4, NCH, 3], F32)
    affn = sb("affn", [128, 4, NCH, 3], F32)
    gfin = sb("gfin", [128, NCH], F32)
    psum = [nc.alloc_psum_tensor("ps%d" % i, [128, 512], F32) for i in range(8)]
    dps = [Dep() for _ in range(8)]

    k.op(k.pool, lambda: nc.gpsimd.memset(ones_bf[:], 1.0), writes=[d_const])
    k.dma(k.sp, ident[:], I["ident"], writes=[d_const])
    k.dma(k.sp, gfin[:], I["gfinT"], writes=[d_mod])

    STQ = [k.act]
    cast_rr = [0]

    def cast(out, in_, reads, writes):
        cast_rr[0] ^= 1
        if cast_rr[0]:
            k.op(k.dve, lambda: nc.vector.tensor_copy(out=out, in_=in_), reads=reads, writes=writes)
        else:
            k.op(k.pool, lambda: nc.gpsimd.tensor_copy(out=out, in_=in_), reads=reads, writes=writes)

    def mm(out, lhsT, rhs, start, stop, reads, writes, last=True):
        k.op(k.pe, lambda: nc.tensor.matmul(out, lhsT=lhsT, rhs=rhs, start=start, stop=stop),
             reads=reads, writes=writes, inc=last, skip_self=True)

    def mmk(ps, dp, w, wcol0, M, rhs_of, kc, n, reads, pbase=0):
        for kk in range(kc):
            mm(ps[pbase:pbase + M, :n], w[:, kk, wcol0:wcol0 + M], rhs_of(kk), kk == 0, kk == kc - 1,
               reads, [dp], last=(kk == kc - 1))

    ld_rr = [0]

    def load_w(stg, dstg, dst, ddst, W, kc, ncols, dst_col0=0):
        for k0 in range(0, kc, 2):
            kk = min(2, kc - k0)
            for c0 in range(0, ncols, 2048):
                nn = min(2048, ncols - c0)
                i = ld_rr[0] % len(stg); ld_rr[0] += 1
                view = stg[i][:, :, :].rearrange("p a b -> p (a b)")[:, 0:kk * nn].rearrange("p (k n) -> p k n", n=nn)
                src = W[k0 * 128:(k0 + kk) * 128, c0:c0 + nn].rearrange("(k p) n -> p k n", p=128)
                k.dma(k.sp, view, src, writes=[dstg[i]])
                cast(dst[:, k0:k0 + kk, dst_col0 + c0:dst_col0 + c0 + nn], view,
                     reads=[dstg[i]], writes=[ddst])

    def load_w_gen(stg, dstg, dst, ddst, W, kc, ncols):
        for k0 in range(0, kc, 2):
            kk = min(2, kc - k0)
            for c0 in range(0, ncols, 2048):
                nn = min(2048, ncols - c0)
                i = ld_rr[0] % len(stg); ld_rr[0] += 1
                view = stg[i][:, :, :].rearrange("p a b -> p (a b)")[:, 0:kk * nn].rearrange("p (k n) -> p k n", n=nn)
                src = W[k0 * 128:(k0 + kk) * 128, c0:c0 + nn].rearrange("(k p) n -> p k n", p=128)
                k.dma(k.sp, view, src, writes=[dstg[i]])
                cast(dst[:, k0:k0 + kk, c0:c0 + nn], view, reads=[dstg[i]], writes=[ddst])
                yield

    def mkstg(n=2):
        return [sb("stg", [128, NCH, 512], F32) for _ in range(n)], [Dep() for _ in range(n)]

    sc = sb("sc", [128, NCH, 3], F32); dsc = Dep()
    bm = sb("bm", [128, 4, 48], F32); dbm = Dep()
    gm = sb("gm", [128, 4, NCH], F32)
    gf = sb("gf", [128, 4, NCH], F32)
    k.dma(k.sp, sc[:], I["cT"], writes=[dsc])
    k.dma(k.sp, bm[:], I["bmodT"], writes=[dbm])
    k.dma(k.sp, gm[:], I["gmixT"], writes=[dbm])
    k.dma(k.sp, gf[:], I["gffnT"], writes=[dbm])
    k.op(k.act, lambda: nc.scalar.activation(out=sc[:], in_=sc[:], func=AF.Silu), reads=[dsc], writes=[dsc])

    sc_bf = sb("sc_bf", [128, NCH, 3], BF16)
    k.op(k.dve, lambda: nc.vector.tensor_copy(out=sc_bf[:], in_=sc[:]), reads=[dsc], writes=[dsc])

    class AdaBg:
        def __init__(self, layers, nbuf, banks):
            self.layers = [l for l in layers if l < depth]
            self.nbuf = nbuf
            self.nblk = 12 * len(self.layers)
            if self.nblk:
                self.stg = [sb("adstg", [128, NCH, 512], F32) for _ in range(nbuf)]
                self.dstg = [Dep() for _ in range(nbuf)]
                self.wb = [sb("adwb", [128, NCH, 512], BF16) for _ in range(nbuf)]
                self.dwb = [Dep() for _ in range(nbuf)]
                self.row = [sb("adrow", [3, 512], F32) for _ in range(2)]
                self.drow = [Dep(), Dep()]
            self.bR, self.bT = banks
            self.bl = 0; self.bc = 0

        def load(self, n):
            for _ in range(n):
                if self.bl >= self.nblk:
                    return
                i = self.bl % self.nbuf
                l = self.layers[self.bl // 12]; blk = self.bl % 12
                src = I["w_mod"][l, :, blk * 512:(blk + 1) * 512].rearrange("(k p) n -> p k n", p=128)
                k.dma(k.sp, self.stg[i][:], src, writes=[self.dstg[i]])
                k.op(k.dve, lambda i=i: nc.vector.tensor_copy(out=self.wb[i][:], in_=self.stg[i][:]), reads=[self.dstg[i]], writes=[self.dwb[i]])
                self.bl += 1

        def compute(self):
            while self.bc < self.bl:
                i = self.bc % self.nbuf
                li = self.bc // 12; blk = self.bc % 12
                pR = psum[self.bR]; dR = dps[self.bR]
                pT = psum[self.bT]; dT = dps[self.bT]
                for kk in range(NCH):
                    mm(pR[0:3, 0:512], sc_bf[:, kk, :], self.wb[i][:, kk, :], kk == 0, kk == NCH - 1, [self.dwb[i], dsc], [dR], last=(kk == NCH - 1))
                r = self.bc % 2
                k.op(k.dve, lambda r=r: nc.vector.tensor_copy(out=self.row[r][:, :], in_=pR[0:3, 0:512]), reads=[dR], writes=[self.drow[r]])
                for m in range(4):
                    j = blk * 4 + m
                    mm(pT[:, j * 3:(j + 1) * 3], self.row[r][0:3, m * 128:(m + 1) * 128], ident[0:3, 0:3], True, True, [self.drow[r], d_const], [dT])
                self.bc += 1
                if blk == 11:
                    self.evac(li)

        def evac(self, li):
            l = self.layers[li]
            ps = psum[self.bT]; dp = dps[self.bT]
            pv = ps[:, 0:144].rearrange("p (j s) -> p j s", s=3)
            for s_ in range(3):
                k.op(k.dve, lambda s_=s_: nc.vector.tensor_tensor(out=mod[:, l, :, s_], in0=pv[:, :, s_], in1=bm[:, l, :], op=ALU.add),
                     reads=[dp, dbm], writes=[d_mod])
            for s_ in range(3):
                k.op(k.dve, lambda s_=s_: nc.vector.scalar_tensor_tensor(
                    out=amix[:, l, :, s_], in0=mod[:, l, 8:16, s_], scalar=1.0, in1=gm[:, l, :], op0=ALU.add, op1=ALU.mult),
                    reads=[d_mod, dbm], writes=[d_mod])
                k.op(k.dve, lambda s_=s_: nc.vector.scalar_tensor_tensor(
                    out=affn[:, l, :, s_], in0=mod[:, l, 32:40, s_], scalar=1.0, in1=gf[:, l, :], op0=ALU.add, op1=ALU.mult),
                    reads=[d_mod, dbm], writes=[d_mod])

        def step(self, n):
            self.compute()
            self.load(n)

        def finish(self):
            while self.bc < self.nblk:
                self.load(self.nbuf)
                self.compute()

    def phase_adaln0():
        with ExitStack() as es:
            cur[0] = es
            bg = AdaBg([0], 3, [0, 1])
            bg.finish()
            k.barrier()
        cur[0] = None

    class NormBufs:
        def __init__(self):
            self.xt = sb("xt", [128, NCH, 512], F32); self.dxt = Dep()
            self.hT = sb("hT", [128, NCH, 512], BF16); self.dhT = Dep()
            self.sq = [sb("sq", [128, 512], BF16) for _ in range(2)]; self.dsq = [Dep(), Dep()]
            self.tmp = [sb("tmp", [128, 512], F32) for _ in range(2)]; self.dtmp = [Dep(), Dep()]
            self.rs = sb("rs", [128, 512], F32); self.drs = Dep()

    def rstd(nb, x_of, dx, nfe, n, ps_i=7):
        ps = psum[ps_i]; dp = dps[ps_i]
        for c in range(nfe):
            s = c % 2
            k.op(k.act, lambda c=c, s=s: nc.scalar.activation(out=nb.sq[s][:, :n], in_=x_of(c), func=AF.Square),
                 reads=[dx], writes=[nb.dsq[s]])
            mm(ps[:, :n], ones_bf[:], nb.sq[s][:, :n], c == 0, c == nfe - 1, [d_const, nb.dsq[s]], [dp], last=True)
        k.op(k.act, lambda: nc.scalar.activation(out=nb.rs[:, :n], in_=ps[:, :n], func=AF.Ln, scale=1.0 / (nfe * 128), bias=eps_t[:, 0:1]),
             reads=[dp, d_const], writes=[nb.drs])
        k.op(k.act, lambda: nc.scalar.activation(out=nb.rs[:, :n], in_=nb.rs[:, :n], func=AF.Exp, scale=-0.5),
             reads=[nb.drs], writes=[nb.drs])

    def apply_norm(nb, x_of, dx, out_of, dout_, nfe, n, scale_ap, shift_ap, extra_reads=()):
        for c in range(nfe):
            s = c % 2
            k.op(k.dve, lambda c=c, s=s: nc.vector.tensor_tensor(out=nb.tmp[s][:, :n], in0=x_of(c), in1=nb.rs[:, :n], op=ALU.mult),
                 reads=[dx, nb.drs], writes=[nb.dtmp[s]])
            if shift_ap is not None:
                k.op(k.pool, lambda c=c, s=s: nc.gpsimd.tensor_scalar(out=out_of(c), in0=nb.tmp[s][:, :n], scalar1=scale_ap(c),
                                                                   scalar2=shift_ap(c), op0=ALU.mult, op1=ALU.add),
                     reads=[nb.dtmp[s], d_mod] + list(extra_reads), writes=[dout_])
            else:
                k.op(k.pool, lambda c=c, s=s: nc.gpsimd.tensor_scalar(out=out_of(c), in0=nb.tmp[s][:, :n], scalar1=scale_ap(c),
                                                                   scalar2=1.0, op0=ALU.mult, op1=ALU.mult),
                     reads=[nb.dtmp[s], d_mod] + list(extra_reads), writes=[dout_])

    eps_t = sb("eps_t", [128, 1], F32)
    k.op(k.pool, lambda: nc.gpsimd.memset(eps_t[:], EPS), writes=[d_const])

    def xsrc(first_layer):
        return I["xT"] if first_layer else xres

    def load_x_and_norm(nb, ti, l, first_layer, a_tile, shift_j):
        col0, n, seq = tiles[ti]
        k.dma(k.sp, nb.xt[:, :, :n], xsrc(first_layer)[:, col0:col0 + n].rearrange("(c p) t -> p c t", p=128),
              reads=[] if first_layer else [dxres[ti]], writes=[nb.dxt])
        rstd(nb, lambda c: nb.xt[:, c, :n], nb.dxt, NCH, n)
        apply_norm(nb, lambda c: nb.xt[:, c, :n], nb.dxt, lambda c: nb.hT[:, c, :n], nb.dhT, NCH, n,
                   lambda c: a_tile[:, l, c, seq:seq + 1], lambda c: mod[:, l, shift_j * 8 + c, seq:seq + 1])

    def phase_oproj(l, w_o_dram, first_layer, win, dwin):
        prev = cur[0]
        with ExitStack() as es:
            cur[0] = es
            wo = sb("wo", [128, NCH, D], BF16); dwo = Dep()
            stg, dstg = mkstg(2)
            load_w(stg, dstg, wo, dwo, w_o_dram, NCH, D)
            gen = load_w_gen(stg, dstg, win, dwin, I["w_ffn_in"][l], NCH, 2 * HID)
            at = [sb("oat", [128, NCH, 512], BF16) for i in range(2)]
            dat = [Dep() for _ in range(2)]
            xt = [sb("oxt", [128, NCH, 512], F32) for i in range(2)]
            dx = [Dep() for _ in range(2)]
            def ld_tile(ti):
                col0, n, seq = tiles[ti]; b = ti % 2
                k.dma(k.sp, at[b][:, :, :n], attnT[:, col0:col0 + n].rearrange("(c p) t -> p c t", p=128),
                      reads=[dattn], writes=[dat[b]])
                k.dma(k.sp, xt[b][:, :, :n], xsrc(first_layer)[:, col0:col0 + n].rearrange("(c p) t -> p c t", p=128),
                      reads=[] if first_layer else [dxres[ti]], writes=[dx[b]])

            ld_tile(0)
            for ti, (col0, n, seq) in enumerate(tiles):
                b = ti % 2
                if ti + 1 < len(tiles):
                    ld_tile(ti + 1)
                for m in range(NCH):
                    pi = m % 4; ps = psum[pi]; dp = dps[pi]
                    mmk(ps, dp, wo, m * 128, 128, lambda kk: at[b][:, kk, :n], NCH, n, [dwo, dat[b]])
                    k.op(k.dve, lambda m=m, ps=ps: nc.vector.scalar_tensor_tensor(
                        out=xt[b][:, m, :n], in0=ps[:, :n], scalar=mod[:, l, 16 + m, seq:seq + 1], in1=xt[b][:, m, :n],
                        op0=ALU.mult, op1=ALU.add), reads=[dp, d_mod, dx[b]], writes=[dx[b]])
                k.dma(STQ[0], xres[:, col0:col0 + n].rearrange("(c p) t -> p c t", p=128), xt[b][:, :, :n],
                      reads=[dx[b]], writes=[dxres[ti]])
                next(gen, None); next(gen, None)
            for _ in gen:
                pass
            k.barrier()
        cur[0] = prev

    def phase_ffn(l, final, win, dwin):
        prev = cur[0]
        with ExitStack() as es:
            cur[0] = es
            wout = sb("wout", [128, HCH, D], BF16); dwout = Dep()
            with ExitStack() as es2:
                cur[0] = es2
                stg, dstg = mkstg(2)
                load_w(stg, dstg, wout, dwout, I["w_ffn_out"][l], HCH, D)
                k.barrier()
            cur[0] = es
            hTs = [sb("hTf", [128, NCH, 512], BF16) for _ in range(2)]; dhTs = [Dep(), Dep()]
            sq = [sb("sqf", [128, 512], BF16) for _ in range(2)]; dsq = [Dep(), Dep()]
            tmp = [sb("tmpf", [128, 512], F32) for _ in range(2)]; dtmp = [Dep(), Dep()]
            rs = sb("rsf", [128, 512], F32); drs = Dep()
            NX = 4
            xc = [sb("xcf", [128, 512], F32) for _ in range(NX)]; dxc = [Dep() for _ in range(NX)]
            actT = sb("actT", [128, HCH, 512], BF16); dact = Dep()
            sg = [sb("sg", [128, 512], F32) for i in range(2)]
            dsg = [Dep() for _ in range(2)]
            xi = [0]

            def ldx(ti, c):
                col0, n, seq = tiles[ti]
                i = xi[0] % NX; xi[0] += 1
                k.dma(k.sp, xc[i][:, :n], xres[c * 128:(c + 1) * 128, col0:col0 + n], reads=[dxres[ti]], writes=[dxc[i]])
                return i

            def norm_tile(ti):
                col0, n, seq = tiles[ti]
                hT = hTs[ti % 2]; dhT = dhTs[ti % 2]
                ps = psum[7]; dp = dps[7]
                for c in range(NCH):
                    i = ldx(ti, c); s_ = c % 2
                    k.op(k.act, lambda i=i, s_=s_: nc.scalar.activation(out=sq[s_][:, :n], in_=xc[i][:, :n], func=AF.Square),
                         reads=[dxc[i]], writes=[dsq[s_]])
                    mm(ps[:, :n], ones_bf[:], sq[s_][:, :n], c == 0, c == NCH - 1, [d_const, dsq[s_]], [dp], last=True)
                k.op(k.act, lambda: nc.scalar.activation(out=rs[:, :n], in_=ps[:, :n], func=AF.Ln, scale=1.0 / D, bias=eps_t[:, 0:1]),
                     reads=[dp, d_const], writes=[drs])
                k.op(k.act, lambda: nc.scalar.activation(out=rs[:, :n], in_=rs[:, :n], func=AF.Exp, scale=-0.5), reads=[drs], writes=[drs])
                for c in range(NCH):
                    i = ldx(ti, c); s_ = c % 2
                    k.op(k.dve, lambda i=i, s_=s_: nc.vector.tensor_tensor(out=tmp[s_][:, :n], in0=xc[i][:, :n], in1=rs[:, :n], op=ALU.mult),
                         reads=[dxc[i], drs], writes=[dtmp[s_]])
                    k.op(k.pool, lambda c=c, s_=s_: nc.gpsimd.tensor_scalar(out=hT[:, c, :n], in0=tmp[s_][:, :n], scalar1=affn[:, l, c, seq:seq + 1],
                                                                     scalar2=mod[:, l, 24 + c, seq:seq + 1], op0=ALU.mult, op1=ALU.add),
                         reads=[dtmp[s_], d_mod], writes=[dhT])

            norm_tile(0)
            for ti, (col0, n, seq) in enumerate(tiles):
                hT = hTs[ti % 2]; dhT = dhTs[ti % 2]
                if ti + 1 < len(tiles):
                    norm_tile(ti + 1)
                for j in range(HCH):
                    pg = psum[(2 * j) % 6]; dg = dps[(2 * j) % 6]
                    pu = psum[(2 * j + 1) % 6]; du = dps[(2 * j + 1) % 6]
                    mmk(pg, dg, win, j * 128, 128, lambda kk: hT[:, kk, :n], NCH, n, [dwin, dhT])
                    mmk(pu, du, win, HID + j * 128, 128, lambda kk: hT[:, kk, :n], NCH, n, [dwin, dhT])
                    s = j % 2
                    k.op(k.act, lambda s=s, pg=pg: nc.scalar.activation(out=sg[s][:, :n], in_=pg[:, :n], func=AF.Silu),
                         reads=[dg], writes=[dsg[s]])
                    k.op(k.dve, lambda s=s, pu=pu, j=j: nc.vector.tensor_tensor(out=actT[:, j, :n], in0=pu[:, :n], in1=sg[s][:, :n], op=ALU.mult),
                         reads=[du, dsg[s]], writes=[dact])
                wdeps = []
                for m in range(NCH):
                    pi = 6 + (m % 2); ps = psum[pi]; dp = dps[pi]
                    for j in range(HCH):
                        mm(ps[:, :n], wout[:, j, m * 128:(m + 1) * 128], actT[:, j, :n], j == 0, j == HCH - 1,
                           [dwout, dact], [dp], last=(j == HCH - 1))
                    i = ldx(ti, m)
                    k.op(k.dve, lambda m=m, ps=ps, i=i: nc.vector.scalar_tensor_tensor(
                        out=xc[i][:, :n], in0=ps[:, :n], scalar=mod[:, l, 40 + m, seq:seq + 1], in1=xc[i][:, :n],
                        op0=ALU.mult, op1=ALU.add), reads=[dp, d_mod, dxc[i]], writes=[dxc[i]])
                    wd = Dep(); wdeps.append(wd)
                    k.dma(STQ[0], xres[m * 128:(m + 1) * 128, col0:col0 + n], xc[i][:, :n], reads=[dxc[i], dxres[ti]], writes=[wd])
                merged = Dep()
                for wd in wdeps:
                    if wd.w is not None:
                        merged.r[wd.w[0]] = (wd.w[1], wd.w[2])
                dxres_w[ti] = merged
            k.barrier()
        cur[0] = prev

    dxres_w = {}

    def phase_final():
        with ExitStack() as es:
            cur[0] = es
            nbs = [NormBufs(), NormBufs()]
            for ti, (col0, n, seq) in enumerate(tiles):
                nb = nbs[ti % 2]
                k.dma(k.sp, nb.xt[:, :, :n], xres[:, col0:col0 + n].rearrange("(c p) t -> p c t", p=128), reads=[dxres[ti]], writes=[nb.dxt])
                rstd(nb, lambda c: nb.xt[:, c, :n], nb.dxt, NCH, n)
                for c in range(NCH):
                    k.op(k.dve, lambda c=c: nc.vector.scalar_tensor_tensor(
                        out=nb.xt[:, c, :n], in0=nb.xt[:, c, :n], scalar=gfin[:, c:c + 1], in1=nb.rs[:, :n],
                        op0=ALU.mult, op1=ALU.mult), reads=[nb.dxt, nb.drs, d_mod], writes=[nb.dxt])
                k.dma(STQ[0], O["yT"][:, col0:col0 + n].rearrange("(c p) t -> p c t", p=128), nb.xt[:, :, :n], reads=[nb.dxt])
            k.barrier()
        cur[0] = None

    tri_bf = sb("tri_bf", [128, 128], BF16)
    anti_bf = sb("anti_bf", [128, 128], BF16)
    ident_bf = sb("ident_bf", [128, 128], BF16)

    def load_const_bf(dst, src_ap, shape):
        with ExitStack() as es:
            cur[0] = es
            t = sb("cst", shape, F32); dt_ = Dep()
            k.dma(k.sp, t[:], src_ap, writes=[dt_])
            k.op(k.dve, lambda: nc.vector.tensor_copy(out=dst, in_=t[:]), reads=[dt_], writes=[d_const])
            k.barrier()
        cur[0] = None

    load_const_bf(tri_bf[:], I["tri"], [128, 128])
    load_const_bf(anti_bf[:], I["antiI"], [128, 128])
    load_const_bf(ident_bf[:], I["ident"], [128, 128])

    def cbase(b):
        return TT + b * P

    def ingest_cache(kc_ap, vc_ap, Lc, F, krow0=0):
        nf = (F + 127) // 128
        kin = [sb("kin", [128, F], F32) for _ in range(2)]; dkin = [Dep(), Dep()]
        kto = [sb("kto", [128, nf, 128], BF16) for _ in range(2)]; dkto = [Dep(), Dep()]
        vin = [sb("vin", [128, F], F32) for _ in range(2)]; dvin = [Dep(), Dep()]
        vbo = [sb("vbo", [128, F], BF16) for _ in range(2)]; dvbo = [Dep(), Dep()]
        it = 0
        for b in range(2):
            for t0 in range(0, Lc, 128):
                nt = min(128, Lc - t0)
                s = it % 2; it += 1
                k.dma(k.sp, kin[s][:nt, :], kc_ap[b, t0:t0 + nt, :], writes=[dkin[s]])
                for c in range(nf):
                    fw = min(128, F - c * 128)
                    pi = c % 4; ps = psum[pi]; dp = dps[pi]
                    mm(ps[:fw, :nt], kin[s][:nt, c * 128:c * 128 + fw], ident[:nt, :nt], True, True,
                       [dkin[s], d_const], [dp])
                    k.op(k.act, lambda c=c, ps=ps, fw=fw: nc.scalar.activation(out=kto[s][:fw, c, :nt], in_=ps[:fw, :nt], func=AF.Copy),
                         reads=[dp], writes=[dkto[s]])
                if F >= 128:
                    k.dma(STQ[0], kT_s[krow0:krow0 + F, cbase(b) + t0:cbase(b) + t0 + nt].rearrange("(c p) t -> p c t", p=128),
                          kto[s][:, :, :nt], reads=[dkto[s]], writes=[dkk])
                else:
                    k.dma(STQ[0], kT_s[krow0:krow0 + F, cbase(b) + t0:cbase(b) + t0 + nt], kto[s][:F, 0, :nt],
                          reads=[dkto[s]], writes=[dkk])
                if vc_ap is not None:
                    k.dma(k.sp, vin[s][:nt, :], vc_ap[b, t0:t0 + nt, :], writes=[dvin[s]])
                    cast(vbo[s][:nt, :], vin[s][:nt, :], [dvin[s]], [dvbo[s]])
                    k.dma(STQ[0], v_s[cbase(b) + t0:cbase(b) + t0 + nt, 0:F], vbo[s][:nt, :], reads=[dvbo[s]], writes=[dv])

    def phase_qkv(l, mix, first_layer):
        wq = {1: "w_b_qkv", 2: "w_c_qkv", 3: "w_d_qkv"}[mix]
        NQ = 1024
        NKF = 256 if mix == 2 else 1024
        NW = NQ + 2 * NKF
        nkc = NKF // 128
        Lc = LC[mix]
        with ExitStack() as es:
            cur[0] = es
            w = sb("wqkv", [128, NCH, NW], BF16); dw = Dep()
            with ExitStack() as es2:
                cur[0] = es2
                stg, dstg = mkstg(2)
                load_w(stg, dstg, w, dw, I[wq], NCH, NW)
                kc = {1: "cache_b_k", 2: "cache_c_k", 3: "cache_d_k"}[mix]
                vc = {1: "cache_b_v", 2: "cache_c_v", 3: "cache_d_v"}[mix]
                ingest_cache(I[kc], I[vc], Lc, NKF)
                k.barrier()
            cur[0] = es
            if mix == 2:
                wsw = sb("wsw", [128, NCH, NQ + NKF], BF16)
                wv = w[:, :, 0:NQ + NKF].rearrange("p k (h d) -> p k h d", d=64)
                sv = wsw[:, :, :].rearrange("p k (h d) -> p k h d", d=64)
                for kk in range(NCH):
                    k.op(k.pool, lambda kk=kk: nc.gpsimd.tensor_copy(out=sv[:, kk, :, 16:64], in_=wv[:, kk, :, 16:64]), reads=[dw], writes=[dw])
                    k.op(k.pool, lambda kk=kk: nc.gpsimd.tensor_copy(out=sv[:, kk, :, 0:8], in_=wv[:, kk, :, 8:16]), reads=[dw], writes=[dw])
                    k.op(k.pool, lambda kk=kk: nc.gpsimd.tensor_copy(out=sv[:, kk, :, 8:16], in_=wv[:, kk, :, 0:8]), reads=[dw], writes=[dw])
                bc = sb("bc", [128, 12], F32); bcs = sb("bcs", [128, 12], F32)
                brow = sb("brow", [128, 256], F32)
                rc = sb("rc", [128, TT], F32); rsn = sb("rsn", [128, TT], F32)
                k.dma(k.sp, bc[:], I["bcT"], writes=[dw])
                k.dma(k.sp, bcs[:], I["bcswT"], writes=[dw])
                k.dma(k.sp, brow[:], I["bc_row"][0:1, 1280:1536].partition_broadcast(128), writes=[dw])
                k.dma(k.sp, rc[:], I["ropeC_cos"], writes=[dw])
                k.dma(k.sp, rsn[:], I["ropeC_sin"], writes=[dw])
            nbs = [NormBufs(), NormBufs()]
            qo = sb("qo", [128, NCH, 512], BF16); dqo = Dep()
            ko = sb("ko", [128, nkc, 512], BF16); dko = Dep()
            kf = sb("kf", [128, nkc, 512], F32); dkf = Dep()
            t1 = [sb("t1", [128, 512], F32) for _ in range(2)]; dt1 = [Dep(), Dep()]
            t2 = [sb("t2", [128, 512], F32) for _ in range(2)]; dt2 = [Dep(), Dep()]
            ktm = [sb("ktm", [128, NKF], F32) for _ in range(2)]; dktm = [Dep(), Dep()]
            vtm = [sb("vtm", [128, NKF], F32) for _ in range(2)]; dvtm = [Dep(), Dep()]
            vtb = [sb("vtb", [128, NKF], BF16) for _ in range(2)]; dvtb = [Dep(), Dep()]
            keep = {1: S, 2: min(128, S), 3: min(512, S)}[mix]
            ko_name = {1: "b_k", 2: "c_k", 3: "d_k"}[mix]
            vo_name = {1: "b_v", 2: "c_v", 3: "d_v"}[mix]
            it = 0
            pr = 0
            load_x_and_norm(nbs[0], 0, l, first_layer, amix, 0)
            for ti, (col0, n, seq) in enumerate(tiles):
                nb = nbs[ti % 2]
                if ti + 1 < len(tiles):
                    load_x_and_norm(nbs[(ti + 1) % 2], ti + 1, l, first_layer, amix, 0)
                hrhs = lambda kk, nb=nb, n=n: nb.hT[:, kk, :n]
                for m in range(NCH + nkc):
                    isq = m < NCH
                    wcol = m * 128 if isq else NQ + (m - NCH) * 128
                    mi = m if isq else m - NCH
                    pi = pr % 6; pr += 1; ps = psum[pi]; dp = dps[pi]
                    mmk(ps, dp, w, wcol, 128, hrhs, NCH, n, [dw, nb.dhT])
                    dst_bf = qo[:, mi, :n] if isq else ko[:, mi, :n]
                    ddst = dqo if isq else dko
                    if mix != 2:
                        k.op(k.act, lambda ps=ps, dst_bf=dst_bf: nc.scalar.activation(out=dst_bf, in_=ps[:, :n], func=AF.Copy),
                             reads=[dp], writes=[ddst])
                        if not isq:
                            k.op(k.act, lambda ps=ps, mi=mi: nc.scalar.activation(out=kf[:, mi, :n], in_=ps[:, :n], func=AF.Copy),
                                 reads=[dp], writes=[dkf])
                    else:
                        pi2 = pr % 6; pr += 1; ps2 = psum[pi2]; dp2 = dps[pi2]
                        wc2 = m * 128
                        mmk(ps2, dp2, wsw, wc2, 128, hrhs, NCH, n, [dw, nb.dhT])
                        bcol = m
                        s_ = it % 2; it += 1
                        k.op(k.dve, lambda ps=ps, s_=s_, bcol=bcol: nc.vector.scalar_tensor_tensor(
                            out=t1[s_][:, :n], in0=ps[:, :n], scalar=bc[:, bcol:bcol + 1], in1=rc[:, col0:col0 + n],
                            op0=ALU.add, op1=ALU.mult), reads=[dp, dw], writes=[dt1[s_]])
                        k.op(k.dve, lambda ps2=ps2, s_=s_, bcol=bcol: nc.vector.scalar_tensor_tensor(
                            out=t2[s_][:, :n], in0=ps2[:, :n], scalar=bcs[:, bcol:bcol + 1], in1=rsn[:, col0:col0 + n],
                            op0=ALU.add, op1=ALU.mult), reads=[dp2, dw], writes=[dt2[s_]])
                        if isq:
                            k.op(k.pool, lambda s_=s_, dst_bf=dst_bf: nc.gpsimd.tensor_tensor(out=dst_bf, in0=t1[s_][:, :n], in1=t2[s_][:, :n], op=ALU.add),
                                 reads=[dt1[s_], dt2[s_]], writes=[ddst])
                        else:
                            k.op(k.pool, lambda s_=s_, mi=mi: nc.gpsimd.tensor_tensor(out=kf[:, mi, :n], in0=t1[s_][:, :n], in1=t2[s_][:, :n], op=ALU.add),
                                 reads=[dt1[s_], dt2[s_]], writes=[dkf])
                            k.op(k.pool, lambda mi=mi, dst_bf=dst_bf: nc.gpsimd.tensor_copy(out=dst_bf, in_=kf[:, mi, :n]),
                                 reads=[dkf], writes=[ddst])
                k.dma(STQ[0], qT_s[0:NQ, col0:col0 + n].rearrange("(c p) t -> p c t", p=128), qo[:, :, :n], reads=[dqo], writes=[dq])
                k.dma(STQ[0], kT_s[0:NKF, col0:col0 + n].rearrange("(c p) t -> p c t", p=128), ko[:, :, :n], reads=[dko], writes=[dkk])
                for s0 in range(0, n, 128):
                    nt = min(128, n - s0)
                    tok0 = col0 + s0
                    b_ = it % 2; it += 1
                    if seq == 0:
                        lo = S - keep
                        want = tok0 >= lo
                    else:
                        want = True
                    if want:
                        for c in range(nkc):
                            pi = pr % 6; pr += 1; ps = psum[pi]; dp = dps[pi]
                            mm(ps[:nt, :128], kf[:, c, s0:s0 + nt], ident[:, :], True, True, [dkf, d_const], [dp])
                            k.op(k.act, lambda ps=ps, c=c, b_=b_: nc.scalar.activation(out=ktm[b_][:nt, c * 128:(c + 1) * 128], in_=ps[:nt, :128], func=AF.Copy),
                                 reads=[dp], writes=[dktm[b_]])
                    for fb in range(0, NKF, 512):
                        fw = min(512, NKF - fb)
                        pi = pr % 6; pr += 1; ps = psum[pi]; dp = dps[pi]
                        for kk in range(NCH):
                            mm(ps[:nt, :fw], nb.hT[:, kk, s0:s0 + nt], w[:, kk, NQ + NKF + fb:NQ + NKF + fb + fw],
                               kk == 0, kk == NCH - 1, [dw, nb.dhT], [dp], last=(kk == NCH - 1))
                        if mix == 2:
                            k.op(k.dve, lambda ps=ps, b_=b_, fb=fb, fw=fw: nc.vector.tensor_tensor(out=vtm[b_][:nt, fb:fb + fw], in0=ps[:nt, :fw], in1=brow[:nt, fb:fb + fw], op=ALU.add),
                                 reads=[dp, dw], writes=[dvtm[b_]])
                        else:
                            k.op(k.act, lambda ps=ps, b_=b_, fb=fb, fw=fw: nc.scalar.activation(out=vtm[b_][:nt, fb:fb + fw], in_=ps[:nt, :fw], func=AF.Copy),
                                 reads=[dp], writes=[dvtm[b_]])
                    k.op(k.dve, lambda b_=b_: nc.vector.tensor_copy(out=vtb[b_][:nt, :], in_=vtm[b_][:nt, :]), reads=[dvtm[b_]], writes=[dvtb[b_]])
                    k.dma(STQ[0], v_s[tok0:tok0 + nt, 0:NKF], vtb[b_][:nt, :], reads=[dvtb[b_]], writes=[dv])
                    if want:
                        if mix == 1:
                            ka = O["b_k"][tok0:tok0 + nt, :]; va = O["b_v"][tok0:tok0 + nt, :]
                        elif seq == 0:
                            ka = O[ko_name + "_p"][tok0 - lo:tok0 - lo + nt, :]; va = O[vo_name + "_p"][tok0 - lo:tok0 - lo + nt, :]
                        else:
                            ka = O[ko_name + "_s"][seq - 1, Lc - T_S:Lc, :]; va = O[vo_name + "_s"][seq - 1, Lc - T_S:Lc, :]
                        k.dma(STQ[0], ka, ktm[b_][:nt, :], reads=[dktm[b_]])
                        k.dma(STQ[0], va, vtm[b_][:nt, :], reads=[dvtm[b_]])
            if mix != 1:
                for b in range(2):
                    k.dma(k.sp, O[ko_name + "_s"][b, 0:Lc - T_S, :], I[kc][b, T_S:Lc, :])
                    k.dma(k.sp, O[vo_name + "_s"][b, 0:Lc - T_S, :], I[vc][b, T_S:Lc, :])
            k.barrier()
        cur[0] = None

    class AttnBufs:
        def __init__(self, Lc, vw, nh=2):
            self.Lc = Lc
            self.ncs = (Lc + 127) // 128
            self.nslots = S // 128 + 2 * (self.ncs + 1)
            self.Kf = sb("Kf", [128, TT + 2 * Lc], BF16); self.dK = Dep()
            self.Qf = sb("Qf", [128, TT], BF16); self.dQ = Dep()
            self.Qz = [self.Qf, sb("Qf1", [128, TT], BF16)] if nh == 2 else [self.Qf]
            self.Va = sb("Va", [128, self.nslots, nh, vw], BF16); self.dV = Dep()

        def zero_q(self):
            k.op(k.pool, lambda: nc.gpsimd.memset(self.Qz[0][64:128, :], 0.0), writes=[self.dQ])
            k.op(k.pool, lambda: nc.gpsimd.memset(self.Qz[1][0:64, :], 0.0), writes=[self.dQ])

        def load_q_pair(self, u):
            k.dma(k.sp, self.Qz[0][0:64, :], qT_s[u * 128:u * 128 + 64, 0:TT], reads=[dq], writes=[self.dQ])
            k.dma(k.sp, self.Qz[1][64:128, :], qT_s[u * 128 + 64:(u + 1) * 128, 0:TT], reads=[dq], writes=[self.dQ])

        def kcol(self, b, j=0):
            return TT + b * self.Lc + j

        def slot_cache(self, b, t):
            return S // 128 + b * (self.ncs + 1) + t

        def slot_new(self, b):
            return S // 128 + b * (self.ncs + 1) + self.ncs

    def load_K(ab, rows_dst, krow0, nrows):
        Lc = ab.Lc
        k.dma(k.sp, ab.Kf[rows_dst:rows_dst + nrows, 0:TT], kT_s[krow0:krow0 + nrows, 0:TT], reads=[dkk], writes=[ab.dK])
        for b in range(2):
            k.dma(k.sp, ab.Kf[rows_dst:rows_dst + nrows, ab.kcol(b):ab.kcol(b) + Lc],
                  kT_s[krow0:krow0 + nrows, cbase(b):cbase(b) + Lc], reads=[dkk], writes=[ab.dK])

    def load_V(ab, hsel, vcol0):
        Lc = ab.Lc
        for t0 in range(0, S // 128, 8):
            t1_ = min(S // 128, t0 + 8)
            k.dma(k.sp, ab.Va[:, t0:t1_, hsel, 0:64], v_s[t0 * 128:t1_ * 128, vcol0:vcol0 + 64].rearrange("(t p) d -> p t d", p=128),
                  reads=[dv], writes=[ab.dV])
        for b in range(2):
            for t0 in range(0, ab.ncs, 8):
                t1_ = min(ab.ncs, t0 + 8)
                k.dma(k.sp, ab.Va[:, ab.slot_cache(b, t0):ab.slot_cache(b, t0) + (t1_ - t0), hsel, 0:64],
                      v_s[cbase(b) + t0 * 128:cbase(b) + t1_ * 128, vcol0:vcol0 + 64].rearrange("(t p) d -> p t d", p=128), reads=[dv], writes=[ab.dV])
            k.dma(k.sp, ab.Va[0:T_S, ab.slot_new(b), hsel, 0:64], v_s[S + b * T_S:S + (b + 1) * T_S, vcol0:vcol0 + 64],
                  reads=[dv], writes=[ab.dV])

    def load_masks(name, nm):
        mt = sb("mask", [128, nm, 512], BF16); dm = Dep()
        with ExitStack() as es2:
            old = cur[0]; cur[0] = es2
            st = sb("mstg", [128, 512], F32); dst_ = Dep()
            for i in range(nm):
                k.dma(k.sp, st[:], I[name][i], writes=[dst_])
                k.op(k.dve, lambda i=i: nc.vector.tensor_copy(out=mt[:, i, :], in_=st[:]), reads=[dst_], writes=[dm])
            k.barrier()
            cur[0] = old
        return mt, dm

    cnt = {"s": 0, "o": 0, "p": 0}

    def phase_attn_softmax(mix, lnext):
        Lc = LC[mix]
        kdim = 96 if mix == 0 else 64
        scale = float(kdim) ** -0.5
        OFFE = 511
        STQ[0] = k.sp
        with ExitStack() as es:
            cur[0] = es
            mname, nm = {0: ("maskA", 4), 2: ("maskC", 5), 3: ("maskD", 8)}[mix]
            mt, dm = load_masks(mname, nm)
            abs_ = [AttnBufs(Lc, 128, 1 if mix == 0 else 2) for _ in range(2)]
            for ab in abs_:
                k.op(k.pool, lambda ab=ab: nc.gpsimd.memset(ab.Va[:, :, :, 64:128], 1.0), writes=[ab.dV])
                if mix != 0:
                    ab.zero_q()
            pt = [sb("pt", [128, 512], BF16) for _ in range(4)]; dpt = [Dep() for _ in range(4)]
            if mix == 2:
                esink = sb("esink", [128, 16], F32); des = Dep()
                k.dma(k.sp, esink[:], I["sink_c"][0:1, :].partition_broadcast(128), writes=[des])
                k.op(k.act, lambda: nc.scalar.activation(out=esink[:], in_=esink[:], func=AF.Exp), reads=[des], writes=[des])
            if mix == 3:
                et = sb("ext", [16, 1536], F32); det = Dep()
                k.op(k.pool, lambda: nc.gpsimd.memset(et[:], 0.0), writes=[det])
                k.dma(k.sp, et[:, OFFE - 128:OFFE + 129], I["rel_bias_d"], writes=[det])
                k.op(k.dve, lambda: nc.vector.tensor_scalar(out=et[:, 0:OFFE - 128], in0=et[:, 0:OFFE - 128], scalar1=et[:, OFFE - 128:OFFE - 127],
                                                          scalar2=None, op0=ALU.add), reads=[det], writes=[det])
                k.op(k.dve, lambda: nc.vector.tensor_scalar(out=et[:, OFFE + 129:1536], in0=et[:, OFFE + 129:1536], scalar1=et[:, OFFE + 128:OFFE + 129],
                                                          scalar2=None, op0=ALU.add), reads=[det], writes=[det])
                dext = Dep()
                k.dma(STQ[0], ext_s, et[:], reads=[det], writes=[dext])
                Hst = [sb("Hst", [128, 512], F32) for _ in range(4)]; dHst = [Dep() for _ in range(4)]
                Hb = [sb("Hb", [128, 8 + Lc // 128 + 1, 512], BF16) for _ in range(2)]; dHb = [Dep(), Dep()]
                anti32 = sb("anti32", [32, 32], BF16)
                a32s = sb("a32s", [32, 32], F32); da32 = Dep()
                k.dma(k.sp, a32s[:], I["antiI"][96:128, 0:32], writes=[da32])
                k.op(k.dve, lambda: nc.vector.tensor_copy(out=anti32[:], in_=a32s[:]), reads=[da32], writes=[d_const])
            units = 16 if mix == 0 else 8
            hst_i = [0]

            def load_unit(u, ab):
                if mix == 0:
                    load_K(ab, 0, u * 64, 64)
                    load_K(ab, 64, D, 32)
                    k.dma(k.sp, ab.Qf[0:96, :], qT_s[u * 96:(u + 1) * 96, 0:TT], reads=[dq], writes=[ab.dQ])
                    load_V(ab, 0, u * 64)
                else:
                    if mix == 2:
                        g = u // 2
                        load_K(ab, 0, g * 64, 64)
                        load_K(ab, 64, g * 64, 64)
                        load_V(ab, 0, g * 64)
                    else:
                        load_K(ab, 0, u * 128, 128)
                        load_V(ab, 0, u * 128)
                        load_V(ab, 1, u * 128 + 64)
                    ab.load_q_pair(u)

            def load_H(h, hb_i):
                specs = [(8 + 0, 0, 0, 0)]
                specs = []
                for r in range(-4, 4):
                    specs.append((r + 4, OFFE - 127 - 128 * r, 128, 512))
                for kt in range(Lc // 128):
                    specs.append((8 + kt, OFFE + Lc - 128 * kt - 127, 128, T_S))
                specs.append((8 + Lc // 128, OFFE - 31, 32, T_S))
                for (slot, base, nr, ncol) in specs:
                    s_ = hst_i[0] % 4; hst_i[0] += 1
                    src = bass.AP(ext_s.tensor, h * 1536 + base, [[1, nr], [1, ncol]])
                    k.dma(k.sp, Hst[s_][:nr, :ncol], src, reads=[dext], writes=[dHst[s_]])
                    if slot < 8:
                        k.op(k.dve, lambda s_=s_, slot=slot, nr=nr, ncol=ncol: nc.vector.scalar_tensor_tensor(
                            out=Hb[hb_i][:nr, slot, :ncol], in0=Hst[s_][:nr, :ncol], scalar=8.0, in1=mt[:nr, slot, :ncol], op0=ALU.mult, op1=ALU.add),
                            reads=[dHst[s_], dm], writes=[dHb[hb_i]])
                    else:
                        k.op(k.dve, lambda s_=s_, slot=slot, nr=nr, ncol=ncol: nc.vector.tensor_scalar(
                            out=Hb[hb_i][:nr, slot, :ncol], in0=Hst[s_][:nr, :ncol], scalar1=8.0, scalar2=None, op0=ALU.mult),
                            reads=[dHst[s_]], writes=[dHb[hb_i]])

            obanks = {0: [4, 5], 2: [4, 5], 3: [4, 5, 6, 7]}[mix]
            NO = len(obanks)
            rden = [sb("rden", [128, 512], F32) for _ in range(NO)]; drd = [Dep() for _ in range(NO)]
            ot = [sb("ot", [128, 512], BF16) for _ in range(NO)]; dot_ = [Dep() for _ in range(NO)]
            pend = []
            SK = 2

            def push(fA, fP):
                fA()
                pend.append(fP)
                if len(pend) > SK:
                    pend.pop(0)()

            pstore = []

            def flush():
                while pend:
                    pend.pop(0)()
                while pstore:
                    pstore.pop(0)()

            def run_q(ab, pbase, hsel, h, qcol0, nq, blocks, hb_i):
                oc = cnt["o"]; cnt["o"] += 1
                ri = oc % NO
                psO = psum[obanks[ri]]; dO = dps[obanks[ri]]
                nb_ = len(blocks)

                def finalize():
                    if mix == 2:
                        k.op(k.dve, lambda: nc.vector.tensor_scalar(out=rden[ri][64:128, :nq], in0=psO[64:128, :nq], scalar1=esink[64:128, h:h + 1],
                                                                  scalar2=None, op0=ALU.add), reads=[dO, des], writes=[drd[ri]])
                        k.op(k.dve, lambda: nc.vector.reciprocal(out=rden[ri][64:128, :nq], in_=rden[ri][64:128, :nq]), reads=[drd[ri]], writes=[drd[ri]])
                    else:
                        k.op(k.dve, lambda: nc.vector.reciprocal(out=rden[ri][64:128, :nq], in_=psO[64:128, :nq]), reads=[dO], writes=[drd[ri]])
                    k.op(k.dve, lambda: nc.vector.tensor_tensor(out=ot[ri][0:64, :nq], in0=psO[0:64, :nq], in1=rden[ri][64:128, :nq], op=ALU.mult),
                         reads=[dO, drd[ri]], writes=[dot_[ri]])
                    while pstore:
                        pstore.pop(0)()
                    pstore.append(lambda: k.dma(STQ[0], attnT[h * 64:(h + 1) * 64, qcol0:qcol0 + nq], ot[ri][0:64, :nq], reads=[dot_[ri]], writes=[dattn]))

                for j, (kcol, nk, slot, mask, hslot) in enumerate(blocks):
                    st = {}

                    def fA(kcol=kcol, nk=nk, mask=mask, hslot=hslot, st=st):
                        si = cnt["s"] % 4; cnt["s"] += 1
                        psS = psum[si]; dS = dps[si]
                        more = (hslot is not None) or (mask is not None)
                        if mix == 0:
                            mm(psS[:nk, :nq], ab.Kf[0:96, kcol:kcol + nk], ab.Qf[0:96, qcol0:qcol0 + nq],
                               True, not more, [ab.dK, ab.dQ], [dS], last=(not more))
                        else:
                            mm(psS[:nk, :nq], ab.Kf[:, kcol:kcol + nk], ab.Qz[pbase // 64][:, qcol0:qcol0 + nq],
                               True, not more, [ab.dK, ab.dQ], [dS], last=(not more))
                        if hslot is not None:
                            al = anti_bf[:, :] if nk == 128 else anti32[:, :]
                            mm(psS[:nk, :nq], al, Hb[hb_i][:nk, hslot, :nq], False, True, [d_const, dHb[hb_i]], [dS])
                        elif mask is not None:
                            mm(psS[:nk, :nq], ident_bf[:nk, :nk], mt[:nk, mask, :nq], False, True, [d_const, dm], [dS])
                        pi = cnt["p"] % 4; cnt["p"] += 1
                        k.op(k.act, lambda: nc.scalar.activation(out=pt[pi][:nk, :nq], in_=psS[:nk, :nq], func=AF.Exp, scale=scale),
                             reads=[dS], writes=[dpt[pi]])
                        st["pi"] = pi

                    def fP(j=j, nk=nk, slot=slot, st=st):
                        pi = st["pi"]
                        mm(psO[:, :nq], ab.Va[:nk, slot, hsel, :], pt[pi][:nk, :nq], j == 0, j == nb_ - 1, [ab.dV, dpt[pi]], [dO],
                           last=(j == nb_ - 1))
                        if j == nb_ - 1:
                            finalize()

                    push(fA, fP)

            nper = 2
            bg = AdaBg({0: [1, 2], 2: [3]}.get(mix, []), 4, [6, 7])
            if mix == 3:
                load_H(0, 0)
            load_unit(0, abs_[0])
            bg.load(nper)
            for u in range(units):
                ab = abs_[u % 2]
                if u + 1 < units:
                    flush()
                    load_unit(u + 1, abs_[(u + 1) % 2])
                bg.step(nper)
                heads = [(0, 0, u)] if mix == 0 else [(0, 0, 2 * u), (64, 0 if mix == 2 else 1, 2 * u + 1)]
                for (pbase, hsel, h) in heads:
                    hb_i = h % 2
                    if mix == 3 and h + 1 < 16:
                        load_H(h + 1, (h + 1) % 2)
                    for qi in range(NPT):
                        blocks = []
                        if mix == 0:
                            for kt in range(0, 4 * qi + 4):
                                blocks.append((kt * 128, 128, kt, (kt - 4 * qi) if kt >= 4 * qi else None, None))
                        elif mix == 2:
                            for kt in range(max(0, 4 * qi - 1), 4 * qi + 4):
                                blocks.append((kt * 128, 128, kt, kt - 4 * qi + 1, None))
                        else:
                            for kt in range(max(0, 4 * qi - 4), 4 * qi + 4):
                                blocks.append((kt * 128, 128, kt, None, kt - 4 * qi + 4))
                        run_q(ab, pbase, hsel, h, qi * 512, 512, blocks, hb_i)
                    for b in range(2):
                        blocks = []
                        for t in range(ab.ncs):
                            blocks.append((ab.kcol(b, t * 128), min(128, Lc - t * 128), ab.slot_cache(b, t), None, (8 + t) if mix == 3 else None))
                        blocks.append((S + b * T_S, T_S, ab.slot_new(b), None, (8 + Lc // 128) if mix == 3 else None))
                        run_q(ab, pbase, hsel, h, S + b * T_S, T_S, blocks, hb_i)
            flush()
            bg.finish()
            k.barrier()
        cur[0] = None
        STQ[0] = k.act

    def phase_proj_a(l, first_layer):
        with ExitStack() as es:
            cur[0] = es
            wdn = sb("wdn", [128, NCH, 672], BF16); dw = Dep()
            wkr = sb("wkr", [128, NCH, 96], BF16)
            wkrs = sb("wkrs", [128, NCH, 96], BF16)
            wuq = sb("wuq", [128, 3, 1536], BF16)
            wuqs = sb("wuqs", [128, 3, 1536], BF16)
            wuk = sb("wuk", [128, 2, D], BF16)
            wuv = sb("wuv", [128, 2, D], BF16)
            gq = sb("gq", [128, 3], F32); gkv = sb("gkv", [128, 2], F32)
            rc = sb("rc", [128, TT], F32); rsn = sb("rsn", [128, TT], F32)
            k.dma(k.sp, gq[:], I["gaqT"], writes=[d_mod])
            k.dma(k.sp, gkv[:], I["gakvT"], writes=[d_mod])
            k.dma(k.sp, rc[:], I["ropeA_cos"], writes=[dw])
            k.dma(k.sp, rsn[:], I["ropeA_sin"], writes=[dw])
            nb = NormBufs()
            cknb_c = [sb("cknc", [128, 2, 128], BF16) for _ in range(2)]; dcknc = [Dep(), Dep()]
            with ExitStack() as es2:
                cur[0] = es2
                stg, dstg = mkstg(2)
                load_w(stg, dstg, wdn, dw, I["w_a_down"], NCH, 672)
                load_w(stg, dstg, wuq, dw, I["w_a_uq"], 3, 1536)
                load_w(stg, dstg, wuk, dw, I["w_a_uk"], 2, D)
                load_w(stg, dstg, wuv, dw, I["w_a_uv"], 2, D)
                k.op(k.pool, lambda: nc.gpsimd.memset(wkr[:], 0.0), writes=[dw])
                k.op(k.pool, lambda: nc.gpsimd.memset(wkrs[:], 0.0), writes=[dw])
                k.op(k.pool, lambda: nc.gpsimd.tensor_copy(out=wkr[:, :, 64:96], in_=wdn[:, :, 640:672]), reads=[dw], writes=[dw])
                k.op(k.pool, lambda: nc.gpsimd.tensor_copy(out=wkrs[:, :, 64:80], in_=wdn[:, :, 656:672]), reads=[dw], writes=[dw])
                k.op(k.pool, lambda: nc.gpsimd.tensor_copy(out=wkrs[:, :, 80:96], in_=wdn[:, :, 640:656]), reads=[dw], writes=[dw])
                qv = wuq[:, :, :].rearrange("p k (h d) -> p k h d", d=96)
                sv = wuqs[:, :, :].rearrange("p k (h d) -> p k h d", d=96)
                for kk in range(3):
                    k.op(k.pool, lambda kk=kk: nc.gpsimd.tensor_copy(out=sv[:, kk, :, 0:64], in_=qv[:, kk, :, 0:64]), reads=[dw], writes=[dw])
                    k.op(k.pool, lambda kk=kk: nc.gpsimd.tensor_copy(out=sv[:, kk, :, 64:80], in_=qv[:, kk, :, 80:96]), reads=[dw], writes=[dw])
                    k.op(k.pool, lambda kk=kk: nc.gpsimd.tensor_copy(out=sv[:, kk, :, 80:96], in_=qv[:, kk, :, 64:80]), reads=[dw], writes=[dw])
                ingest_cache(I["cache_a_kr"], None, P, 32, krow0=D)
                cin = [sb("cin", [128, 256], F32) for _ in range(2)]; dcin = [Dep(), Dep()]
                kto = [sb("ktoa", [128, NCH, 128], BF16) for _ in range(2)]; dkto = [Dep(), Dep()]
                vbo = [sb("vboa", [128, D], BF16) for _ in range(2)]; dvbo = [Dep(), Dep()]
                it = 0
                for b in range(2):
                    for t0 in range(0, P, 128):
                        s = it % 2; it += 1
                        k.dma(k.sp, cin[s][:, :], I["cache_a_ckv"][b, t0:t0 + 128, :], writes=[dcin[s]])
                        for c in range(2):
                            ps = psum[c]; dp = dps[c]
                            mm(ps[:, :128], cin[s][:, c * 128:(c + 1) * 128], ident[:, :], True, True, [dcin[s], d_const], [dp])
                            k.op(k.act, lambda c=c, ps=ps, s=s: nc.scalar.activation(out=cknb_c[s][:, c, :], in_=ps[:, :128], func=AF.Copy),
                                 reads=[dp], writes=[dcknc[s]])
                        for m in range(NCH):
                            pi = 2 + m % 2; ps = psum[pi]; dp = dps[pi]
                            mmk(ps, dp, wuk, m * 128, 128, lambda kk: cknb_c[s][:, kk, :], 2, 128, [dw, dcknc[s]])
                            k.op(k.act, lambda m=m, ps=ps, s=s: nc.scalar.activation(out=kto[s][:, m, :], in_=ps[:, :128], func=AF.Copy),
                                 reads=[dp], writes=[dkto[s]])
                        k.dma(STQ[0], kT_s[0:D, cbase(b) + t0:cbase(b) + t0 + 128].rearrange("(c p) t -> p c t", p=128), kto[s][:, :, :],
                              reads=[dkto[s]], writes=[dkk])
                        for fb in range(2):
                            pi = 4 + fb; ps = psum[pi]; dp = dps[pi]
                            for c in range(2):
                                mm(ps[:, :512], cknb_c[s][:, c, :], wuv[:, c, fb * 512:(fb + 1) * 512], c == 0, c == 1, [dw, dcknc[s]], [dp], last=(c == 1))
                            k.op(k.dve, lambda fb=fb, ps=ps, s=s: nc.vector.tensor_copy(out=vbo[s][:, fb * 512:(fb + 1) * 512], in_=ps[:, :512]),
                                 reads=[dp], writes=[dvbo[s]])
                        k.dma(STQ[0], v_s[cbase(b) + t0:cbase(b) + t0 + 128, :], vbo[s][:, :], reads=[dvbo[s]], writes=[dv])
                k.barrier()
            cur[0] = es
            cqf = sb("cqf", [128, 3, 512], F32); dcqf = Dep()
            cqn = sb("cqn", [128, 3, 512], BF16); dcqn = Dep()
            ckf = sb("ckf", [128, 2, 512], F32); dckf = Dep()
            ckn32 = sb("ckn32", [128, 2, 512], F32); dckn32 = Dep()
            cknb = sb("cknb", [128, 2, 512], BF16); dcknb = Dep()
            t1 = [sb("t1", [128, 512], F32) for _ in range(2)]; dt1 = [Dep(), Dep()]
            t2 = [sb("t2", [128, 512], F32) for _ in range(2)]; dt2 = [Dep(), Dep()]
            krf = sb("krf", [128, 512], F32); dkrf = Dep()
            krb = sb("krb", [128, 512], BF16); dkrb = Dep()
            qall = sb("qall", [128, 16, 512], BF16); dqall = Dep()
            ko = sb("koa", [128, NCH, 512], BF16); dko = Dep()
            ctm = [sb("ctm", [128, 256], F32) for _ in range(2)]; dctm = [Dep(), Dep()]
            krtm = [sb("krtm", [128, 32], F32) for _ in range(2)]; dkrtm = [Dep(), Dep()]
            vtb = [sb("vtba", [128, D], BF16) for _ in range(2)]; dvtb = [Dep(), Dep()]
            it = 0
            pr = 0
            for ti, (col0, n, seq) in enumerate(tiles):
                load_x_and_norm(nb, ti, l, first_layer, amix, 0)
                hrhs = lambda kk: nb.hT[:, kk, :n]
                for j in range(3):
                    pi = pr % 6; pr += 1; ps = psum[pi]; dp = dps[pi]
                    mmk(ps, dp, wdn, j * 128, 128, hrhs, NCH, n, [dw, nb.dhT])
                    k.op(k.act, lambda j=j, ps=ps: nc.scalar.activation(out=cqf[:, j, :n], in_=ps[:, :n], func=AF.Copy), reads=[dp], writes=[dcqf])
                for j in range(2):
                    pi = pr % 6; pr += 1; ps = psum[pi]; dp = dps[pi]
                    mmk(ps, dp, wdn, 384 + j * 128, 128, hrhs, NCH, n, [dw, nb.dhT])
                    k.op(k.act, lambda j=j, ps=ps: nc.scalar.activation(out=ckf[:, j, :n], in_=ps[:, :n], func=AF.Copy), reads=[dp], writes=[dckf])
                pi = pr % 6; pr += 1; ps1 = psum[pi]; dp1 = dps[pi]
                mmk(ps1, dp1, wkr, 0, 96, hrhs, NCH, n, [dw, nb.dhT])
                pi = pr % 6; pr += 1; ps2 = psum[pi]; dp2 = dps[pi]
                mmk(ps2, dp2, wkrs, 0, 96, hrhs, NCH, n, [dw, nb.dhT])
                s_ = it % 2; it += 1
                k.op(k.dve, lambda: nc.vector.tensor_tensor(out=t1[s_][64:96, :n], in0=ps1[64:96, :n], in1=rc[64:96, col0:col0 + n], op=ALU.mult),
                     reads=[dp1, dw], writes=[dt1[s_]])
                k.op(k.dve, lambda: nc.vector.tensor_tensor(out=t2[s_][64:96, :n], in0=ps2[64:96, :n], in1=rsn[64:96, col0:col0 + n], op=ALU.mult),
                     reads=[dp2, dw], writes=[dt2[s_]])
                k.op(k.pool, lambda: nc.gpsimd.tensor_tensor(out=krf[64:96, :n], in0=t1[s_][64:96, :n], in1=t2[s_][64:96, :n], op=ALU.add),
                     reads=[dt1[s_], dt2[s_]], writes=[dkrf])
                k.op(k.pool, lambda: nc.gpsimd.tensor_copy(out=krb[64:96, :n], in_=krf[64:96, :n]), reads=[dkrf], writes=[dkrb])
                k.dma(STQ[0], kT_s[D:D + 32, col0:col0 + n], krb[64:96, :n], reads=[dkrb], writes=[dkk])
                rstd(nb, lambda c: cqf[:, c, :n], dcqf, 3, n)
                apply_norm(nb, lambda c: cqf[:, c, :n], dcqf, lambda c: cqn[:, c, :n], dcqn, 3, n, lambda c: gq[:, c:c + 1], None)
                rstd(nb, lambda c: ckf[:, c, :n], dckf, 2, n)
                apply_norm(nb, lambda c: ckf[:, c, :n], dckf, lambda c: ckn32[:, c, :n], dckn32, 2, n, lambda c: gkv[:, c:c + 1], None)
                k.op(k.pool, lambda: nc.gpsimd.tensor_copy(out=cknb[:, :, :n], in_=ckn32[:, :, :n]), reads=[dckn32], writes=[dcknb])
                for h in range(16):
                    pi = pr % 6; pr += 1; ps1 = psum[pi]; dp1 = dps[pi]
                    mmk(ps1, dp1, wuq, h * 96, 96, lambda kk: cqn[:, kk, :n], 3, n, [dw, dcqn])
                    pi = pr % 6; pr += 1; ps2 = psum[pi]; dp2 = dps[pi]
                    mmk(ps2, dp2, wuqs, h * 96, 96, lambda kk: cqn[:, kk, :n], 3, n, [dw, dcqn])
                    s_ = it % 2; it += 1
                    k.op(k.act, lambda h=h, ps1=ps1: nc.scalar.activation(out=qall[0:64, h, :n], in_=ps1[0:64, :n], func=AF.Copy), reads=[dp1], writes=[dqall])
                    k.op(k.dve, lambda s_=s_, ps1=ps1: nc.vector.tensor_tensor(out=t1[s_][64:96, :n], in0=ps1[64:96, :n], in1=rc[64:96, col0:col0 + n], op=ALU.mult),
                         reads=[dp1, dw], writes=[dt1[s_]])
                    k.op(k.dve, lambda s_=s_, ps2=ps2: nc.vector.tensor_tensor(out=t2[s_][64:96, :n], in0=ps2[64:96, :n], in1=rsn[64:96, col0:col0 + n], op=ALU.mult),
                         reads=[dp2, dw], writes=[dt2[s_]])
                    k.op(k.pool, lambda s_=s_, h=h: nc.gpsimd.tensor_tensor(out=qall[64:96, h, :n], in0=t1[s_][64:96, :n], in1=t2[s_][64:96, :n], op=ALU.add),
                         reads=[dt1[s_], dt2[s_]], writes=[dqall])
                k.dma(STQ[0], qT_s[:, col0:col0 + n].rearrange("(h r) t -> r h t", r=96), qall[0:96, :, :n], reads=[dqall], writes=[dq])
                for m in range(NCH):
                    pi = pr % 6; pr += 1; ps = psum[pi]; dp = dps[pi]
                    mmk(ps, dp, wuk, m * 128, 128, lambda kk: cknb[:, kk, :n], 2, n, [dw, dcknb])
                    k.op(k.act, lambda m=m, ps=ps: nc.scalar.activation(out=ko[:, m, :n], in_=ps[:, :n], func=AF.Copy), reads=[dp], writes=[dko])
                k.dma(STQ[0], kT_s[0:D, col0:col0 + n].rearrange("(c p) t -> p c t", p=128), ko[:, :, :n], reads=[dko], writes=[dkk])
                for s0 in range(0, n, 128):
                    nt = min(128, n - s0)
                    tok0 = col0 + s0
                    b_ = it % 2; it += 1
                    for fb in range(2):
                        pi = pr % 6; pr += 1; ps = psum[pi]; dp = dps[pi]
                        for c in range(2):
                            mm(ps[:nt, :512], cknb[:, c, s0:s0 + nt], wuv[:, c, fb * 512:(fb + 1) * 512], c == 0, c == 1, [dw, dcknb], [dp], last=(c == 1))
                        k.op(k.dve, lambda fb=fb, ps=ps: nc.vector.tensor_copy(out=vtb[b_][:nt, fb * 512:(fb + 1) * 512], in_=ps[:nt, :512]),
                             reads=[dp], writes=[dvtb[b_]])
                    k.dma(STQ[0], v_s[tok0:tok0 + nt, :], vtb[b_][:nt, :], reads=[dvtb[b_]], writes=[dv])
                    for c in range(2):
                        pi = pr % 6; pr += 1; ps = psum[pi]; dp = dps[pi]
                        mm(ps[:nt, :128], ckn32[:, c, s0:s0 + nt], ident[:, :], True, True, [dckn32, d_const], [dp])
                        k.op(k.act, lambda c=c, ps=ps: nc.scalar.activation(out=ctm[b_][:nt, c * 128:(c + 1) * 128], in_=ps[:nt, :128], func=AF.Copy),
                             reads=[dp], writes=[dctm[b_]])
                    k.dma(STQ[0], O["a_ckv"][tok0:tok0 + nt, :], ctm[b_][:nt, :], reads=[dctm[b_]])
                    pi = pr % 6; pr += 1; ps = psum[pi]; dp = dps[pi]
                    mm(ps[:nt, :32], krf[64:96, s0:s0 + nt], ident[64:96, 64:96], True, True, [dkrf, d_const], [dp])
                    k.op(k.act, lambda ps=ps: nc.scalar.activation(out=krtm[b_][:nt, :], in_=ps[:nt, :32], func=AF.Copy), reads=[dp], writes=[dkrtm[b_]])
                    k.dma(STQ[0], O["a_kr"][tok0:tok0 + nt, :], krtm[b_][:nt, :], reads=[dkrtm[b_]])
            k.barrier()
        cur[0] = None

    def phase_attn_b(lnext):
        Lc = P
        STQ[0] = k.sp
        with ExitStack() as es:
            cur[0] = es
            mt, dm = load_masks("maskB", 4)
            mtp, dmp = load_masks("maskBp", 4)
            ms = sb("maskBs", [32, 2, 512], BF16)
            with ExitStack() as es2:
                cur[0] = es2
                st = sb("mstg2", [32, 2, 512], F32); dst_ = Dep()
                k.dma(k.sp, st[:, 0, :], I["maskBs"], writes=[dst_])
                k.dma(k.sp, st[:, 1, :], I["maskBsp"], writes=[dst_])
                k.op(k.dve, lambda: nc.vector.tensor_copy(out=ms[:], in_=st[:]), reads=[dst_], writes=[dm])
                k.barrier()
            cur[0] = es
            abs_ = [AttnBufs(Lc, 64) for _ in range(2)]
            for ab in abs_:
                ab.zero_q()
            Kn = [sb("Kn", [128, TT + 2 * Lc], BF16) for _ in range(2)]; dKn = [Dep(), Dep()]
            NE, NL, NA = 3, 4, 3
            et = [sb("et", [128, 512], F32) for _ in range(NE)]; det = [Dep() for _ in range(NE)]
            Lt = [sb("Lt", [128, 512], BF16) for _ in range(NL)]; dLt = [Dep() for _ in range(NL)]
            at = [sb("at", [128, 512], BF16) for _ in range(NA)]; dat = [Dep() for _ in range(NA)]
            Ra = [sb("Ra", [128, 512], BF16) for _ in range(2)]; dRa = [Dep(), Dep()]
            ot = [sb("otb", [128, 512], BF16) for _ in range(2)]; dot_ = [Dep(), Dep()]
            c = {"s": 0, "c": 0, "o": 0, "l": 0, "e": 0, "a": 0}

            seq = []
            tstep = [0]

            def advance():
                t = tstep[0]; tstep[0] += 1
                if t < len(seq):
                    seq[t][0]()
                if 0 <= t - 2 < len(seq):
                    seq[t - 2][2]()
                if t < len(seq):
                    seq[t][1]()
                if 0 <= t - 3 < len(seq):
                    seq[t - 3][3]()

            pstore = []

            def flush():
                while tstep[0] < len(seq) + 3:
                    advance()
                seq.clear(); tstep[0] = 0
                while pstore:
                    pstore.pop(0)()

            def run_q(ab, Knb, dKnb, pbase, hsel, h, qcol0, nq, blocks):
                oc = c["o"]; c["o"] += 1
                ri = oc % 2
                psO = psum[6 + ri]; dO = dps[6 + ri]
                nb_ = len(blocks)
                for j, (kcol, nk, slot, mneg, mpos) in enumerate(blocks):
                    st = {}

                    def fA(kcol=kcol, nk=nk, mneg=mneg, st=st):
                        si = c["s"] % 3; c["s"] += 1
                        psS = psum[si]; dS = dps[si]
                        mm(psS[:nk, :nq], ab.Kf[:, kcol:kcol + nk], ab.Qz[pbase // 64][:, qcol0:qcol0 + nq],
                           True, mneg is None, [ab.dK, ab.dQ], [dS], last=(mneg is None))
                        if mneg is not None:
                            mm(psS[:nk, :nq], ident_bf[:nk, :nk], mneg[:nk, :nq], False, True, [d_const, dm], [dS])
                        ei = c["e"] % NE; c["e"] += 1
                        li = c["l"] % NL; c["l"] += 1
                        k.op(k.act, lambda: nc.scalar.activation(out=et[ei][:nk, :nq], in_=psS[:nk, :nq], func=AF.Exp, scale=0.125),
                             reads=[dS], writes=[det[ei]])
                        st["ei"] = ei; st["li"] = li

                    def fL(nk=nk, st=st):
                        ei = st["ei"]; li = st["li"]
                        k.op(k.act, lambda: nc.scalar.activation(out=Lt[li][:nk, :nq], in_=et[ei][:nk, :nq], func=AF.Ln, bias=1.0),
                             reads=[det[ei]], writes=[dLt[li]])

                    def f2(j=j, kcol=kcol, nk=nk, mpos=mpos, st=st):
                        li = st["li"]
                        ci = 3 + c["c"] % 3; c["c"] += 1
                        psC = psum[ci]; dC = dps[ci]
                        mm(psC[:nk, :nq], tri_bf[:nk, :nk], Lt[li][:nk, :nq], True, False, [d_const, dLt[li]], [dC], last=False)
                        if j > 0:
                            mm(psC[:nk, :nq], ones_bf[:, :nk], Ra[j % 2][:, :nq], False, False, [d_const, dRa[j % 2]], [dC], last=False)
                        if mpos is not None:
                            mm(psC[:nk, :nq], ident_bf[:nk, :nk], mpos[:nk, :nq], False, False, [d_const, dm, dmp], [dC], last=False)
                        mm(psC[:nk, :nq], Knb[:, kcol:kcol + nk], ab.Qz[pbase // 64][:, qcol0:qcol0 + nq], False, True,
                           [dKnb, ab.dQ], [dC])
                        ai = c["a"] % NA; c["a"] += 1
                        k.op(k.act, lambda: nc.scalar.activation(out=at[ai][:nk, :nq], in_=psC[:nk, :nq], func=AF.Exp, scale=-1.0),
                             reads=[dC], writes=[dat[ai]])
                        st["ai"] = ai
                        rn = (j + 1) % 2
                        if j + 1 < nb_:
                            if j == 0:
                                if nk < 128:
                                    k.op(k.dve, lambda: nc.vector.memset(Ra[rn][:, :nq], 0.0), writes=[dRa[rn]])
                                k.op(k.dve, lambda: nc.vector.tensor_copy(out=Ra[rn][:nk, :nq], in_=Lt[li][:nk, :nq]), reads=[dLt[li]], writes=[dRa[rn]])
                            else:
                                k.op(k.dve, lambda: nc.vector.tensor_tensor(out=Ra[rn][:, :nq], in0=Ra[j % 2][:, :nq], in1=Lt[li][:, :nq], op=ALU.add),
                                     reads=[dRa[j % 2], dLt[li]], writes=[dRa[rn]])

                    def f3(j=j, nk=nk, slot=slot, st=st):
                        ai = st["ai"]
                        mm(psO[:, :nq], ab.Va[:nk, slot, :, :], at[ai][:nk, :nq], j == 0, j == nb_ - 1, [ab.dV, dat[ai]], [dO], last=(j == nb_ - 1))
                        if j == nb_ - 1:
                            k.op(k.dve, lambda: nc.vector.tensor_copy(out=ot[ri][pbase:pbase + 64, :nq], in_=psO[pbase:pbase + 64, :nq]), reads=[dO], writes=[dot_[ri]])
                            while pstore:
                                pstore.pop(0)()
                            pstore.append(lambda: k.dma(STQ[0], attnT[h * 64:(h + 1) * 64, qcol0:qcol0 + nq], ot[ri][pbase:pbase + 64, :nq], reads=[dot_[ri]], writes=[dattn]))

                    seq.append((fA, fL, f2, f3))
                    advance()

            def load_unit(u):
                ab = abs_[u % 2]
                load_K(ab, 0, u * 128, 128)
                ab.load_q_pair(u)
                load_V(ab, 0, u * 128)
                load_V(ab, 1, u * 128 + 64)
                k.op(k.pool, lambda u=u, ab=ab: nc.gpsimd.tensor_scalar(out=Kn[u % 2][:, :], in0=ab.Kf[:, :], scalar1=-0.125, scalar2=1.0, op0=ALU.mult, op1=ALU.mult),
                     reads=[ab.dK], writes=[dKn[u % 2]])

            bg = AdaBg([], 4, [7, 7])
            load_unit(0)
            bg.load(2)
            for u in range(8):
                ab = abs_[u % 2]
                if u + 1 < 8:
                    flush()
                    load_unit(u + 1)
                bg.step(2)
                for (pbase, hsel, h) in [(0, 0, 2 * u), (64, 1, 2 * u + 1)]:
                    for qi in range(NPT):
                        blocks = []
                        for kt in range(4 * qi + 3, -1, -1):
                            dg = kt >= 4 * qi
                            blocks.append((kt * 128, 128, kt, mt[:, kt - 4 * qi, :] if dg else None, mtp[:, kt - 4 * qi, :] if dg else None))
                        run_q(ab, Kn[u % 2], dKn[u % 2], pbase, hsel, h, qi * 512, 512, blocks)
                    for b in range(2):
                        blocks = [(S + b * T_S, T_S, ab.slot_new(b), ms[:, 0, :], ms[:, 1, :])]
                        for t in range(ab.ncs - 1, -1, -1):
                            blocks.append((ab.kcol(b, t * 128), 128, ab.slot_cache(b, t), None, None))
                        run_q(ab, Kn[u % 2], dKn[u % 2], pbase, hsel, h, S + b * T_S, T_S, blocks)
            flush()
            bg.finish()
            k.barrier()
        cur[0] = None
        STQ[0] = k.act

    one_t = sb("one_t", [128, 1], F32)
    k.op(k.pool, lambda: nc.gpsimd.memset(one_t[:], 1.0), writes=[d_const])

    phase_adaln0()
    wo_names = ["w_a_o", "w_b_o", "w_c_o", "w_d_o"]
    for l in range(depth):
        mix = l % 4
        first = (l == 0)
        if mix == 0:
            phase_proj_a(l, first)
            phase_attn_softmax(0, l + 1)
        elif mix == 1:
            phase_qkv(l, 1, first)
            phase_attn_b(l + 1)
        else:
            phase_qkv(l, mix, first)
            phase_attn_softmax(mix, l + 1)
        with ExitStack() as eL:
            cur[0] = eL
            win = sb("win", [128, NCH, 2 * HID], BF16); dwin = Dep()
            phase_oproj(l, I[wo_names[mix]], first, win, dwin)
            phase_ffn(l, l == depth - 1, win, dwin)
        cur[0] = None
    phase_final()
    k.barrier()
    return nc, k


ROPE_THETA = 500000.0


def _consts(S, P):
    TT = S + 64
    c = {}
    c["ident"] = np.eye(128, dtype=np.float32)
    j = np.arange(128)[:, None]; kk = np.arange(128)[None, :]
    c["tri"] = (j >= kk).astype(np.float32)
    c["antiI"] = (j + kk == 127).astype(np.float32)
    p = np.arange(128)[:, None]; ql = np.arange(512)[None, :]
    qc = ql // 64

    def kc_of(r):
        return np.floor_divide(128 * r + p, 64)
    c["maskA"] = (np.stack([(kc_of(r) <= qc) for r in range(4)]).astype(np.float32) - 1.0) * 30000.0
    mB = np.stack([((128 * r + p) < ql) for r in range(4)]).astype(np.float32)
    c["maskB"] = (mB - 1.0) * 30000.0
    c["maskBp"] = (1.0 - mB) * 30000.0
    mb = np.zeros((32, 512), np.float32)
    mb[:, :32] = (np.arange(32)[:, None] < np.arange(32)[None, :])
    c["maskBs"] = (mb - 1.0) * 30000.0
    c["maskBsp"] = (1.0 - mb) * 30000.0
    c["maskC"] = (np.stack([((kc_of(r) <= qc) & (kc_of(r) >= qc - 2)) for r in range(-1, 4)]).astype(np.float32) - 1.0) * 30000.0
    c["maskD"] = (np.stack([((kc_of(r) <= qc) & (kc_of(r) >= qc - 8)) for r in range(-4, 4)]).astype(np.float32) - 1.0) * 30000.0
    c["maskD"] = np.ascontiguousarray(c["maskD"][:, ::-1, :])
    pos = np.concatenate([np.arange(S), P + np.arange(32), P + np.arange(32)]).astype(np.float32)
    ca = np.zeros((128, TT), np.float32); sa = np.zeros((128, TT), np.float32)
    inv = (ROPE_THETA ** (-np.arange(16, dtype=np.float32) * 2.0 / 32)).astype(np.float32)
    ang = pos[None, :] * inv[:, None]
    ca[64:80] = np.cos(ang); ca[80:96] = np.cos(ang)
    sa[64:80] = -np.sin(ang); sa[80:96] = np.sin(ang)
    c["ropeA_cos"] = ca; c["ropeA_sin"] = sa
    cc = np.ones((128, TT), np.float32); sc = np.zeros((128, TT), np.float32)
    inv = (ROPE_THETA ** (-np.arange(8, dtype=np.float32) * 2.0 / 16)).astype(np.float32)
    ang = pos[None, :] * inv[:, None]
    for hb in (0, 64):
        cc[hb:hb + 8] = np.cos(ang); cc[hb + 8:hb + 16] = np.cos(ang)
        sc[hb:hb + 8] = -np.sin(ang); sc[hb + 8:hb + 16] = np.sin(ang)
    c["ropeC_cos"] = cc; c["ropeC_sin"] = sc
    return c


def _prep(inp, i, S, P, consts):
    f = np.ascontiguousarray
    m = dict(consts)
    xs = inp["x_sample"]
    m["xT"] = f(np.concatenate([inp["x_prompt"][i].T, xs[2 * i].T, xs[2 * i + 1].T], axis=1))
    cv = np.stack([inp["c_prompt"][i], inp["c_sample"][2 * i], inp["c_sample"][2 * i + 1]])
    m["cT"] = f(cv.reshape(3, 8, 128).transpose(2, 1, 0))
    m["w_mod"] = inp["w_mod"]
    m["bmodT"] = f(inp["b_mod"].reshape(4, 48, 128).transpose(2, 0, 1))
    m["gmixT"] = f(inp["g_mix"].reshape(4, 8, 128).transpose(2, 0, 1))
    m["gffnT"] = f(inp["g_ffn"].reshape(4, 8, 128).transpose(2, 0, 1))
    m["gfinT"] = f(inp["g_final"].reshape(8, 128).T)
    m["w_ffn_in"] = inp["w_ffn_in"]; m["w_ffn_out"] = inp["w_ffn_out"]
    m["w_a_down"] = inp["w_a_down"][0]
    m["gaqT"] = f(inp["g_a_q"][0].reshape(3, 128).T); m["gakvT"] = f(inp["g_a_kv"][0].reshape(2, 128).T)
    m["w_a_uq"] = inp["w_a_uq"][0]
    m["w_a_uk"] = inp["w_a_uk"][0].reshape(256, 1024); m["w_a_uv"] = inp["w_a_uv"][0].reshape(256, 1024)
    m["w_a_o"] = inp["w_a_o"][0]
    sl = slice(2 * i, 2 * i + 2)
    m["cache_a_ckv"] = f(inp["cache_a_ckv"][0, sl]); m["cache_a_kr"] = f(inp["cache_a_krope"][0, sl])
    m["w_b_qkv"] = inp["w_b_qkv"][0]; m["w_b_o"] = inp["w_b_o"][0]
    m["cache_b_k"] = f(inp["cache_b_k"][0, sl]).reshape(2, P, 1024); m["cache_b_v"] = f(inp["cache_b_v"][0, sl]).reshape(2, P, 1024)
    m["w_c_qkv"] = inp["w_c_qkv"][0]
    b = inp["b_c_qkv"][0]
    m["bcT"] = f(b.reshape(12, 128).T)
    idx = np.arange(1536); d = idx % 64
    src = np.where((idx < 1280) & (d < 8), idx + 8, np.where((idx < 1280) & (d >= 8) & (d < 16), idx - 8, idx))
    m["bcswT"] = f(b[src].reshape(12, 128).T)
    m["bc_row"] = f(b.reshape(1, 1536))
    m["sink_c"] = f(inp["sink_c"].reshape(1, 16))
    m["w_c_o"] = inp["w_c_o"][0]
    lc = inp["cache_c_k"].shape[2]
    m["cache_c_k"] = f(inp["cache_c_k"][0, sl]).reshape(2, lc, 256); m["cache_c_v"] = f(inp["cache_c_v"][0, sl]).reshape(2, lc, 256)
    m["w_d_qkv"] = inp["w_d_qkv"][0]; m["rel_bias_d"] = inp["rel_bias_d"][0]; m["w_d_o"] = inp["w_d_o"][0]
    ld = inp["cache_d_k"].shape[2]
    m["cache_d_k"] = f(inp["cache_d_k"][0, sl]).reshape(2, ld, 1024); m["cache_d_v"] = f(inp["cache_d_v"][0, sl]).reshape(2, ld, 1024)
    return {k_: np.ascontiguousarray(v, dtype=np.float32) for k_, v in m.items()}


def _assemble(res, S, P, lc, ld):
    n = len(res)
    f32 = np.float32
    yp = np.stack([r["yT"][:, :S].T for r in res]).astype(f32)
    ys = np.stack([r["yT"][:, S + 32 * b:S + 32 * (b + 1)].T for r in res for b in range(2)]).astype(f32)

    def pp(name, shp):
        return np.stack([r[name][:S].reshape((S,) + shp) for r in res])[None].astype(f32)

    def ss(name, shp):
        return np.stack([r[name][S + 32 * b:S + 32 * (b + 1)].reshape((32,) + shp) for r in res for b in range(2)])[None].astype(f32)

    def p2(name, shp):
        return np.stack([r[name].reshape(shp) for r in res])[None].astype(f32)

    def s2(name, shp):
        return np.stack([r[name][b].reshape(shp) for r in res for b in range(2)])[None].astype(f32)

    kp_c = min(128, S); kp_d = min(512, S)
    return (yp, ys, pp("a_ckv", (256,)), pp("a_kr", (32,)), ss("a_ckv", (256,)), ss("a_kr", (32,)),
            pp("b_k", (16, 64)), pp("b_v", (16, 64)), ss("b_k", (16, 64)), ss("b_v", (16, 64)),
            p2("c_k_p", (kp_c, 4, 64)), p2("c_v_p", (kp_c, 4, 64)), s2("c_k_s", (lc, 4, 64)), s2("c_v_s", (lc, 4, 64)),
            p2("d_k_p", (kp_d, 16, 64)), p2("d_v_p", (kp_d, 16, 64)), s2("d_k_s", (ld, 16, 64)), s2("d_v_s", (ld, 16, 64)))


def kernel(**inputs):
    inp = {k_: np.asarray(v) for k_, v in inputs.items()}
    nb, S = inp["x_prompt"].shape[0], inp["x_prompt"].shape[1]
    P = inp["cache_a_ckv"].shape[2]
    nc, _ = build(S, P)
    consts = _consts(S, P)
    in_maps = [_prep(inp, i, S, P, consts) for i in range(nb)]
    res = run_bass_kernel_spmd(nc, in_maps, core_ids=list(range(nb)))
    return _assemble(res.results, S, P, inp["cache_c_k"].shape[2], inp["cache_d_k"].shape[2])
```

```python
import numpy as np
import concourse.bass as bass
import concourse.mybir as mybir
from concourse.bass_utils import run_bass_kernel_spmd

F32 = mybir.dt.float32
BF16 = mybir.dt.bfloat16
AF = mybir.ActivationFunctionType
ALU = mybir.AluOpType

NDS = 40


class Dep:
    __slots__ = ("w", "r")

    def __init__(self):
        self.w = None
        self.r = {}


class Eng:
    def __init__(self, nc, eng, name):
        self.e = eng
        self.key = name
        self.sem = nc.alloc_semaphore(name)
        self.n = 0
        self.seen = {}


class K:
    def __init__(self, nc):
        self.nc = nc
        self.pe = Eng(nc, nc.tensor, "s_pe")
        self.act = Eng(nc, nc.scalar, "s_act")
        self.dve = Eng(nc, nc.vector, "s_dve")
        self.pool = Eng(nc, nc.gpsimd, "s_pool")
        self.sp = Eng(nc, nc.sync, "s_sp")
        self.engs = [self.pe, self.act, self.dve, self.pool, self.sp]
        self.dsems = [nc.alloc_semaphore("s_d%d" % i) for i in range(NDS)]
        self.dval = [0] * NDS
        self.dnext = 0
        self.ninst = 0

    def _wait(self, E, ev):
        k, s, v = ev
        if E.seen.get(k, 0) >= v:
            return
        E.e.wait_ge(s, v)
        E.seen[k] = v

    def _deps(self, E, reads, writes, skip_self=False):
        for b in reads:
            if b.w is not None and not (skip_self and b.w[0] == E.key):
                self._wait(E, b.w)
        for b in writes:
            if b.w is not None and not (skip_self and b.w[0] == E.key):
                self._wait(E, b.w)
            for k, (s, v) in b.r.items():
                if skip_self and k == E.key:
                    continue
                self._wait(E, (k, s, v))

    def _mark(self, ev, reads, writes):
        k, s, v = ev
        for b in reads:
            o = b.r.get(k)
            if o is None or o[1] < v:
                b.r[k] = (s, v)
        for b in writes:
            b.w = ev
            b.r = {}

    def op(self, E, fn, reads=(), writes=(), inc=True, skip_self=False):
        self._deps(E, reads, writes, skip_self)
        ins = fn()
        self.ninst += 1
        if inc:
            E.n += 1
            ins.then_inc(E.sem, 1)
            ev = (E.key, E.sem, E.n)
        else:
            ev = (E.key, E.sem, E.n + 1)
        self._mark(ev, reads, writes)
        return ins

    def dma(self, Q, out, in_, reads=(), writes=(), **kw):
        i = self.dnext
        self.dnext = (i + 1) % NDS
        key = "d%d" % i
        if self.dval[i] > 0:
            self._wait(Q, (key, self.dsems[i], self.dval[i]))
        self._deps(Q, reads, writes)
        self.dval[i] += 16
        Q.e.dma_start(out=out, in_=in_, **kw).then_inc(self.dsems[i], 16)
        self.ninst += 1
        self._mark((key, self.dsems[i], self.dval[i]), reads, writes)

    def barrier(self):
        sp = self.sp
        for i in range(NDS):
            if self.dval[i] > 0:
                self._wait(sp, ("d%d" % i, self.dsems[i], self.dval[i]))
        for E in self.engs:
            if E is not sp and E.n > 0:
                self._wait(sp, (E.key, E.sem, E.n))
        sp.n += 1
        sp.e.sem_inc(sp.sem, 1)
        for E in self.engs:
            if E is not sp:
                self._wait(E, (sp.key, sp.sem, sp.n))
                for F in self.engs:
                    E.seen[F.key] = max(E.seen.get(F.key, 0), F.n)
                for i in range(NDS):
                    E.seen["d%d" % i] = self.dval[i]

from contextlib import ExitStack

D = 1024
NCH = 8
T_S = 32
HID = 2816
HCH = 22
EPS = 1e-6


def build(S, P, depth=4):
    nc = bass.Bass("TRN2", target_bir_lowering=False)
    k = K(nc)
    TT = S + 2 * T_S
    NPT = S // 512
    tiles = [(t * 512, 512, 0) for t in range(NPT)] + [(S, T_S, 1), (S + T_S, T_S, 2)]
    LC = [P, P, min(128, P), min(512, P)]
    NKV = [16, 16, 4, 16]

    def din(name, shape, dt=F32):
        return nc.dram_tensor(name, list(shape), dt, kind="ExternalInput").ap()

    def dout(name, shape, dt=F32):
        return nc.dram_tensor(name, list(shape), dt, kind="ExternalOutput").ap()

    def dscr(name, shape, dt):
        return nc.dram_tensor(name, list(shape), dt, kind="Internal").ap()

    I = {}
    for nm, shp in [("xT", [D, TT]), ("cT", [128, NCH, 3]), ("w_mod", [4, D, 6 * D]), ("bmodT", [128, 4, 48]),
                    ("gmixT", [128, 4, NCH]), ("gffnT", [128, 4, NCH]), ("gfinT", [128, NCH]),
                    ("w_ffn_in", [4, D, 2 * HID]), ("w_ffn_out", [4, HID, D]), ("ident", [128, 128]),
                    ("tri", [128, 128]), ("antiI", [128, 128]),
                    ("w_a_down", [D, 672]), ("gaqT", [128, 3]), ("gakvT", [128, 2]), ("w_a_uq", [384, 1536]),
                    ("w_a_uk", [256, 1024]), ("w_a_uv", [256, 1024]), ("w_a_o", [D, D]),
                    ("cache_a_ckv", [2, P, 256]), ("cache_a_kr", [2, P, 32]),
                    ("ropeA_cos", [128, TT]), ("ropeA_sin", [128, TT]), ("maskA", [4, 128, 512]),
                    ("w_b_qkv", [D, 3 * D]), ("w_b_o", [D, D]), ("cache_b_k", [2, P, D]), ("cache_b_v", [2, P, D]),
                    ("maskB", [4, 128, 512]), ("maskBp", [4, 128, 512]), ("maskBs", [32, 512]), ("maskBsp", [32, 512]),
                    ("w_c_qkv", [D, 1536]), ("bcT", [128, 12]), ("bcswT", [128, 12]), ("bc_row", [1, 1536]), ("sink_c", [1, 16]),
                    ("w_c_o", [D, D]),
                    ("cache_c_k", [2, LC[2], 256]), ("cache_c_v", [2, LC[2], 256]),
                    ("ropeC_cos", [128, TT]), ("ropeC_sin", [128, TT]), ("maskC", [5, 128, 512]),
                    ("chunkU", [64, TT + 2 * LC[2]]), ("chunkW", [64, TT]),
                    ("w_d_qkv", [D, 3 * D]), ("rel_bias_d", [16, 257]), ("w_d_o", [D, D]),
                    ("cache_d_k", [2, LC[3], D]), ("cache_d_v", [2, LC[3], D]), ("maskD", [8, 128, 512])]:
        I[nm] = din(nm, shp)
    O = {}
    for nm, shp in [("yT", [D, TT]), ("a_ckv", [TT, 256]), ("a_kr", [TT, 32]), ("b_k", [TT, D]), ("b_v", [TT, D]),
                    ("c_k_p", [min(128, S), 256]), ("c_v_p", [min(128, S), 256]),
                    ("c_k_s", [2, LC[2], 256]), ("c_v_s", [2, LC[2], 256]),
                    ("d_k_p", [min(512, S), D]), ("d_v_p", [min(512, S), D]),
                    ("d_k_s", [2, LC[3], D]), ("d_v_s", [2, LC[3], D])]:
        O[nm] = dout(nm, shp)

    xres = dscr("xres", [D, TT], F32)
    dxres = [Dep() for _ in tiles]
    attnT = dscr("attnT", [D, TT], BF16)
    dattn = Dep()
    TTX = TT + 2 * P
    qT_s = dscr("qT_s", [16 * 96, TT], BF16)
    kT_s = dscr("kT_s", [D + 32, TTX], BF16)
    v_s = dscr("v_s", [TTX, D], BF16)
    dq = Dep(); dkk = Dep(); dv = Dep()
    ext_s = dscr("ext_s", [16, 1536], F32)

    cur = [None]
    uid = [0]

    def sb(name, shape, dt):
        uid[0] += 1
        if cur[0] is None:
            return nc.alloc_sbuf_tensor("%s_%d" % (name, uid[0]), list(shape), dt)
        return cur[0].enter_context(nc.sbuf_tensor("%s_%d" % (name, uid[0]), list(shape), dt))

    ones_bf = sb("ones_bf", [128, 128], BF16); d_const = Dep()
    ident = sb("ident_s", [128, 128], F32)
    mod = sb("mod", [128, 4, 48, 3], F32); d_mod = Dep()
    amix = sb("amix", [128, 4, NCH, 3], F32)
    affn = sb("affn", [128, 4, NCH, 3], F32)
    gfin = sb("gfin", [128, NCH], F32)
    psum = [nc.alloc_psum_tensor("ps%d" % i, [128, 512], F32) for i in range(8)]
    dps = [Dep() for _ in range(8)]

    k.op(k.pool, lambda: nc.gpsimd.memset(ones_bf[:], 1.0), writes=[d_const])
    k.dma(k.sp, ident[:], I["ident"], writes=[d_const])
    k.dma(k.sp, gfin[:], I["gfinT"], writes=[d_mod])

    STQ = [k.act]
    cast_rr = [0]

    def cast(out, in_, reads, writes):
        cast_rr[0] ^= 1
        if cast_rr[0]:
            k.op(k.dve, lambda: nc.vector.tensor_copy(out=out, in_=in_), reads=reads, writes=writes)
        else:
            k.op(k.pool, lambda: nc.gpsimd.tensor_copy(out=out, in_=in_), reads=reads, writes=writes)

    def mm(out, lhsT, rhs, start, stop, reads, writes, last=True):
        k.op(k.pe, lambda: nc.tensor.matmul(out, lhsT=lhsT, rhs=rhs, start=start, stop=stop),
             reads=reads, writes=writes, inc=last, skip_self=True)

    def mmk(ps, dp, w, wcol0, M, rhs_of, kc, n, reads, pbase=0):
        for kk in range(kc):
            mm(ps[pbase:pbase + M, :n], w[:, kk, wcol0:wcol0 + M], rhs_of(kk), kk == 0, kk == kc - 1,
               reads, [dp], last=(kk == kc - 1))

    ld_rr = [0]

    def load_w(stg, dstg, dst, ddst, W, kc, ncols, dst_col0=0):
        for k0 in range(0, kc, 2):
            kk = min(2, kc - k0)
            for c0 in range(0, ncols, 2048):
                nn = min(2048, ncols - c0)
                i = ld_rr[0] % len(stg); ld_rr[0] += 1
                view = stg[i][:, :, :].rearrange("p a b -> p (a b)")[:, 0:kk * nn].rearrange("p (k n) -> p k n", n=nn)
                src = W[k0 * 128:(k0 + kk) * 128, c0:c0 + nn].rearrange("(k p) n -> p k n", p=128)
                k.dma(k.sp, view, src, writes=[dstg[i]])
                cast(dst[:, k0:k0 + kk, dst_col0 + c0:dst_col0 + c0 + nn], view,
                     reads=[dstg[i]], writes=[ddst])

    def load_w_gen(stg, dstg, dst, ddst, W, kc, ncols):
        for k0 in range(0, kc, 2):
            kk = min(2, kc - k0)
            for c0 in range(0, ncols, 2048):
                nn = min(2048, ncols - c0)
                i = ld_rr[0] % len(stg); ld_rr[0] += 1
                view = stg[i][:, :, :].rearrange("p a b -> p (a b)")[:, 0:kk * nn].rearrange("p (k n) -> p k n", n=nn)
                src = W[k0 * 128:(k0 + kk) * 128, c0:c0 + nn].rearrange("(k p) n -> p k n", p=128)
                k.dma(k.sp, view, src, writes=[dstg[i]])
                cast(dst[:, k0:k0 + kk, c0:c0 + nn], view, reads=[dstg[i]], writes=[ddst])
                yield

    def mkstg(n=2):
        return [sb("stg", [128, NCH, 512], F32) for _ in range(n)], [Dep() for _ in range(n)]

    sc = sb("sc", [128, NCH, 3], F32); dsc = Dep()
    bm = sb("bm", [128, 4, 48], F32); dbm = Dep()
    gm = sb("gm", [128, 4, NCH], F32)
    gf = sb("gf", [128, 4, NCH], F32)
    k.dma(k.sp, sc[:], I["cT"], writes=[dsc])
    k.dma(k.sp, bm[:], I["bmodT"], writes=[dbm])
    k.dma(k.sp, gm[:], I["gmixT"], writes=[dbm])
    k.dma(k.sp, gf[:], I["gffnT"], writes=[dbm])
    k.op(k.act, lambda: nc.scalar.activation(out=sc[:], in_=sc[:], func=AF.Silu), reads=[dsc], writes=[dsc])

    class AdaBg:
        def __init__(self, layers, nbuf, banks):
            self.layers = [l for l in layers if l < depth]
            self.nbuf = nbuf
            self.nblk = 12 * len(self.layers)
            if self.nblk:
                self.stg = [sb("adstg", [128, NCH, 512], F32) for _ in range(nbuf)]
                self.dstg = [Dep() for _ in range(nbuf)]
            self.banks = banks
            self.bl = 0; self.bc = 0

        def load(self, n):
            for _ in range(n):
                if self.bl >= self.nblk:
                    return
                i = self.bl % self.nbuf
                l = self.layers[self.bl // 12]; blk = self.bl % 12
                src = I["w_mod"][l, :, blk * 512:(blk + 1) * 512].rearrange("(k p) n -> p k n", p=128)
                k.dma(k.sp, self.stg[i][:], src, writes=[self.dstg[i]])
                self.bl += 1

        def compute(self):
            while self.bc < self.bl:
                i = self.bc % self.nbuf
                li = self.bc // 12; blk = self.bc % 12
                ps = psum[self.banks[li]]; dp = dps[self.banks[li]]
                for m in range(4):
                    j = blk * 4 + m
                    for kk in range(NCH):
                        mm(ps[:, j * 3:(j + 1) * 3], self.stg[i][:, kk, m * 128:(m + 1) * 128], sc[:, kk, :],
                           kk == 0, kk == NCH - 1, [self.dstg[i], dsc], [dp], last=(kk == NCH - 1))
                self.bc += 1
                if blk == 11:
                    self.evac(li)

        def evac(self, li):
            l = self.layers[li]
            ps = psum[self.banks[li]]; dp = dps[self.banks[li]]
            pv = ps[:, 0:144].rearrange("p (j s) -> p j s", s=3)
            for s_ in range(3):
                k.op(k.dve, lambda s_=s_: nc.vector.tensor_tensor(out=mod[:, l, :, s_], in0=pv[:, :, s_], in1=bm[:, l, :], op=ALU.add),
                     reads=[dp, dbm], writes=[d_mod])
            for s_ in range(3):
                k.op(k.dve, lambda s_=s_: nc.vector.scalar_tensor_tensor(
                    out=amix[:, l, :, s_], in0=mod[:, l, 8:16, s_], scalar=1.0, in1=gm[:, l, :], op0=ALU.add, op1=ALU.mult),
                    reads=[d_mod, dbm], writes=[d_mod])
                k.op(k.dve, lambda s_=s_: nc.vector.scalar_tensor_tensor(
                    out=affn[:, l, :, s_], in0=mod[:, l, 32:40, s_], scalar=1.0, in1=gf[:, l, :], op0=ALU.add, op1=ALU.mult),
                    reads=[d_mod, dbm], writes=[d_mod])

        def step(self, n):
            self.compute()
            self.load(n)

        def finish(self):
            while self.bc < self.nblk:
                self.load(self.nbuf)
                self.compute()

    def phase_adaln0():
        with ExitStack() as es:
            cur[0] = es
            bg = AdaBg([0], 3, [0])
            bg.finish()
            k.barrier()
        cur[0] = None

    class NormBufs:
        def __init__(self):
            self.xt = sb("xt", [128, NCH, 512], F32); self.dxt = Dep()
            self.hT = sb("hT", [128, NCH, 512], BF16); self.dhT = Dep()
            self.sq = [sb("sq", [128, 512], BF16) for _ in range(2)]; self.dsq = [Dep(), Dep()]
            self.tmp = [sb("tmp", [128, 512], F32) for _ in range(2)]; self.dtmp = [Dep(), Dep()]
            self.rs = sb("rs", [128, 512], F32); self.drs = Dep()

    def rstd(nb, x_of, dx, nfe, n, ps_i=7):
        ps = psum[ps_i]; dp = dps[ps_i]
        for c in range(nfe):
            s = c % 2
            k.op(k.act, lambda c=c, s=s: nc.scalar.activation(out=nb.sq[s][:, :n], in_=x_of(c), func=AF.Square),
                 reads=[dx], writes=[nb.dsq[s]])
            mm(ps[:, :n], ones_bf[:], nb.sq[s][:, :n], c == 0, c == nfe - 1, [d_const, nb.dsq[s]], [dp], last=True)
        k.op(k.act, lambda: nc.scalar.activation(out=nb.rs[:, :n], in_=ps[:, :n], func=AF.Ln, scale=1.0 / (nfe * 128), bias=eps_t[:, 0:1]),
             reads=[dp, d_const], writes=[nb.drs])
        k.op(k.act, lambda: nc.scalar.activation(out=nb.rs[:, :n], in_=nb.rs[:, :n], func=AF.Exp, scale=-0.5),
             reads=[nb.drs], writes=[nb.drs])

    def apply_norm(nb, x_of, dx, out_of, dout_, nfe, n, scale_ap, shift_ap, extra_reads=()):
        for c in range(nfe):
            s = c % 2
            k.op(k.dve, lambda c=c, s=s: nc.vector.tensor_tensor(out=nb.tmp[s][:, :n], in0=x_of(c), in1=nb.rs[:, :n], op=ALU.mult),
                 reads=[dx, nb.drs], writes=[nb.dtmp[s]])
            if shift_ap is not None:
                k.op(k.pool, lambda c=c, s=s: nc.gpsimd.tensor_scalar(out=out_of(c), in0=nb.tmp[s][:, :n], scalar1=scale_ap(c),
                                                                   scalar2=shift_ap(c), op0=ALU.mult, op1=ALU.add),
                     reads=[nb.dtmp[s], d_mod] + list(extra_reads), writes=[dout_])
            else:
                k.op(k.pool, lambda c=c, s=s: nc.gpsimd.tensor_scalar(out=out_of(c), in0=nb.tmp[s][:, :n], scalar1=scale_ap(c),
                                                                   scalar2=1.0, op0=ALU.mult, op1=ALU.mult),
                     reads=[nb.dtmp[s], d_mod] + list(extra_reads), writes=[dout_])

    eps_t = sb("eps_t", [128, 1], F32)
    k.op(k.pool, lambda: nc.gpsimd.memset(eps_t[:], EPS), writes=[d_const])

    def xsrc(first_layer):
        return I["xT"] if first_layer else xres

    def load_x_and_norm(nb, ti, l, first_layer, a_tile, shift_j):
        col0, n, seq = tiles[ti]
        k.dma(k.sp, nb.xt[:, :, :n], xsrc(first_layer)[:, col0:col0 + n].rearrange("(c p) t -> p c t", p=128),
              reads=[] if first_layer else [dxres[ti]], writes=[nb.dxt])
        rstd(nb, lambda c: nb.xt[:, c, :n], nb.dxt, NCH, n)
        apply_norm(nb, lambda c: nb.xt[:, c, :n], nb.dxt, lambda c: nb.hT[:, c, :n], nb.dhT, NCH, n,
                   lambda c: a_tile[:, l, c, seq:seq + 1], lambda c: mod[:, l, shift_j * 8 + c, seq:seq + 1])

    def phase_oproj(l, w_o_dram, first_layer, win, dwin):
        prev = cur[0]
        with ExitStack() as es:
            cur[0] = es
            wo = sb("wo", [128, NCH, D], BF16); dwo = Dep()
            stg, dstg = mkstg(2)
            load_w(stg, dstg, wo, dwo, w_o_dram, NCH, D)
            gen = load_w_gen(stg, dstg, win, dwin, I["w_ffn_in"][l], NCH, 2 * HID)
            at = [sb("oat", [128, NCH, 512], BF16) for i in range(2)]
            dat = [Dep() for _ in range(2)]
            xt = [sb("oxt", [128, NCH, 512], F32) for i in range(2)]
            dx = [Dep() for _ in range(2)]
            def ld_tile(ti):
                col0, n, seq = tiles[ti]; b = ti % 2
                k.dma(k.sp, at[b][:, :, :n], attnT[:, col0:col0 + n].rearrange("(c p) t -> p c t", p=128),
                      reads=[dattn], writes=[dat[b]])
                k.dma(k.sp, xt[b][:, :, :n], xsrc(first_layer)[:, col0:col0 + n].rearrange("(c p) t -> p c t", p=128),
                      reads=[] if first_layer else [dxres[ti]], writes=[dx[b]])

            ld_tile(0)
            for ti, (col0, n, seq) in enumerate(tiles):
                b = ti % 2
                if ti + 1 < len(tiles):
                    ld_tile(ti + 1)
                for m in range(NCH):
                    pi = m % 4; ps = psum[pi]; dp = dps[pi]
                    mmk(ps, dp, wo, m * 128, 128, lambda kk: at[b][:, kk, :n], NCH, n, [dwo, dat[b]])
                    k.op(k.dve, lambda m=m, ps=ps: nc.vector.scalar_tensor_tensor(
                        out=xt[b][:, m, :n], in0=ps[:, :n], scalar=mod[:, l, 16 + m, seq:seq + 1], in1=xt[b][:, m, :n],
                        op0=ALU.mult, op1=ALU.add), reads=[dp, d_mod, dx[b]], writes=[dx[b]])
                k.dma(STQ[0], xres[:, col0:col0 + n].rearrange("(c p) t -> p c t", p=128), xt[b][:, :, :n],
                      reads=[dx[b]], writes=[dxres[ti]])
                next(gen, None); next(gen, None)
            for _ in gen:
                pass
            k.barrier()
        cur[0] = prev

    def phase_ffn(l, final, win, dwin):
        prev = cur[0]
        with ExitStack() as es:
            cur[0] = es
            wout = sb("wout", [128, HCH, D], BF16); dwout = Dep()
            with ExitStack() as es2:
                cur[0] = es2
                stg, dstg = mkstg(2)
                load_w(stg, dstg, wout, dwout, I["w_ffn_out"][l], HCH, D)
                k.barrier()
            cur[0] = es
            hTs = [sb("hTf", [128, NCH, 512], BF16) for _ in range(2)]; dhTs = [Dep(), Dep()]
            sq = [sb("sqf", [128, 512], BF16) for _ in range(2)]; dsq = [Dep(), Dep()]
            tmp = [sb("tmpf", [128, 512], F32) for _ in range(2)]; dtmp = [Dep(), Dep()]
            rs = sb("rsf", [128, 512], F32); drs = Dep()
            NX = 4
            xc = [sb("xcf", [128, 512], F32) for _ in range(NX)]; dxc = [Dep() for _ in range(NX)]
            actT = sb("actT", [128, HCH, 512], BF16); dact = Dep()
            sg = [sb("sg", [128, 512], F32) for i in range(2)]
            dsg = [Dep() for _ in range(2)]
            xi = [0]

            def ldx(ti, c):
                col0, n, seq = tiles[ti]
                i = xi[0] % NX; xi[0] += 1
                k.dma(k.sp, xc[i][:, :n], xres[c * 128:(c + 1) * 128, col0:col0 + n], reads=[dxres[ti]], writes=[dxc[i]])
                return i

            def norm_tile(ti):
                col0, n, seq = tiles[ti]
                hT = hTs[ti % 2]; dhT = dhTs[ti % 2]
                ps = psum[7]; dp = dps[7]
                for c in range(NCH):
                    i = ldx(ti, c); s_ = c % 2
                    k.op(k.act, lambda i=i, s_=s_: nc.scalar.activation(out=sq[s_][:, :n], in_=xc[i][:, :n], func=AF.Square),
                         reads=[dxc[i]], writes=[dsq[s_]])
                    mm(ps[:, :n], ones_bf[:], sq[s_][:, :n], c == 0, c == NCH - 1, [d_const, dsq[s_]], [dp], last=True)
                k.op(k.act, lambda: nc.scalar.activation(out=rs[:, :n], in_=ps[:, :n], func=AF.Ln, scale=1.0 / D, bias=eps_t[:, 0:1]),
                     reads=[dp, d_const], writes=[drs])
                k.op(k.act, lambda: nc.scalar.activation(out=rs[:, :n], in_=rs[:, :n], func=AF.Exp, scale=-0.5), reads=[drs], writes=[drs])
                for c in range(NCH):
                    i = ldx(ti, c); s_ = c % 2
                    k.op(k.dve, lambda i=i, s_=s_: nc.vector.tensor_tensor(out=tmp[s_][:, :n], in0=xc[i][:, :n], in1=rs[:, :n], op=ALU.mult),
                         reads=[dxc[i], drs], writes=[dtmp[s_]])
                    k.op(k.pool, lambda c=c, s_=s_: nc.gpsimd.tensor_scalar(out=hT[:, c, :n], in0=tmp[s_][:, :n], scalar1=affn[:, l, c, seq:seq + 1],
                                                                     scalar2=mod[:, l, 24 + c, seq:seq + 1], op0=ALU.mult, op1=ALU.add),
                         reads=[dtmp[s_], d_mod], writes=[dhT])

            norm_tile(0)
            for ti, (col0, n, seq) in enumerate(tiles):
                hT = hTs[ti % 2]; dhT = dhTs[ti % 2]
                if ti + 1 < len(tiles):
                    norm_tile(ti + 1)
                for j in range(HCH):
                    pg = psum[(2 * j) % 6]; dg = dps[(2 * j) % 6]
                    pu = psum[(2 * j + 1) % 6]; du = dps[(2 * j + 1) % 6]
                    mmk(pg, dg, win, j * 128, 128, lambda kk: hT[:, kk, :n], NCH, n, [dwin, dhT])
                    mmk(pu, du, win, HID + j * 128, 128, lambda kk: hT[:, kk, :n], NCH, n, [dwin, dhT])
                    s = j % 2
                    k.op(k.act, lambda s=s, pg=pg: nc.scalar.activation(out=sg[s][:, :n], in_=pg[:, :n], func=AF.Silu),
                         reads=[dg], writes=[dsg[s]])
                    k.op(k.dve, lambda s=s, pu=pu, j=j: nc.vector.tensor_tensor(out=actT[:, j, :n], in0=pu[:, :n], in1=sg[s][:, :n], op=ALU.mult),
                         reads=[du, dsg[s]], writes=[dact])
                wdeps = []
                for m in range(NCH):
                    pi = 6 + (m % 2); ps = psum[pi]; dp = dps[pi]
                    for j in range(HCH):
                        mm(ps[:, :n], wout[:, j, m * 128:(m + 1) * 128], actT[:, j, :n], j == 0, j == HCH - 1,
                           [dwout, dact], [dp], last=(j == HCH - 1))
                    i = ldx(ti, m)
                    k.op(k.dve, lambda m=m, ps=ps, i=i: nc.vector.scalar_tensor_tensor(
                        out=xc[i][:, :n], in0=ps[:, :n], scalar=mod[:, l, 40 + m, seq:seq + 1], in1=xc[i][:, :n],
                        op0=ALU.mult, op1=ALU.add), reads=[dp, d_mod, dxc[i]], writes=[dxc[i]])
                    wd = Dep(); wdeps.append(wd)
                    k.dma(STQ[0], xres[m * 128:(m + 1) * 128, col0:col0 + n], xc[i][:, :n], reads=[dxc[i], dxres[ti]], writes=[wd])
                merged = Dep()
                for wd in wdeps:
                    if wd.w is not None:
                        merged.r[wd.w[0]] = (wd.w[1], wd.w[2])
                dxres_w[ti] = merged
            k.barrier()
        cur[0] = prev

    dxres_w = {}

    def phase_final():
        with ExitStack() as es:
            cur[0] = es
            nbs = [NormBufs(), NormBufs()]
            for ti, (col0, n, seq) in enumerate(tiles):
                nb = nbs[ti % 2]
                k.dma(k.sp, nb.xt[:, :, :n], xres[:, col0:col0 + n].rearrange("(c p) t -> p c t", p=128), reads=[dxres[ti]], writes=[nb.dxt])
                rstd(nb, lambda c: nb.xt[:, c, :n], nb.dxt, NCH, n)
                for c in range(NCH):
                    k.op(k.dve, lambda c=c: nc.vector.scalar_tensor_tensor(
                        out=nb.xt[:, c, :n], in0=nb.xt[:, c, :n], scalar=gfin[:, c:c + 1], in1=nb.rs[:, :n],
                        op0=ALU.mult, op1=ALU.mult), reads=[nb.dxt, nb.drs, d_mod], writes=[nb.dxt])
                k.dma(STQ[0], O["yT"][:, col0:col0 + n].rearrange("(c p) t -> p c t", p=128), nb.xt[:, :, :n], reads=[nb.dxt])
            k.barrier()
        cur[0] = None

    tri_bf = sb("tri_bf", [128, 128], BF16)
    anti_bf = sb("anti_bf", [128, 128], BF16)
    ident_bf = sb("ident_bf", [128, 128], BF16)

    def load_const_bf(dst, src_ap, shape):
        with ExitStack() as es:
            cur[0] = es
            t = sb("cst", shape, F32); dt_ = Dep()
            k.dma(k.sp, t[:], src_ap, writes=[dt_])
            k.op(k.dve, lambda: nc.vector.tensor_copy(out=dst, in_=t[:]), reads=[dt_], writes=[d_const])
            k.barrier()
        cur[0] = None

    load_const_bf(tri_bf[:], I["tri"], [128, 128])
    load_const_bf(anti_bf[:], I["antiI"], [128, 128])
    load_const_bf(ident_bf[:], I["ident"], [128, 128])

    def cbase(b):
        return TT + b * P

    def ingest_cache(kc_ap, vc_ap, Lc, F, krow0=0):
        nf = (F + 127) // 128
        kin = [sb("kin", [128, F], F32) for _ in range(2)]; dkin = [Dep(), Dep()]
        kto = [sb("kto", [128, nf, 128], BF16) for _ in range(2)]; dkto = [Dep(), Dep()]
        vin = [sb("vin", [128, F], F32) for _ in range(2)]; dvin = [Dep(), Dep()]
        vbo = [sb("vbo", [128, F], BF16) for _ in range(2)]; dvbo = [Dep(), Dep()]
        it = 0
        for b in range(2):
            for t0 in range(0, Lc, 128):
                nt = min(128, Lc - t0)
                s = it % 2; it += 1
                k.dma(k.sp, kin[s][:nt, :], kc_ap[b, t0:t0 + nt, :], writes=[dkin[s]])
                for c in range(nf):
                    fw = min(128, F - c * 128)
                    pi = c % 4; ps = psum[pi]; dp = dps[pi]
                    mm(ps[:fw, :nt], kin[s][:nt, c * 128:c * 128 + fw], ident[:nt, :nt], True, True,
                       [dkin[s], d_const], [dp])
                    k.op(k.act, lambda c=c, ps=ps, fw=fw: nc.scalar.activation(out=kto[s][:fw, c, :nt], in_=ps[:fw, :nt], func=AF.Copy),
                         reads=[dp], writes=[dkto[s]])
                if F >= 128:
                    k.dma(STQ[0], kT_s[krow0:krow0 + F, cbase(b) + t0:cbase(b) + t0 + nt].rearrange("(c p) t -> p c t", p=128),
                          kto[s][:, :, :nt], reads=[dkto[s]], writes=[dkk])
                else:
                    k.dma(STQ[0], kT_s[krow0:krow0 + F, cbase(b) + t0:cbase(b) + t0 + nt], kto[s][:F, 0, :nt],
                          reads=[dkto[s]], writes=[dkk])
                if vc_ap is not None:
                    k.dma(k.sp, vin[s][:nt, :], vc_ap[b, t0:t0 + nt, :], writes=[dvin[s]])
                    cast(vbo[s][:nt, :], vin[s][:nt, :], [dvin[s]], [dvbo[s]])
                    k.dma(STQ[0], v_s[cbase(b) + t0:cbase(b) + t0 + nt, 0:F], vbo[s][:nt, :], reads=[dvbo[s]], writes=[dv])

    def phase_qkv(l, mix, first_layer):
        wq = {1: "w_b_qkv", 2: "w_c_qkv", 3: "w_d_qkv"}[mix]
        NQ = 1024
        NKF = 256 if mix == 2 else 1024
        NW = NQ + 2 * NKF
        nkc = NKF // 128
        Lc = LC[mix]
        with ExitStack() as es:
            cur[0] = es
            w = sb("wqkv", [128, NCH, NW], BF16); dw = Dep()
            with ExitStack() as es2:
                cur[0] = es2
                stg, dstg = mkstg(2)
                load_w(stg, dstg, w, dw, I[wq], NCH, NW)
                kc = {1: "cache_b_k", 2: "cache_c_k", 3: "cache_d_k"}[mix]
                vc = {1: "cache_b_v", 2: "cache_c_v", 3: "cache_d_v"}[mix]
                ingest_cache(I[kc], I[vc], Lc, NKF)
                k.barrier()
            cur[0] = es
            if mix == 2:
                wsw = sb("wsw", [128, NCH, NQ + NKF], BF16)
                wv = w[:, :, 0:NQ + NKF].rearrange("p k (h d) -> p k h d", d=64)
                sv = wsw[:, :, :].rearrange("p k (h d) -> p k h d", d=64)
                for kk in range(NCH):
                    k.op(k.pool, lambda kk=kk: nc.gpsimd.tensor_copy(out=sv[:, kk, :, 16:64], in_=wv[:, kk, :, 16:64]), reads=[dw], writes=[dw])
                    k.op(k.pool, lambda kk=kk: nc.gpsimd.tensor_copy(out=sv[:, kk, :, 0:8], in_=wv[:, kk, :, 8:16]), reads=[dw], writes=[dw])
                    k.op(k.pool, lambda kk=kk: nc.gpsimd.tensor_copy(out=sv[:, kk, :, 8:16], in_=wv[:, kk, :, 0:8]), reads=[dw], writes=[dw])
                bc = sb("bc", [128, 12], F32); bcs = sb("bcs", [128, 12], F32)
                brow = sb("brow", [128, 256], F32)
                rc = sb("rc", [128, TT], F32); rsn = sb("rsn", [128, TT], F32)
                k.dma(k.sp, bc[:], I["bcT"], writes=[dw])
                k.dma(k.sp, bcs[:], I["bcswT"], writes=[dw])
                k.dma(k.sp, brow[:], I["bc_row"][0:1, 1280:1536].partition_broadcast(128), writes=[dw])
                k.dma(k.sp, rc[:], I["ropeC_cos"], writes=[dw])
                k.dma(k.sp, rsn[:], I["ropeC_sin"], writes=[dw])
            nbs = [NormBufs(), NormBufs()]
            qo = sb("qo", [128, NCH, 512], BF16); dqo = Dep()
            ko = sb("ko", [128, nkc, 512], BF16); dko = Dep()
            kf = sb("kf", [128, nkc, 512], F32); dkf = Dep()
            t1 = [sb("t1", [128, 512], F32) for _ in range(2)]; dt1 = [Dep(), Dep()]
            t2 = [sb("t2", [128, 512], F32) for _ in range(2)]; dt2 = [Dep(), Dep()]
            ktm = [sb("ktm", [128, NKF], F32) for _ in range(2)]; dktm = [Dep(), Dep()]
            vtm = [sb("vtm", [128, NKF], F32) for _ in range(2)]; dvtm = [Dep(), Dep()]
            vtb = [sb("vtb", [128, NKF], BF16) for _ in range(2)]; dvtb = [Dep(), Dep()]
            keep = {1: S, 2: min(128, S), 3: min(512, S)}[mix]
            ko_name = {1: "b_k", 2: "c_k", 3: "d_k"}[mix]
            vo_name = {1: "b_v", 2: "c_v", 3: "d_v"}[mix]
            it = 0
            pr = 0
            load_x_and_norm(nbs[0], 0, l, first_layer, amix, 0)
            for ti, (col0, n, seq) in enumerate(tiles):
                nb = nbs[ti % 2]
                if ti + 1 < len(tiles):
                    load_x_and_norm(nbs[(ti + 1) % 2], ti + 1, l, first_layer, amix, 0)
                hrhs = lambda kk, nb=nb, n=n: nb.hT[:, kk, :n]
                for m in range(NCH + nkc):
                    isq = m < NCH
                    wcol = m * 128 if isq else NQ + (m - NCH) * 128
                    mi = m if isq else m - NCH
                    pi = pr % 6; pr += 1; ps = psum[pi]; dp = dps[pi]
                    mmk(ps, dp, w, wcol, 128, hrhs, NCH, n, [dw, nb.dhT])
                    dst_bf = qo[:, mi, :n] if isq else ko[:, mi, :n]
                    ddst = dqo if isq else dko
                    if mix != 2:
                        k.op(k.act, lambda ps=ps, dst_bf=dst_bf: nc.scalar.activation(out=dst_bf, in_=ps[:, :n], func=AF.Copy),
                             reads=[dp], writes=[ddst])
                        if not isq:
                            k.op(k.act, lambda ps=ps, mi=mi: nc.scalar.activation(out=kf[:, mi, :n], in_=ps[:, :n], func=AF.Copy),
                                 reads=[dp], writes=[dkf])
                    else:
                        pi2 = pr % 6; pr += 1; ps2 = psum[pi2]; dp2 = dps[pi2]
                        wc2 = m * 128
                        mmk(ps2, dp2, wsw, wc2, 128, hrhs, NCH, n, [dw, nb.dhT])
                        bcol = m
                        s_ = it % 2; it += 1
                        k.op(k.dve, lambda ps=ps, s_=s_, bcol=bcol: nc.vector.scalar_tensor_tensor(
                            out=t1[s_][:, :n], in0=ps[:, :n], scalar=bc[:, bcol:bcol + 1], in1=rc[:, col0:col0 + n],
                            op0=ALU.add, op1=ALU.mult), reads=[dp, dw], writes=[dt1[s_]])
                        k.op(k.dve, lambda ps2=ps2, s_=s_, bcol=bcol: nc.vector.scalar_tensor_tensor(
                            out=t2[s_][:, :n], in0=ps2[:, :n], scalar=bcs[:, bcol:bcol + 1], in1=rsn[:, col0:col0 + n],
                            op0=ALU.add, op1=ALU.mult), reads=[dp2, dw], writes=[dt2[s_]])
                        if isq:
                            k.op(k.pool, lambda s_=s_, dst_bf=dst_bf: nc.gpsimd.tensor_tensor(out=dst_bf, in0=t1[s_][:, :n], in1=t2[s_][:, :n], op=ALU.add),
                                 reads=[dt1[s_], dt2[s_]], writes=[ddst])
                        else:
                            k.op(k.pool, lambda s_=s_, mi=mi: nc.gpsimd.tensor_tensor(out=kf[:, mi, :n], in0=t1[s_][:, :n], in1=t2[s_][:, :n], op=ALU.add),
                                 reads=[dt1[s_], dt2[s_]], writes=[dkf])
                            k.op(k.pool, lambda mi=mi, dst_bf=dst_bf: nc.gpsimd.tensor_copy(out=dst_bf, in_=kf[:, mi, :n]),
                                 reads=[dkf], writes=[ddst])
                k.dma(STQ[0], qT_s[0:NQ, col0:col0 + n].rearrange("(c p) t -> p c t", p=128), qo[:, :, :n], reads=[dqo], writes=[dq])
                k.dma(STQ[0], kT_s[0:NKF, col0:col0 + n].rearrange("(c p) t -> p c t", p=128), ko[:, :, :n], reads=[dko], writes=[dkk])
                for s0 in range(0, n, 128):
                    nt = min(128, n - s0)
                    tok0 = col0 + s0
                    b_ = it % 2; it += 1
                    if seq == 0:
                        lo = S - keep
                        want = tok0 >= lo
                    else:
                        want = True
                    if want:
                        for c in range(nkc):
                            pi = pr % 6; pr += 1; ps = psum[pi]; dp = dps[pi]
                            mm(ps[:nt, :128], kf[:, c, s0:s0 + nt], ident[:, :], True, True, [dkf, d_const], [dp])
                            k.op(k.act, lambda ps=ps, c=c, b_=b_: nc.scalar.activation(out=ktm[b_][:nt, c * 128:(c + 1) * 128], in_=ps[:nt, :128], func=AF.Copy),
                                 reads=[dp], writes=[dktm[b_]])
                    for fb in range(0, NKF, 512):
                        fw = min(512, NKF - fb)
                        pi = pr % 6; pr += 1; ps = psum[pi]; dp = dps[pi]
                        for kk in range(NCH):
                            mm(ps[:nt, :fw], nb.hT[:, kk, s0:s0 + nt], w[:, kk, NQ + NKF + fb:NQ + NKF + fb + fw],
                               kk == 0, kk == NCH - 1, [dw, nb.dhT], [dp], last=(kk == NCH - 1))
                        if mix == 2:
                            k.op(k.dve, lambda ps=ps, b_=b_, fb=fb, fw=fw: nc.vector.tensor_tensor(out=vtm[b_][:nt, fb:fb + fw], in0=ps[:nt, :fw], in1=brow[:nt, fb:fb + fw], op=ALU.add),
                                 reads=[dp, dw], writes=[dvtm[b_]])
                        else:
                            k.op(k.act, lambda ps=ps, b_=b_, fb=fb, fw=fw: nc.scalar.activation(out=vtm[b_][:nt, fb:fb + fw], in_=ps[:nt, :fw], func=AF.Copy),
                                 reads=[dp], writes=[dvtm[b_]])
                    k.op(k.dve, lambda b_=b_: nc.vector.tensor_copy(out=vtb[b_][:nt, :], in_=vtm[b_][:nt, :]), reads=[dvtm[b_]], writes=[dvtb[b_]])
                    k.dma(STQ[0], v_s[tok0:tok0 + nt, 0:NKF], vtb[b_][:nt, :], reads=[dvtb[b_]], writes=[dv])
                    if want:
                        if mix == 1:
                            ka = O["b_k"][tok0:tok0 + nt, :]; va = O["b_v"][tok0:tok0 + nt, :]
                        elif seq == 0:
                            ka = O[ko_name + "_p"][tok0 - lo:tok0 - lo + nt, :]; va = O[vo_name + "_p"][tok0 - lo:tok0 - lo + nt, :]
                        else:
                            ka = O[ko_name + "_s"][seq - 1, Lc - T_S:Lc, :]; va = O[vo_name + "_s"][seq - 1, Lc - T_S:Lc, :]
                        k.dma(STQ[0], ka, ktm[b_][:nt, :], reads=[dktm[b_]])
                        k.dma(STQ[0], va, vtm[b_][:nt, :], reads=[dvtm[b_]])
            if mix != 1:
                for b in range(2):
                    k.dma(k.sp, O[ko_name + "_s"][b, 0:Lc - T_S, :], I[kc][b, T_S:Lc, :])
                    k.dma(k.sp, O[vo_name + "_s"][b, 0:Lc - T_S, :], I[vc][b, T_S:Lc, :])
            k.barrier()
        cur[0] = None

    class AttnBufs:
        def __init__(self, Lc, vw, nh=2):
            self.Lc = Lc
            self.ncs = (Lc + 127) // 128
            self.nslots = S // 128 + 2 * (self.ncs + 1)
            self.Kf = sb("Kf", [128, TT + 2 * Lc], BF16); self.dK = Dep()
            self.Qf = sb("Qf", [128, TT], BF16); self.dQ = Dep()
            self.Qz = [self.Qf, sb("Qf1", [128, TT], BF16)] if nh == 2 else [self.Qf]
            self.Va = sb("Va", [128, self.nslots, nh, vw], BF16); self.dV = Dep()

        def zero_q(self):
            k.op(k.pool, lambda: nc.gpsimd.memset(self.Qz[0][64:128, :], 0.0), writes=[self.dQ])
            k.op(k.pool, lambda: nc.gpsimd.memset(self.Qz[1][0:64, :], 0.0), writes=[self.dQ])

        def load_q_pair(self, u):
            k.dma(k.sp, self.Qz[0][0:64, :], qT_s[u * 128:u * 128 + 64, 0:TT], reads=[dq], writes=[self.dQ])
            k.dma(k.sp, self.Qz[1][64:128, :], qT_s[u * 128 + 64:(u + 1) * 128, 0:TT], reads=[dq], writes=[self.dQ])

        def kcol(self, b, j=0):
            return TT + b * self.Lc + j

        def slot_cache(self, b, t):
            return S // 128 + b * (self.ncs + 1) + t

        def slot_new(self, b):
            return S // 128 + b * (self.ncs + 1) + self.ncs

    def load_K(ab, rows_dst, krow0, nrows):
        Lc = ab.Lc
        k.dma(k.sp, ab.Kf[rows_dst:rows_dst + nrows, 0:TT], kT_s[krow0:krow0 + nrows, 0:TT], reads=[dkk], writes=[ab.dK])
        for b in range(2):
            k.dma(k.sp, ab.Kf[rows_dst:rows_dst + nrows, ab.kcol(b):ab.kcol(b) + Lc],
                  kT_s[krow0:krow0 + nrows, cbase(b):cbase(b) + Lc], reads=[dkk], writes=[ab.dK])

    def load_V(ab, hsel, vcol0):
        Lc = ab.Lc
        for t0 in range(0, S // 128, 8):
            t1_ = min(S // 128, t0 + 8)
            k.dma(k.sp, ab.Va[:, t0:t1_, hsel, 0:64], v_s[t0 * 128:t1_ * 128, vcol0:vcol0 + 64].rearrange("(t p) d -> p t d", p=128),
                  reads=[dv], writes=[ab.dV])
        for b in range(2):
            for t0 in range(0, ab.ncs, 8):
                t1_ = min(ab.ncs, t0 + 8)
                k.dma(k.sp, ab.Va[:, ab.slot_cache(b, t0):ab.slot_cache(b, t0) + (t1_ - t0), hsel, 0:64],
                      v_s[cbase(b) + t0 * 128:cbase(b) + t1_ * 128, vcol0:vcol0 + 64].rearrange("(t p) d -> p t d", p=128), reads=[dv], writes=[ab.dV])
            k.dma(k.sp, ab.Va[0:T_S, ab.slot_new(b), hsel, 0:64], v_s[S + b * T_S:S + (b + 1) * T_S, vcol0:vcol0 + 64],
                  reads=[dv], writes=[ab.dV])

    def load_masks(name, nm):
        mt = sb("mask", [128, nm, 512], BF16); dm = Dep()
        with ExitStack() as es2:
            old = cur[0]; cur[0] = es2
            st = sb("mstg", [128, 512], F32); dst_ = Dep()
            for i in range(nm):
                k.dma(k.sp, st[:], I[name][i], writes=[dst_])
                k.op(k.dve, lambda i=i: nc.vector.tensor_copy(out=mt[:, i, :], in_=st[:]), reads=[dst_], writes=[dm])
            k.barrier()
            cur[0] = old
        return mt, dm

    cnt = {"s": 0, "o": 0, "p": 0}

    def phase_attn_softmax(mix, lnext):
        Lc = LC[mix]
        kdim = 96 if mix == 0 else 64
        scale = float(kdim) ** -0.5
        OFFE = 511
        STQ[0] = k.sp
        with ExitStack() as es:
            cur[0] = es
            if mix == 2:
                mt, dm = None, Dep()
            else:
                mname, nm = {0: ("maskA", 4), 3: ("maskD", 8)}[mix]
                mt, dm = load_masks(mname, nm)
            abs_ = [AttnBufs(Lc, 128, 1 if mix == 0 else 2) for _ in range(2)]
            for ab in abs_:
                k.op(k.pool, lambda ab=ab: nc.gpsimd.memset(ab.Va[:, :, :, 64:128], 1.0), writes=[ab.dV])
                if mix == 3:
                    ab.zero_q()
            if mix == 2:
                with ExitStack() as es2:
                    cur[0] = es2
                    su = sb("stgU", [128, TT + 2 * Lc], F32); dsu = Dep()
                    sw = sb("stgW", [128, TT], F32); dsw = Dep()
                    k.dma(k.sp, su[64:128, :], I["chunkU"], writes=[dsu])
                    k.dma(k.sp, sw[64:128, :], I["chunkW"], writes=[dsw])
                    for ab in abs_:
                        k.op(k.dve, lambda ab=ab: nc.vector.tensor_copy(out=ab.Kf[64:128, :], in_=su[64:128, :]), reads=[dsu], writes=[ab.dK])
                        for qz in ab.Qz:
                            k.op(k.dve, lambda qz=qz: nc.vector.tensor_copy(out=qz[64:128, :], in_=sw[64:128, :]), reads=[dsw], writes=[ab.dQ])
                    k.barrier()
                cur[0] = es
            pt = [sb("pt", [128, 512], BF16) for _ in range(4)]; dpt = [Dep() for _ in range(4)]
            if mix == 2:
                esink = sb("esink", [128, 16], F32); des = Dep()
                k.dma(k.sp, esink[:], I["sink_c"][0:1, :].partition_broadcast(128), writes=[des])
                k.op(k.act, lambda: nc.scalar.activation(out=esink[:], in_=esink[:], func=AF.Exp), reads=[des], writes=[des])
            if mix == 3:
                et = sb("ext", [16, 1536], F32); det = Dep()
                k.op(k.pool, lambda: nc.gpsimd.memset(et[:], 0.0), writes=[det])
                k.dma(k.sp, et[:, OFFE - 128:OFFE + 129], I["rel_bias_d"], writes=[det])
                k.op(k.dve, lambda: nc.vector.tensor_scalar(out=et[:, 0:OFFE - 128], in0=et[:, 0:OFFE - 128], scalar1=et[:, OFFE - 128:OFFE - 127],
                                                          scalar2=None, op0=ALU.add), reads=[det], writes=[det])
                k.op(k.dve, lambda: nc.vector.tensor_scalar(out=et[:, OFFE + 129:1536], in0=et[:, OFFE + 129:1536], scalar1=et[:, OFFE + 128:OFFE + 129],
                                                          scalar2=None, op0=ALU.add), reads=[det], writes=[det])
                dext = Dep()
                k.dma(STQ[0], ext_s, et[:], reads=[det], writes=[dext])
                Hst = [sb("Hst", [128, 512], F32) for _ in range(4)]; dHst = [Dep() for _ in range(4)]
                Hb = [sb("Hb", [128, 8 + Lc // 128 + 1, 512], BF16) for _ in range(2)]; dHb = [Dep(), Dep()]
                anti32 = sb("anti32", [32, 32], BF16)
                a32s = sb("a32s", [32, 32], F32); da32 = Dep()
                k.dma(k.sp, a32s[:], I["antiI"][96:128, 0:32], writes=[da32])
                k.op(k.dve, lambda: nc.vector.tensor_copy(out=anti32[:], in_=a32s[:]), reads=[da32], writes=[d_const])
            units = 16 if mix == 0 else 8
            hst_i = [0]

            def load_unit(u, ab):
                if mix == 0:
                    load_K(ab, 0, u * 64, 64)
                    load_K(ab, 64, D, 32)
                    k.dma(k.sp, ab.Qf[0:96, :], qT_s[u * 96:(u + 1) * 96, 0:TT], reads=[dq], writes=[ab.dQ])
                    load_V(ab, 0, u * 64)
                else:
                    if mix == 2:
                        g = u // 2
                        load_K(ab, 0, g * 64, 64)
                        load_V(ab, 0, g * 64)
                        k.dma(k.sp, ab.Qz[0][0:64, :], qT_s[u * 128:u * 128 + 64, 0:TT], reads=[dq], writes=[ab.dQ])
                        k.dma(k.sp, ab.Qz[1][0:64, :], qT_s[u * 128 + 64:(u + 1) * 128, 0:TT], reads=[dq], writes=[ab.dQ])
                    else:
                        load_K(ab, 0, u * 128, 128)
                        load_V(ab, 0, u * 128)
                        load_V(ab, 1, u * 128 + 64)
                        ab.load_q_pair(u)

            def load_H(h, hb_i):
                specs = [(8 + 0, 0, 0, 0)]
                specs = []
                for r in range(-4, 4):
                    specs.append((r + 4, OFFE - 127 - 128 * r, 128, 512))
                for kt in range(Lc // 128):
                    specs.append((8 + kt, OFFE + Lc - 128 * kt - 127, 128, T_S))
                specs.append((8 + Lc // 128, OFFE - 31, 32, T_S))
                for (slot, base, nr, ncol) in specs:
                    s_ = hst_i[0] % 4; hst_i[0] += 1
                    src = bass.AP(ext_s.tensor, h * 1536 + base, [[1, nr], [1, ncol]])
                    k.dma(k.sp, Hst[s_][:nr, :ncol], src, reads=[dext], writes=[dHst[s_]])
                    if slot < 8:
                        k.op(k.dve, lambda s_=s_, slot=slot, nr=nr, ncol=ncol: nc.vector.scalar_tensor_tensor(
                            out=Hb[hb_i][:nr, slot, :ncol], in0=Hst[s_][:nr, :ncol], scalar=8.0, in1=mt[:nr, slot, :ncol], op0=ALU.mult, op1=ALU.add),
                            reads=[dHst[s_], dm], writes=[dHb[hb_i]])
                    else:
                        k.op(k.dve, lambda s_=s_, slot=slot, nr=nr, ncol=ncol: nc.vector.tensor_scalar(
                            out=Hb[hb_i][:nr, slot, :ncol], in0=Hst[s_][:nr, :ncol], scalar1=8.0, scalar2=None, op0=ALU.mult),
                            reads=[dHst[s_]], writes=[dHb[hb_i]])

            obanks = {0: [4, 5], 2: [4, 5, 7], 3: [4, 5, 6, 7]}[mix]
            NO = len(obanks)
            rden = [sb("rden", [128, 512], F32) for _ in range(NO)]; drd = [Dep() for _ in range(NO)]
            ot = [sb("ot", [128, 512], BF16) for _ in range(NO)]; dot_ = [Dep() for _ in range(NO)]
            pend = []
            SK = 2

            def push(fA, fP):
                fA()
                pend.append(fP)
                if len(pend) > SK:
                    pend.pop(0)()

            pstore = []

            def flush():
                while pend:
                    pend.pop(0)()
                while pstore:
                    pstore.pop(0)()

            def run_q(ab, pbase, hsel, h, qcol0, nq, blocks, hb_i):
                oc = cnt["o"]; cnt["o"] += 1
                ri = oc % NO
                psO = psum[obanks[ri]]; dO = dps[obanks[ri]]
                nb_ = len(blocks)

                def finalize():
                    if mix == 2:
                        k.op(k.dve, lambda: nc.vector.tensor_scalar(out=rden[ri][64:128, :nq], in0=psO[64:128, :nq], scalar1=esink[64:128, h:h + 1],
                                                                  scalar2=None, op0=ALU.add), reads=[dO, des], writes=[drd[ri]])
                        k.op(k.dve, lambda: nc.vector.reciprocal(out=rden[ri][64:128, :nq], in_=rden[ri][64:128, :nq]), reads=[drd[ri]], writes=[drd[ri]])
                    else:
                        k.op(k.dve, lambda: nc.vector.reciprocal(out=rden[ri][64:128, :nq], in_=psO[64:128, :nq]), reads=[dO], writes=[drd[ri]])
                    k.op(k.dve, lambda: nc.vector.tensor_tensor(out=ot[ri][0:64, :nq], in0=psO[0:64, :nq], in1=rden[ri][64:128, :nq], op=ALU.mult),
                         reads=[dO, drd[ri]], writes=[dot_[ri]])
                    while pstore:
                        pstore.pop(0)()
                    pstore.append(lambda: k.dma(STQ[0], attnT[h * 64:(h + 1) * 64, qcol0:qcol0 + nq], ot[ri][0:64, :nq], reads=[dot_[ri]], writes=[dattn]))

                for j, (kcol, nk, slot, mask, hslot) in enumerate(blocks):
                    st = {}

                    def fA(kcol=kcol, nk=nk, mask=mask, hslot=hslot, st=st):
                        si = cnt["s"] % 4; cnt["s"] += 1
                        psS = psum[si]; dS = dps[si]
                        more = (hslot is not None) or (mask is not None)
                        if mix == 0:
                            mm(psS[:nk, :nq], ab.Kf[0:96, kcol:kcol + nk], ab.Qf[0:96, qcol0:qcol0 + nq],
                               True, not more, [ab.dK, ab.dQ], [dS], last=(not more))
                        else:
                            mm(psS[:nk, :nq], ab.Kf[:, kcol:kcol + nk], ab.Qz[pbase // 64][:, qcol0:qcol0 + nq],
                               True, not more, [ab.dK, ab.dQ], [dS], last=(not more))
                        if hslot is not None:
                            al = anti_bf[:, :] if nk == 128 else anti32[:, :]
                            mm(psS[:nk, :nq], al, Hb[hb_i][:nk, hslot, :nq], False, True, [d_const, dHb[hb_i]], [dS])
                        elif mask is not None:
                            mm(psS[:nk, :nq], ident_bf[:nk, :nk], mt[:nk, mask, :nq], False, True, [d_const, dm], [dS])
                        pi = cnt["p"] % 4; cnt["p"] += 1
                        k.op(k.act, lambda: nc.scalar.activation(out=pt[pi][:nk, :nq], in_=psS[:nk, :nq], func=AF.Exp, scale=scale),
                             reads=[dS], writes=[dpt[pi]])
                        st["pi"] = pi

                    def fP(j=j, nk=nk, slot=slot, st=st):
                        pi = st["pi"]
                        mm(psO[:, :nq], ab.Va[:nk, slot, hsel, :], pt[pi][:nk, :nq], j == 0, j == nb_ - 1, [ab.dV, dpt[pi]], [dO],
                           last=(j == nb_ - 1))
                        if j == nb_ - 1:
                            finalize()

                    push(fA, fP)

            nper = 2
            bg = AdaBg({0: [1, 2], 2: [3]}.get(mix, []), 4, [6, 7])
            if mix == 3:
                load_H(0, 0)
            load_unit(0, abs_[0])
            bg.load(nper)
            for u in range(units):
                ab = abs_[u % 2]
                if u + 1 < units:
                    flush()
                    load_unit(u + 1, abs_[(u + 1) % 2])
                bg.step(nper)
                heads = [(0, 0, u)] if mix == 0 else [(0, 0, 2 * u), (64, 0 if mix == 2 else 1, 2 * u + 1)]
                for (pbase, hsel, h) in heads:
                    hb_i = h % 2
                    if mix == 3 and h + 1 < 16:
                        load_H(h + 1, (h + 1) % 2)
                    for qi in range(NPT):
                        blocks = []
                        if mix == 0:
                            for kt in range(0, 4 * qi + 4):
                                blocks.append((kt * 128, 128, kt, (kt - 4 * qi) if kt >= 4 * qi else None, None))
                        elif mix == 2:
                            for kt in range(max(0, 4 * qi - 1), 4 * qi + 4):
                                blocks.append((kt * 128, 128, kt, None, None))
                        else:
                            for kt in range(max(0, 4 * qi - 4), 4 * qi + 4):
                                blocks.append((kt * 128, 128, kt, None, kt - 4 * qi + 4))
                        run_q(ab, pbase, hsel, h, qi * 512, 512, blocks, hb_i)
                    for b in range(2):
                        blocks = []
                        for t in range(ab.ncs):
                            blocks.append((ab.kcol(b, t * 128), min(128, Lc - t * 128), ab.slot_cache(b, t), None, (8 + t) if mix == 3 else None))
                        blocks.append((S + b * T_S, T_S, ab.slot_new(b), None, (8 + Lc // 128) if mix == 3 else None))
                        run_q(ab, pbase, hsel, h, S + b * T_S, T_S, blocks, hb_i)
            flush()
            bg.finish()
            k.barrier()
        cur[0] = None
        STQ[0] = k.act

    def phase_proj_a(l, first_layer):
        with ExitStack() as es:
            cur[0] = es
            wdn = sb("wdn", [128, NCH, 672], BF16); dw = Dep()
            wkr = sb("wkr", [128, NCH, 96], BF16)
            wkrs = sb("wkrs", [128, NCH, 96], BF16)
            wuq = sb("wuq", [128, 3, 1536], BF16)
            wuqs = sb("wuqs", [128, 3, 1536], BF16)
            wuk = sb("wuk", [128, 2, D], BF16)
            wuv = sb("wuv", [128, 2, D], BF16)
            gq = sb("gq", [128, 3], F32); gkv = sb("gkv", [128, 2], F32)
            rc = sb("rc", [128, TT], F32); rsn = sb("rsn", [128, TT], F32)
            k.dma(k.sp, gq[:], I["gaqT"], writes=[d_mod])
            k.dma(k.sp, gkv[:], I["gakvT"], writes=[d_mod])
            k.dma(k.sp, rc[:], I["ropeA_cos"], writes=[dw])
            k.dma(k.sp, rsn[:], I["ropeA_sin"], writes=[dw])
            nb = NormBufs()
            cknb_c = [sb("cknc", [128, 2, 128], BF16) for _ in range(2)]; dcknc = [Dep(), Dep()]
            with ExitStack() as es2:
                cur[0] = es2
                stg, dstg = mkstg(2)
                load_w(stg, dstg, wdn, dw, I["w_a_down"], NCH, 672)
                load_w(stg, dstg, wuq, dw, I["w_a_uq"], 3, 1536)
                load_w(stg, dstg, wuk, dw, I["w_a_uk"], 2, D)
                load_w(stg, dstg, wuv, dw, I["w_a_uv"], 2, D)
                k.op(k.pool, lambda: nc.gpsimd.memset(wkr[:], 0.0), writes=[dw])
                k.op(k.pool, lambda: nc.gpsimd.memset(wkrs[:], 0.0), writes=[dw])
                k.op(k.pool, lambda: nc.gpsimd.tensor_copy(out=wkr[:, :, 64:96], in_=wdn[:, :, 640:672]), reads=[dw], writes=[dw])
                k.op(k.pool, lambda: nc.gpsimd.tensor_copy(out=wkrs[:, :, 64:80], in_=wdn[:, :, 656:672]), reads=[dw], writes=[dw])
                k.op(k.pool, lambda: nc.gpsimd.tensor_copy(out=wkrs[:, :, 80:96], in_=wdn[:, :, 640:656]), reads=[dw], writes=[dw])
                qv = wuq[:, :, :].rearrange("p k (h d) -> p k h d", d=96)
                sv = wuqs[:, :, :].rearrange("p k (h d) -> p k h d", d=96)
                for kk in range(3):
                    k.op(k.pool, lambda kk=kk: nc.gpsimd.tensor_copy(out=sv[:, kk, :, 0:64], in_=qv[:, kk, :, 0:64]), reads=[dw], writes=[dw])
                    k.op(k.pool, lambda kk=kk: nc.gpsimd.tensor_copy(out=sv[:, kk, :, 64:80], in_=qv[:, kk, :, 80:96]), reads=[dw], writes=[dw])
                    k.op(k.pool, lambda kk=kk: nc.gpsimd.tensor_copy(out=sv[:, kk, :, 80:96], in_=qv[:, kk, :, 64:80]), reads=[dw], writes=[dw])
                ingest_cache(I["cache_a_kr"], None, P, 32, krow0=D)
                cin = [sb("cin", [128, 256], F32) for _ in range(2)]; dcin = [Dep(), Dep()]
                kto = [sb("ktoa", [128, NCH, 128], BF16) for _ in range(2)]; dkto = [Dep(), Dep()]
                vbo = [sb("vboa", [128, D], BF16) for _ in range(2)]; dvbo = [Dep(), Dep()]
                it = 0
                for b in range(2):
                    for t0 in range(0, P, 128):
                        s = it % 2; it += 1
                        k.dma(k.sp, cin[s][:, :], I["cache_a_ckv"][b, t0:t0 + 128, :], writes=[dcin[s]])
                        for c in range(2):
                            ps = psum[c]; dp = dps[c]
                            mm(ps[:, :128], cin[s][:, c * 128:(c + 1) * 128], ident[:, :], True, True, [dcin[s], d_const], [dp])
                            k.op(k.act, lambda c=c, ps=ps, s=s: nc.scalar.activation(out=cknb_c[s][:, c, :], in_=ps[:, :128], func=AF.Copy),
                                 reads=[dp], writes=[dcknc[s]])
                        for m in range(NCH):
                            pi = 2 + m % 2; ps = psum[pi]; dp = dps[pi]
                            mmk(ps, dp, wuk, m * 128, 128, lambda kk: cknb_c[s][:, kk, :], 2, 128, [dw, dcknc[s]])
                            k.op(k.act, lambda m=m, ps=ps, s=s: nc.scalar.activation(out=kto[s][:, m, :], in_=ps[:, :128], func=AF.Copy),
                                 reads=[dp], writes=[dkto[s]])
                        k.dma(STQ[0], kT_s[0:D, cbase(b) + t0:cbase(b) + t0 + 128].rearrange("(c p) t -> p c t", p=128), kto[s][:, :, :],
                              reads=[dkto[s]], writes=[dkk])
                        for fb in range(2):
                            pi = 4 + fb; ps = psum[pi]; dp = dps[pi]
                            for c in range(2):
                                mm(ps[:, :512], cknb_c[s][:, c, :], wuv[:, c, fb * 512:(fb + 1) * 512], c == 0, c == 1, [dw, dcknc[s]], [dp], last=(c == 1))
                            k.op(k.dve, lambda fb=fb, ps=ps, s=s: nc.vector.tensor_copy(out=vbo[s][:, fb * 512:(fb + 1) * 512], in_=ps[:, :512]),
                                 reads=[dp], writes=[dvbo[s]])
                        k.dma(STQ[0], v_s[cbase(b) + t0:cbase(b) + t0 + 128, :], vbo[s][:, :], reads=[dvbo[s]], writes=[dv])
                k.barrier()
            cur[0] = es
            cqf = sb("cqf", [128, 3, 512], F32); dcqf = Dep()
            cqn = sb("cqn", [128, 3, 512], BF16); dcqn = Dep()
            ckf = sb("ckf", [128, 2, 512], F32); dckf = Dep()
            ckn32 = sb("ckn32", [128, 2, 512], F32); dckn32 = Dep()
            cknb = sb("cknb", [128, 2, 512], BF16); dcknb = Dep()
            t1 = [sb("t1", [128, 512], F32) for _ in range(2)]; dt1 = [Dep(), Dep()]
            t2 = [sb("t2", [128, 512], F32) for _ in range(2)]; dt2 = [Dep(), Dep()]
            krf = sb("krf", [128, 512], F32); dkrf = Dep()
            krb = sb("krb", [128, 512], BF16); dkrb = Dep()
            qall = sb("qall", [128, 16, 512], BF16); dqall = Dep()
            ko = sb("koa", [128, NCH, 512], BF16); dko = Dep()
            ctm = [sb("ctm", [128, 256], F32) for _ in range(2)]; dctm = [Dep(), Dep()]
            krtm = [sb("krtm", [128, 32], F32) for _ in range(2)]; dkrtm = [Dep(), Dep()]
            vtb = [sb("vtba", [128, D], BF16) for _ in range(2)]; dvtb = [Dep(), Dep()]
            it = 0
            pr = 0
            for ti, (col0, n, seq) in enumerate(tiles):
                load_x_and_norm(nb, ti, l, first_layer, amix, 0)
                hrhs = lambda kk: nb.hT[:, kk, :n]
                for j in range(3):
                    pi = pr % 6; pr += 1; ps = psum[pi]; dp = dps[pi]
                    mmk(ps, dp, wdn, j * 128, 128, hrhs, NCH, n, [dw, nb.dhT])
                    k.op(k.act, lambda j=j, ps=ps: nc.scalar.activation(out=cqf[:, j, :n], in_=ps[:, :n], func=AF.Copy), reads=[dp], writes=[dcqf])
                for j in range(2):
                    pi = pr % 6; pr += 1; ps = psum[pi]; dp = dps[pi]
                    mmk(ps, dp, wdn, 384 + j * 128, 128, hrhs, NCH, n, [dw, nb.dhT])
                    k.op(k.act, lambda j=j, ps=ps: nc.scalar.activation(out=ckf[:, j, :n], in_=ps[:, :n], func=AF.Copy), reads=[dp], writes=[dckf])
                pi = pr % 6; pr += 1; ps1 = psum[pi]; dp1 = dps[pi]
                mmk(ps1, dp1, wkr, 0, 96, hrhs, NCH, n, [dw, nb.dhT])
                pi = pr % 6; pr += 1; ps2 = psum[pi]; dp2 = dps[pi]
                mmk(ps2, dp2, wkrs, 0, 96, hrhs, NCH, n, [dw, nb.dhT])
                s_ = it % 2; it += 1
                k.op(k.dve, lambda: nc.vector.tensor_tensor(out=t1[s_][64:96, :n], in0=ps1[64:96, :n], in1=rc[64:96, col0:col0 + n], op=ALU.mult),
                     reads=[dp1, dw], writes=[dt1[s_]])
                k.op(k.dve, lambda: nc.vector.tensor_tensor(out=t2[s_][64:96, :n], in0=ps2[64:96, :n], in1=rsn[64:96, col0:col0 + n], op=ALU.mult),
                     reads=[dp2, dw], writes=[dt2[s_]])
                k.op(k.pool, lambda: nc.gpsimd.tensor_tensor(out=krf[64:96, :n], in0=t1[s_][64:96, :n], in1=t2[s_][64:96, :n], op=ALU.add),
                     reads=[dt1[s_], dt2[s_]], writes=[dkrf])
                k.op(k.pool, lambda: nc.gpsimd.tensor_copy(out=krb[64:96, :n], in_=krf[64:96, :n]), reads=[dkrf], writes=[dkrb])
                k.dma(STQ[0], kT_s[D:D + 32, col0:col0 + n], krb[64:96, :n], reads=[dkrb], writes=[dkk])
                rstd(nb, lambda c: cqf[:, c, :n], dcqf, 3, n)
                apply_norm(nb, lambda c: cqf[:, c, :n], dcqf, lambda c: cqn[:, c, :n], dcqn, 3, n, lambda c: gq[:, c:c + 1], None)
                rstd(nb, lambda c: ckf[:, c, :n], dckf, 2, n)
                apply_norm(nb, lambda c: ckf[:, c, :n], dckf, lambda c: ckn32[:, c, :n], dckn32, 2, n, lambda c: gkv[:, c:c + 1], None)
                k.op(k.pool, lambda: nc.gpsimd.tensor_copy(out=cknb[:, :, :n], in_=ckn32[:, :, :n]), reads=[dckn32], writes=[dcknb])
                for h in range(16):
                    pi = pr % 6; pr += 1; ps1 = psum[pi]; dp1 = dps[pi]
                    mmk(ps1, dp1, wuq, h * 96, 96, lambda kk: cqn[:, kk, :n], 3, n, [dw, dcqn])
                    pi = pr % 6; pr += 1; ps2 = psum[pi]; dp2 = dps[pi]
                    mmk(ps2, dp2, wuqs, h * 96, 96, lambda kk: cqn[:, kk, :n], 3, n, [dw, dcqn])
                    s_ = it % 2; it += 1
                    k.op(k.act, lambda h=h, ps1=ps1: nc.scalar.activation(out=qall[0:64, h, :n], in_=ps1[0:64, :n], func=AF.Copy), reads=[dp1], writes=[dqall])
                    k.op(k.dve, lambda s_=s_, ps1=ps1: nc.vector.tensor_tensor(out=t1[s_][64:96, :n], in0=ps1[64:96, :n], in1=rc[64:96, col0:col0 + n], op=ALU.mult),
                         reads=[dp1, dw], writes=[dt1[s_]])
                    k.op(k.dve, lambda s_=s_, ps2=ps2: nc.vector.tensor_tensor(out=t2[s_][64:96, :n], in0=ps2[64:96, :n], in1=rsn[64:96, col0:col0 + n], op=ALU.mult),
                         reads=[dp2, dw], writes=[dt2[s_]])
                    k.op(k.pool, lambda s_=s_, h=h: nc.gpsimd.tensor_tensor(out=qall[64:96, h, :n], in0=t1[s_][64:96, :n], in1=t2[s_][64:96, :n], op=ALU.add),
                         reads=[dt1[s_], dt2[s_]], writes=[dqall])
                k.dma(STQ[0], qT_s[:, col0:col0 + n].rearrange("(h r) t -> r h t", r=96), qall[0:96, :, :n], reads=[dqall], writes=[dq])
                for m in range(NCH):
                    pi = pr % 6; pr += 1; ps = psum[pi]; dp = dps[pi]
                    mmk(ps, dp, wuk, m * 128, 128, lambda kk: cknb[:, kk, :n], 2, n, [dw, dcknb])
                    k.op(k.act, lambda m=m, ps=ps: nc.scalar.activation(out=ko[:, m, :n], in_=ps[:, :n], func=AF.Copy), reads=[dp], writes=[dko])
                k.dma(STQ[0], kT_s[0:D, col0:col0 + n].rearrange("(c p) t -> p c t", p=128), ko[:, :, :n], reads=[dko], writes=[dkk])
                for s0 in range(0, n, 128):
                    nt = min(128, n - s0)
                    tok0 = col0 + s0
                    b_ = it % 2; it += 1
                    for fb in range(2):
                        pi = pr % 6; pr += 1; ps = psum[pi]; dp = dps[pi]
                        for c in range(2):
                            mm(ps[:nt, :512], cknb[:, c, s0:s0 + nt], wuv[:, c, fb * 512:(fb + 1) * 512], c == 0, c == 1, [dw, dcknb], [dp], last=(c == 1))
                        k.op(k.dve, lambda fb=fb, ps=ps: nc.vector.tensor_copy(out=vtb[b_][:nt, fb * 512:(fb + 1) * 512], in_=ps[:nt, :512]),
                             reads=[dp], writes=[dvtb[b_]])
                    k.dma(STQ[0], v_s[tok0:tok0 + nt, :], vtb[b_][:nt, :], reads=[dvtb[b_]], writes=[dv])
                    for c in range(2):
                        pi = pr % 6; pr += 1; ps = psum[pi]; dp = dps[pi]
                        mm(ps[:nt, :128], ckn32[:, c, s0:s0 + nt], ident[:, :], True, True, [dckn32, d_const], [dp])
                        k.op(k.act, lambda c=c, ps=ps: nc.scalar.activation(out=ctm[b_][:nt, c * 128:(c + 1) * 128], in_=ps[:nt, :128], func=AF.Copy),
                             reads=[dp], writes=[dctm[b_]])
                    k.dma(STQ[0], O["a_ckv"][tok0:tok0 + nt, :], ctm[b_][:nt, :], reads=[dctm[b_]])
                    pi = pr % 6; pr += 1; ps = psum[pi]; dp = dps[pi]
                    mm(ps[:nt, :32], krf[64:96, s0:s0 + nt], ident[64:96, 64:96], True, True, [dkrf, d_const], [dp])
                    k.op(k.act, lambda ps=ps: nc.scalar.activation(out=krtm[b_][:nt, :], in_=ps[:nt, :32], func=AF.Copy), reads=[dp], writes=[dkrtm[b_]])
                    k.dma(STQ[0], O["a_kr"][tok0:tok0 + nt, :], krtm[b_][:nt, :], reads=[dkrtm[b_]])
            k.barrier()
        cur[0] = None

    def phase_attn_b(lnext):
        Lc = P
        STQ[0] = k.sp
        with ExitStack() as es:
            cur[0] = es
            mt, dm = load_masks("maskB", 4)
            mtp, dmp = load_masks("maskBp", 4)
            ms = sb("maskBs", [32, 2, 512], BF16)
            with ExitStack() as es2:
                cur[0] = es2
                st = sb("mstg2", [32, 2, 512], F32); dst_ = Dep()
                k.dma(k.sp, st[:, 0, :], I["maskBs"], writes=[dst_])
                k.dma(k.sp, st[:, 1, :], I["maskBsp"], writes=[dst_])
                k.op(k.dve, lambda: nc.vector.tensor_copy(out=ms[:], in_=st[:]), reads=[dst_], writes=[dm])
                k.barrier()
            cur[0] = es
            abs_ = [AttnBufs(Lc, 64) for _ in range(2)]
            for ab in abs_:
                ab.zero_q()
            Kn = [sb("Kn", [128, TT + 2 * Lc], BF16) for _ in range(2)]; dKn = [Dep(), Dep()]
            NE, NL, NA = 3, 4, 3
            et = [sb("et", [128, 512], F32) for _ in range(NE)]; det = [Dep() for _ in range(NE)]
            Lt = [sb("Lt", [128, 512], BF16) for _ in range(NL)]; dLt = [Dep() for _ in range(NL)]
            at = [sb("at", [128, 512], BF16) for _ in range(NA)]; dat = [Dep() for _ in range(NA)]
            Ra = [sb("Ra", [128, 512], BF16) for _ in range(2)]; dRa = [Dep(), Dep()]
            ot = [sb("otb", [128, 512], BF16) for _ in range(2)]; dot_ = [Dep(), Dep()]
            c = {"s": 0, "c": 0, "o": 0, "l": 0, "e": 0, "a": 0}

            seq = []
            tstep = [0]

            def advance():
                t = tstep[0]; tstep[0] += 1
                if t < len(seq):
                    seq[t][0]()
                if 0 <= t - 2 < len(seq):
                    seq[t - 2][2]()
                if t < len(seq):
                    seq[t][1]()
                if 0 <= t - 3 < len(seq):
                    seq[t - 3][3]()

            pstore = []

            def flush():
                while tstep[0] < len(seq) + 3:
                    advance()
                seq.clear(); tstep[0] = 0
                while pstore:
                    pstore.pop(0)()

            def run_q(ab, Knb, dKnb, pbase, hsel, h, qcol0, nq, blocks):
                oc = c["o"]; c["o"] += 1
                ri = oc % 2
                psO = psum[6 + ri]; dO = dps[6 + ri]
                nb_ = len(blocks)
                for j, (kcol, nk, slot, mneg, mpos) in enumerate(blocks):
                    st = {}

                    def fA(kcol=kcol, nk=nk, mneg=mneg, st=st):
                        si = c["s"] % 3; c["s"] += 1
                        psS = psum[si]; dS = dps[si]
                        mm(psS[:nk, :nq], ab.Kf[:, kcol:kcol + nk], ab.Qz[pbase // 64][:, qcol0:qcol0 + nq],
                           True, mneg is None, [ab.dK, ab.dQ], [dS], last=(mneg is None))
                        if mneg is not None:
                            mm(psS[:nk, :nq], ident_bf[:nk, :nk], mneg[:nk, :nq], False, True, [d_const, dm], [dS])
                        ei = c["e"] % NE; c["e"] += 1
                        li = c["l"] % NL; c["l"] += 1
                        k.op(k.act, lambda: nc.scalar.activation(out=et[ei][:nk, :nq], in_=psS[:nk, :nq], func=AF.Exp, scale=0.125),
                             reads=[dS], writes=[det[ei]])
                        st["ei"] = ei; st["li"] = li

                    def fL(nk=nk, st=st):
                        ei = st["ei"]; li = st["li"]
                        k.op(k.act, lambda: nc.scalar.activation(out=Lt[li][:nk, :nq], in_=et[ei][:nk, :nq], func=AF.Ln, bias=1.0),
                             reads=[det[ei]], writes=[dLt[li]])

                    def f2(j=j, kcol=kcol, nk=nk, mpos=mpos, st=st):
                        li = st["li"]
                        ci = 3 + c["c"] % 3; c["c"] += 1
                        psC = psum[ci]; dC = dps[ci]
                        mm(psC[:nk, :nq], tri_bf[:nk, :nk], Lt[li][:nk, :nq], True, False, [d_const, dLt[li]], [dC], last=False)
                        if j > 0:
                            mm(psC[:nk, :nq], ones_bf[:, :nk], Ra[j % 2][:, :nq], False, False, [d_const, dRa[j % 2]], [dC], last=False)
                        if mpos is not None:
                            mm(psC[:nk, :nq], ident_bf[:nk, :nk], mpos[:nk, :nq], False, False, [d_const, dm, dmp], [dC], last=False)
                        mm(psC[:nk, :nq], Knb[:, kcol:kcol + nk], ab.Qz[pbase // 64][:, qcol0:qcol0 + nq], False, True,
                           [dKnb, ab.dQ], [dC])
                        ai = c["a"] % NA; c["a"] += 1
                        k.op(k.act, lambda: nc.scalar.activation(out=at[ai][:nk, :nq], in_=psC[:nk, :nq], func=AF.Exp, scale=-1.0),
                             reads=[dC], writes=[dat[ai]])
                        st["ai"] = ai
                        rn = (j + 1) % 2
                        if j + 1 < nb_:
                            if j == 0:
                                if nk < 128:
                                    k.op(k.dve, lambda: nc.vector.memset(Ra[rn][:, :nq], 0.0), writes=[dRa[rn]])
                                k.op(k.dve, lambda: nc.vector.tensor_copy(out=Ra[rn][:nk, :nq], in_=Lt[li][:nk, :nq]), reads=[dLt[li]], writes=[dRa[rn]])
                            else:
                                k.op(k.dve, lambda: nc.vector.tensor_tensor(out=Ra[rn][:, :nq], in0=Ra[j % 2][:, :nq], in1=Lt[li][:, :nq], op=ALU.add),
                                     reads=[dRa[j % 2], dLt[li]], writes=[dRa[rn]])

                    def f3(j=j, nk=nk, slot=slot, st=st):
                        ai = st["ai"]
                        mm(psO[:, :nq], ab.Va[:nk, slot, :, :], at[ai][:nk, :nq], j == 0, j == nb_ - 1, [ab.dV, dat[ai]], [dO], last=(j == nb_ - 1))
                        if j == nb_ - 1:
                            k.op(k.dve, lambda: nc.vector.tensor_copy(out=ot[ri][pbase:pbase + 64, :nq], in_=psO[pbase:pbase + 64, :nq]), reads=[dO], writes=[dot_[ri]])
                            while pstore:
                                pstore.pop(0)()
                            pstore.append(lambda: k.dma(STQ[0], attnT[h * 64:(h + 1) * 64, qcol0:qcol0 + nq], ot[ri][pbase:pbase + 64, :nq], reads=[dot_[ri]], writes=[dattn]))

                    seq.append((fA, fL, f2, f3))
                    advance()

            def load_unit(u):
                ab = abs_[u % 2]
                load_K(ab, 0, u * 128, 128)
                ab.load_q_pair(u)
                load_V(ab, 0, u * 128)
                load_V(ab, 1, u * 128 + 64)
                k.op(k.pool, lambda u=u, ab=ab: nc.gpsimd.tensor_scalar(out=Kn[u % 2][:, :], in0=ab.Kf[:, :], scalar1=-0.125, scalar2=1.0, op0=ALU.mult, op1=ALU.mult),
                     reads=[ab.dK], writes=[dKn[u % 2]])

            bg = AdaBg([], 4, [7])
            load_unit(0)
            bg.load(2)
            for u in range(8):
                ab = abs_[u % 2]
                if u + 1 < 8:
                    flush()
                    load_unit(u + 1)
                bg.step(2)
                for (pbase, hsel, h) in [(0, 0, 2 * u), (64, 1, 2 * u + 1)]:
                    for qi in range(NPT):
                        blocks = []
                        for kt in range(4 * qi + 3, -1, -1):
                            dg = kt >= 4 * qi
                            blocks.append((kt * 128, 128, kt, mt[:, kt - 4 * qi, :] if dg else None, mtp[:, kt - 4 * qi, :] if dg else None))
                        run_q(ab, Kn[u % 2], dKn[u % 2], pbase, hsel, h, qi * 512, 512, blocks)
                    for b in range(2):
                        blocks = [(S + b * T_S, T_S, ab.slot_new(b), ms[:, 0, :], ms[:, 1, :])]
                        for t in range(ab.ncs - 1, -1, -1):
                            blocks.append((ab.kcol(b, t * 128), 128, ab.slot_cache(b, t), None, None))
                        run_q(ab, Kn[u % 2], dKn[u % 2], pbase, hsel, h, S + b * T_S, T_S, blocks)
            flush()
            bg.finish()
            k.barrier()
        cur[0] = None
        STQ[0] = k.act

    one_t = sb("one_t", [128, 1], F32)
    k.op(k.pool, lambda: nc.gpsimd.memset(one_t[:], 1.0), writes=[d_const])

    phase_adaln0()
    wo_names = ["w_a_o", "w_b_o", "w_c_o", "w_d_o"]
    for l in range(depth):
        mix = l % 4
        first = (l == 0)
        if mix == 0:
            phase_proj_a(l, first)
            phase_attn_softmax(0, l + 1)
        elif mix == 1:
            phase_qkv(l, 1, first)
            phase_attn_b(l + 1)
        else:
            phase_qkv(l, mix, first)
            phase_attn_softmax(mix, l + 1)
        with ExitStack() as eL:
            cur[0] = eL
            win = sb("win", [128, NCH, 2 * HID], BF16); dwin = Dep()
            phase_oproj(l, I[wo_names[mix]], first, win, dwin)
            phase_ffn(l, l == depth - 1, win, dwin)
        cur[0] = None
    phase_final()
    k.barrier()
    return nc, k


ROPE_THETA = 500000.0


def _consts(S, P):
    TT = S + 64
    c = {}
    c["ident"] = np.eye(128, dtype=np.float32)
    j = np.arange(128)[:, None]; kk = np.arange(128)[None, :]
    c["tri"] = (j >= kk).astype(np.float32)
    c["antiI"] = (j + kk == 127).astype(np.float32)
    p = np.arange(128)[:, None]; ql = np.arange(512)[None, :]
    qc = ql // 64

    def kc_of(r):
        return np.floor_divide(128 * r + p, 64)
    c["maskA"] = (np.stack([(kc_of(r) <= qc) for r in range(4)]).astype(np.float32) - 1.0) * 30000.0
    mB = np.stack([((128 * r + p) < ql) for r in range(4)]).astype(np.float32)
    c["maskB"] = (mB - 1.0) * 30000.0
    c["maskBp"] = (1.0 - mB) * 30000.0
    mb = np.zeros((32, 512), np.float32)
    mb[:, :32] = (np.arange(32)[:, None] < np.arange(32)[None, :])
    c["maskBs"] = (mb - 1.0) * 30000.0
    c["maskBsp"] = (1.0 - mb) * 30000.0
    c["maskC"] = (np.stack([((kc_of(r) <= qc) & (kc_of(r) >= qc - 2)) for r in range(-1, 4)]).astype(np.float32) - 1.0) * 30000.0
    c["maskD"] = (np.stack([((kc_of(r) <= qc) & (kc_of(r) >= qc - 8)) for r in range(-4, 4)]).astype(np.float32) - 1.0) * 30000.0
    c["maskD"] = np.ascontiguousarray(c["maskD"][:, ::-1, :])
    pos = np.concatenate([np.arange(S), P + np.arange(32), P + np.arange(32)]).astype(np.float32)
    ca = np.zeros((128, TT), np.float32); sa = np.zeros((128, TT), np.float32)
    inv = (ROPE_THETA ** (-np.arange(16, dtype=np.float32) * 2.0 / 32)).astype(np.float32)
    ang = pos[None, :] * inv[:, None]
    ca[64:80] = np.cos(ang); ca[80:96] = np.cos(ang)
    sa[64:80] = -np.sin(ang); sa[80:96] = np.sin(ang)
    c["ropeA_cos"] = ca; c["ropeA_sin"] = sa
    cc = np.ones((128, TT), np.float32); sc = np.zeros((128, TT), np.float32)
    inv = (ROPE_THETA ** (-np.arange(8, dtype=np.float32) * 2.0 / 16)).astype(np.float32)
    ang = pos[None, :] * inv[:, None]
    for hb in (0, 64):
        cc[hb:hb + 8] = np.cos(ang); cc[hb + 8:hb + 16] = np.cos(ang)
        sc[hb:hb + 8] = -np.sin(ang); sc[hb + 8:hb + 16] = np.sin(ang)
    c["ropeC_cos"] = cc; c["ropeC_sin"] = sc
    lcc = min(128, P)
    U = np.zeros((64, TT + 2 * lcc), np.float32)
    W = np.zeros((64, TT), np.float32)
    cid = np.arange(64)[:, None]
    kpos = np.arange(S)[None, :]
    if S // 64 <= 64:
        U[:, :S] = (kpos // 64 == cid)
        qc_ = kpos // 64
        W[:, :S] = np.where((cid <= qc_) & (cid >= qc_ - 2), 0.0, -30000.0)
    c["chunkU"] = U; c["chunkW"] = W
    return c


def _prep(inp, i, S, P, consts):
    f = np.ascontiguousarray
    m = dict(consts)
    xs = inp["x_sample"]
    m["xT"] = f(np.concatenate([inp["x_prompt"][i].T, xs[2 * i].T, xs[2 * i + 1].T], axis=1))
    cv = np.stack([inp["c_prompt"][i], inp["c_sample"][2 * i], inp["c_sample"][2 * i + 1]])
    m["cT"] = f(cv.reshape(3, 8, 128).transpose(2, 1, 0))
    m["w_mod"] = inp["w_mod"]
    m["bmodT"] = f(inp["b_mod"].reshape(4, 48, 128).transpose(2, 0, 1))
    m["gmixT"] = f(inp["g_mix"].reshape(4, 8, 128).transpose(2, 0, 1))
    m["gffnT"] = f(inp["g_ffn"].reshape(4, 8, 128).transpose(2, 0, 1))
    m["gfinT"] = f(inp["g_final"].reshape(8, 128).T)
    m["w_ffn_in"] = inp["w_ffn_in"]; m["w_ffn_out"] = inp["w_ffn_out"]
    m["w_a_down"] = inp["w_a_down"][0]
    m["gaqT"] = f(inp["g_a_q"][0].reshape(3, 128).T); m["gakvT"] = f(inp["g_a_kv"][0].reshape(2, 128).T)
    m["w_a_uq"] = inp["w_a_uq"][0]
    m["w_a_uk"] = inp["w_a_uk"][0].reshape(256, 1024); m["w_a_uv"] = inp["w_a_uv"][0].reshape(256, 1024)
    m["w_a_o"] = inp["w_a_o"][0]
    sl = slice(2 * i, 2 * i + 2)
    m["cache_a_ckv"] = f(inp["cache_a_ckv"][0, sl]); m["cache_a_kr"] = f(inp["cache_a_krope"][0, sl])
    m["w_b_qkv"] = inp["w_b_qkv"][0]; m["w_b_o"] = inp["w_b_o"][0]
    m["cache_b_k"] = f(inp["cache_b_k"][0, sl]).reshape(2, P, 1024); m["cache_b_v"] = f(inp["cache_b_v"][0, sl]).reshape(2, P, 1024)
    m["w_c_qkv"] = inp["w_c_qkv"][0]
    b = inp["b_c_qkv"][0]
    m["bcT"] = f(b.reshape(12, 128).T)
    idx = np.arange(1536); d = idx % 64
    src = np.where((idx < 1280) & (d < 8), idx + 8, np.where((idx < 1280) & (d >= 8) & (d < 16), idx - 8, idx))
    m["bcswT"] = f(b[src].reshape(12, 128).T)
    m["bc_row"] = f(b.reshape(1, 1536))
    m["sink_c"] = f(inp["sink_c"].reshape(1, 16))
    m["w_c_o"] = inp["w_c_o"][0]
    lc = inp["cache_c_k"].shape[2]
    m["cache_c_k"] = f(inp["cache_c_k"][0, sl]).reshape(2, lc, 256); m["cache_c_v"] = f(inp["cache_c_v"][0, sl]).reshape(2, lc, 256)
    m["w_d_qkv"] = inp["w_d_qkv"][0]; m["rel_bias_d"] = inp["rel_bias_d"][0]; m["w_d_o"] = inp["w_d_o"][0]
    ld = inp["cache_d_k"].shape[2]
    m["cache_d_k"] = f(inp["cache_d_k"][0, sl]).reshape(2, ld, 1024); m["cache_d_v"] = f(inp["cache_d_v"][0, sl]).reshape(2, ld, 1024)
    return {k_: np.ascontiguousarray(v, dtype=np.float32) for k_, v in m.items()}


def _assemble(res, S, P, lc, ld):
    n = len(res)
    f32 = np.float32
    yp = np.stack([r["yT"][:, :S].T for r in res]).astype(f32)
    ys = np.stack([r["yT"][:, S + 32 * b:S + 32 * (b + 1)].T for r in res for b in range(2)]).astype(f32)

    def pp(name, shp):
        return np.stack([r[name][:S].reshape((S,) + shp) for r in res])[None].astype(f32)

    def ss(name, shp):
        return np.stack([r[name][S + 32 * b:S + 32 * (b + 1)].reshape((32,) + shp) for r in res for b in range(2)])[None].astype(f32)

    def p2(name, shp):
        return np.stack([r[name].reshape(shp) for r in res])[None].astype(f32)

    def s2(name, shp):
        return np.stack([r[name][b].reshape(shp) for r in res for b in range(2)])[None].astype(f32)

    kp_c = min(128, S); kp_d = min(512, S)
    return (yp, ys, pp("a_ckv", (256,)), pp("a_kr", (32,)), ss("a_ckv", (256,)), ss("a_kr", (32,)),
            pp("b_k", (16, 64)), pp("b_v", (16, 64)), ss("b_k", (16, 64)), ss("b_v", (16, 64)),
            p2("c_k_p", (kp_c, 4, 64)), p2("c_v_p", (kp_c, 4, 64)), s2("c_k_s", (lc, 4, 64)), s2("c_v_s", (lc, 4, 64)),
            p2("d_k_p", (kp_d, 16, 64)), p2("d_v_p", (kp_d, 16, 64)), s2("d_k_s", (ld, 16, 64)), s2("d_v_s", (ld, 16, 64)))


def kernel(**inputs):
    inp = {k_: np.asarray(v) for k_, v in inputs.items()}
    nb, S = inp["x_prompt"].shape[0], inp["x_prompt"].shape[1]
    P = inp["cache_a_ckv"].shape[2]
    nc, _ = build(S, P)
    consts = _consts(S, P)
    in_maps = [_prep(inp, i, S, P, consts) for i in range(nb)]
    res = run_bass_kernel_spmd(nc, in_maps, core_ids=list(range(nb)))
    return _assemble(res.results, S, P, inp["cache_c_k"].shape[2], inp["cache_d_k"].shape[2])
```

```python
import numpy as np
import concourse.bass as bass
import concourse.mybir as mybir
from concourse.bass_utils import run_bass_kernel_spmd

F32 = mybir.dt.float32
BF16 = mybir.dt.bfloat16
AF = mybir.ActivationFunctionType
ALU = mybir.AluOpType

NDS = 40


class Dep:
    __slots__ = ("w", "r")

    def __init__(self):
        self.w = None
        self.r = {}


class Eng:
    def __init__(self, nc, eng, name):
        self.e = eng
        self.key = name
        self.sem = nc.alloc_semaphore(name)
        self.n = 0
        self.seen = {}


class K:
    def __init__(self, nc):
        self.nc = nc
        self.pe = Eng(nc, nc.tensor, "s_pe")
        self.act = Eng(nc, nc.scalar, "s_act")
        self.dve = Eng(nc, nc.vector, "s_dve")
        self.pool = Eng(nc, nc.gpsimd, "s_pool")
        self.sp = Eng(nc, nc.sync, "s_sp")
        self.engs = [self.pe, self.act, self.dve, self.pool, self.sp]
        self.dsems = [nc.alloc_semaphore("s_d%d" % i) for i in range(NDS)]
        self.dval = [0] * NDS
        self.dnext = 0
        self.ninst = 0

    def _wait(self, E, ev):
        k, s, v = ev
        if E.seen.get(k, 0) >= v:
            return
        E.e.wait_ge(s, v)
        E.seen[k] = v

    def _deps(self, E, reads, writes, skip_self=False):
        for b in reads:
            if b.w is not None and not (skip_self and b.w[0] == E.key):
                self._wait(E, b.w)
        for b in writes:
            if b.w is not None and not (skip_self and b.w[0] == E.key):
                self._wait(E, b.w)
            for k, (s, v) in b.r.items():
                if skip_self and k == E.key:
                    continue
                self._wait(E, (k, s, v))

    def _mark(self, ev, reads, writes):
        k, s, v = ev
        for b in reads:
            o = b.r.get(k)
            if o is None or o[1] < v:
                b.r[k] = (s, v)
        for b in writes:
            b.w = ev
            b.r = {}

    def op(self, E, fn, reads=(), writes=(), inc=True, skip_self=False):
        self._deps(E, reads, writes, skip_self)
        ins = fn()
        self.ninst += 1
        if inc:
            E.n += 1
            ins.then_inc(E.sem, 1)
            ev = (E.key, E.sem, E.n)
        else:
            ev = (E.key, E.sem, E.n + 1)
        self._mark(ev, reads, writes)
        return ins

    def dma(self, Q, out, in_, reads=(), writes=(), **kw):
        i = self.dnext
        self.dnext = (i + 1) % NDS
        key = "d%d" % i
        if self.dval[i] > 0:
            self._wait(Q, (key, self.dsems[i], self.dval[i]))
        self._deps(Q, reads, writes)
        self.dval[i] += 16
        Q.e.dma_start(out=out, in_=in_, **kw).then_inc(self.dsems[i], 16)
        self.ninst += 1
        self._mark((key, self.dsems[i], self.dval[i]), reads, writes)

    def barrier(self):
        sp = self.sp
        for i in range(NDS):
            if self.dval[i] > 0:
                self._wait(sp, ("d%d" % i, self.dsems[i], self.dval[i]))
        for E in self.engs:
            if E is not sp and E.n > 0:
                self._wait(sp, (E.key, E.sem, E.n))
        sp.n += 1
        sp.e.sem_inc(sp.sem, 1)
        for E in self.engs:
            if E is not sp:
                self._wait(E, (sp.key, sp.sem, sp.n))
                for F in self.engs:
                    E.seen[F.key] = max(E.seen.get(F.key, 0), F.n)
                for i in range(NDS):
                    E.seen["d%d" % i] = self.dval[i]

from contextlib import ExitStack

D = 1024
NCH = 8
T_S = 32
HID = 2816
HCH = 22
EPS = 1e-6


def build(S, P, depth=4):
    nc = bass.Bass("TRN2", target_bir_lowering=False)
    k = K(nc)
    TT = S + 2 * T_S
    NPT = S // 512
    tiles = [(t * 512, 512, 0) for t in range(NPT)] + [(S, T_S, 1), (S + T_S, T_S, 2)]
    LC = [P, P, min(128, P), min(512, P)]
    NKV = [16, 16, 4, 16]

    def din(name, shape, dt=F32):
        return nc.dram_tensor(name, list(shape), dt, kind="ExternalInput").ap()

    def dout(name, shape, dt=F32):
        return nc.dram_tensor(name, list(shape), dt, kind="ExternalOutput").ap()

    def dscr(name, shape, dt):
        return nc.dram_tensor(name, list(shape), dt, kind="Internal").ap()

    I = {}
    for nm, shp in [("xT", [D, TT]), ("cT", [128, NCH, 3]), ("w_mod", [4, D, 6 * D]), ("bmodT", [128, 4, 48]),
                    ("gmixT", [128, 4, NCH]), ("gffnT", [128, 4, NCH]), ("gfinT", [128, NCH]),
                    ("w_ffn_in", [4, D, 2 * HID]), ("w_ffn_out", [4, HID, D]), ("ident", [128, 128]),
                    ("tri", [128, 128]), ("antiI", [128, 128]),
                    ("w_a_down", [D, 672]), ("gaqT", [128, 3]), ("gakvT", [128, 2]), ("w_a_uq", [384, 1536]),
                    ("w_a_uk", [256, 1024]), ("w_a_uv", [256, 1024]), ("w_a_o", [D, D]),
                    ("cache_a_ckv", [2, P, 256]), ("cache_a_kr", [2, P, 32]),
                    ("ropeA_cos", [128, TT]), ("ropeA_sin", [128, TT]), ("maskA", [4, 128, 512]),
                    ("w_b_qkv", [D, 3 * D]), ("w_b_o", [D, D]), ("cache_b_k", [2, P, D]), ("cache_b_v", [2, P, D]),
                    ("maskB", [4, 128, 512]), ("maskBp", [4, 128, 512]), ("maskBs", [32, 512]), ("maskBsp", [32, 512]),
                    ("w_c_qkv", [D, 1536]), ("bcT", [128, 12]), ("bcswT", [128, 12]), ("bc_row", [1, 1536]), ("sink_c", [1, 16]),
                    ("w_c_o", [D, D]),
                    ("cache_c_k", [2, LC[2], 256]), ("cache_c_v", [2, LC[2], 256]),
                    ("ropeC_cos", [128, TT]), ("ropeC_sin", [128, TT]), ("maskC", [5, 128, 512]),
                    ("chunkU", [64, TT + 2 * LC[2]]), ("chunkW", [64, TT]),
                    ("w_d_qkv", [D, 3 * D]), ("rel_bias_d", [16, 257]), ("w_d_o", [D, D]),
                    ("cache_d_k", [2, LC[3], D]), ("cache_d_v", [2, LC[3], D]), ("maskD", [8, 128, 512])]:
        I[nm] = din(nm, shp)
    O = {}
    for nm, shp in [("yT", [D, TT]), ("a_ckv", [TT, 256]), ("a_kr", [TT, 32]), ("b_k", [TT, D]), ("b_v", [TT, D]),
                    ("c_k_p", [min(128, S), 256]), ("c_v_p", [min(128, S), 256]),
                    ("c_k_s", [2, LC[2], 256]), ("c_v_s", [2, LC[2], 256]),
                    ("d_k_p", [min(512, S), D]), ("d_v_p", [min(512, S), D]),
                    ("d_k_s", [2, LC[3], D]), ("d_v_s", [2, LC[3], D])]:
        O[nm] = dout(nm, shp)

    xres = dscr("xres", [D, TT], F32)
    dxres = [Dep() for _ in tiles]
    attnT = dscr("attnT", [D, TT], BF16)
    dattn = Dep()
    TTX = TT + 2 * P
    qT_s = dscr("qT_s", [16 * 96, TT], BF16)
    kT_s = dscr("kT_s", [D + 32, TTX], BF16)
    v_s = dscr("v_s", [TTX, D], BF16)
    dq = Dep(); dkk = Dep(); dv = Dep()
    ext_s = dscr("ext_s", [16, 1536], F32)

    cur = [None]
    uid = [0]

    def sb(name, shape, dt):
        uid[0] += 1
        if cur[0] is None:
            return nc.alloc_sbuf_tensor("%s_%d" % (name, uid[0]), list(shape), dt)
        return cur[0].enter_context(nc.sbuf_tensor("%s_%d" % (name, uid[0]), list(shape), dt))

    ones_bf = sb("ones_bf", [128, 128], BF16); d_const = Dep()
    ident = sb("ident_s", [128, 128], F32)
    mod = sb("mod", [128, 4, 48, 3], F32); d_mod = Dep()
    amix = sb("amix", [128, 4, NCH, 3], F32)
    affn = sb("affn", [128, 4, NCH, 3], F32)
    gfin = sb("gfin", [128, NCH], F32)
    psum = [nc.alloc_psum_tensor("ps%d" % i, [128, 512], F32) for i in range(8)]
    dps = [Dep() for _ in range(8)]

    k.op(k.pool, lambda: nc.gpsimd.memset(ones_bf[:], 1.0), writes=[d_const])
    k.dma(k.sp, ident[:], I["ident"], writes=[d_const])
    k.dma(k.sp, gfin[:], I["gfinT"], writes=[d_mod])

    STQ = [k.act]
    cast_rr = [0]

    def cast(out, in_, reads, writes):
        cast_rr[0] ^= 1
        if cast_rr[0]:
            k.op(k.dve, lambda: nc.vector.tensor_copy(out=out, in_=in_), reads=reads, writes=writes)
        else:
            k.op(k.pool, lambda: nc.gpsimd.tensor_copy(out=out, in_=in_), reads=reads, writes=writes)

    def mm(out, lhsT, rhs, start, stop, reads, writes, last=True):
        k.op(k.pe, lambda: nc.tensor.matmul(out, lhsT=lhsT, rhs=rhs, start=start, stop=stop),
             reads=reads, writes=writes, inc=last, skip_self=True)

    def mmk(ps, dp, w, wcol0, M, rhs_of, kc, n, reads, pbase=0):
        for kk in range(kc):
            mm(ps[pbase:pbase + M, :n], w[:, kk, wcol0:wcol0 + M], rhs_of(kk), kk == 0, kk == kc - 1,
               reads, [dp], last=(kk == kc - 1))

    ld_rr = [0]

    def load_w(stg, dstg, dst, ddst, W, kc, ncols, dst_col0=0):
        for k0 in range(0, kc, 2):
            kk = min(2, kc - k0)
            for c0 in range(0, ncols, 2048):
                nn = min(2048, ncols - c0)
                i = ld_rr[0] % len(stg); ld_rr[0] += 1
                view = stg[i][:, :, :].rearrange("p a b -> p (a b)")[:, 0:kk * nn].rearrange("p (k n) -> p k n", n=nn)
                src = W[k0 * 128:(k0 + kk) * 128, c0:c0 + nn].rearrange("(k p) n -> p k n", p=128)
                k.dma(k.sp, view, src, writes=[dstg[i]])
                cast(dst[:, k0:k0 + kk, dst_col0 + c0:dst_col0 + c0 + nn], view,
                     reads=[dstg[i]], writes=[ddst])

    def load_w_gen(stg, dstg, dst, ddst, W, kc, ncols):
        for k0 in range(0, kc, 2):
            kk = min(2, kc - k0)
            for c0 in range(0, ncols, 2048):
                nn = min(2048, ncols - c0)
                i = ld_rr[0] % len(stg); ld_rr[0] += 1
                view = stg[i][:, :, :].rearrange("p a b -> p (a b)")[:, 0:kk * nn].rearrange("p (k n) -> p k n", n=nn)
                src = W[k0 * 128:(k0 + kk) * 128, c0:c0 + nn].rearrange("(k p) n -> p k n", p=128)
                k.dma(k.sp, view, src, writes=[dstg[i]])
                cast(dst[:, k0:k0 + kk, c0:c0 + nn], view, reads=[dstg[i]], writes=[ddst])
                yield

    def mkstg(n=2):
        return [sb("stg", [128, NCH, 512], F32) for _ in range(n)], [Dep() for _ in range(n)]

    sc = sb("sc", [128, NCH, 3], F32); dsc = Dep()
    bm = sb("bm", [128, 4, 48], F32); dbm = Dep()
    gm = sb("gm", [128, 4, NCH], F32)
    gf = sb("gf", [128, 4, NCH], F32)
    k.dma(k.sp, sc[:], I["cT"], writes=[dsc])
    k.dma(k.sp, bm[:], I["bmodT"], writes=[dbm])
    k.dma(k.sp, gm[:], I["gmixT"], writes=[dbm])
    k.dma(k.sp, gf[:], I["gffnT"], writes=[dbm])
    k.op(k.act, lambda: nc.scalar.activation(out=sc[:], in_=sc[:], func=AF.Silu), reads=[dsc], writes=[dsc])

    class AdaBg:
        def __init__(self, layers, nbuf, banks):
            self.layers = [l for l in layers if l < depth]
            self.nbuf = nbuf
            self.nblk = 12 * len(self.layers)
            if self.nblk:
                self.stg = [sb("adstg", [128, NCH, 512], F32) for _ in range(nbuf)]
                self.dstg = [Dep() for _ in range(nbuf)]
            self.banks = banks
            self.bl = 0; self.bc = 0

        def load(self, n):
            for _ in range(n):
                if self.bl >= self.nblk:
                    return
                i = self.bl % self.nbuf
                l = self.layers[self.bl // 12]; blk = self.bl % 12
                src = I["w_mod"][l, :, blk * 512:(blk + 1) * 512].rearrange("(k p) n -> p k n", p=128)
                k.dma(k.sp, self.stg[i][:], src, writes=[self.dstg[i]])
                self.bl += 1

        def compute(self):
            while self.bc < self.bl:
                i = self.bc % self.nbuf
                li = self.bc // 12; blk = self.bc % 12
                ps = psum[self.banks[li]]; dp = dps[self.banks[li]]
                for m in range(4):
                    j = blk * 4 + m
                    for kk in range(NCH):
                        mm(ps[:, j * 3:(j + 1) * 3], self.stg[i][:, kk, m * 128:(m + 1) * 128], sc[:, kk, :],
                           kk == 0, kk == NCH - 1, [self.dstg[i], dsc], [dp], last=(kk == NCH - 1))
                self.bc += 1
                if blk == 11:
                    self.evac(li)

        def evac(self, li):
            l = self.layers[li]
            ps = psum[self.banks[li]]; dp = dps[self.banks[li]]
            pv = ps[:, 0:144].rearrange("p (j s) -> p j s", s=3)
            for s_ in range(3):
                k.op(k.dve, lambda s_=s_: nc.vector.tensor_tensor(out=mod[:, l, :, s_], in0=pv[:, :, s_], in1=bm[:, l, :], op=ALU.add),
                     reads=[dp, dbm], writes=[d_mod])
            for s_ in range(3):
                k.op(k.dve, lambda s_=s_: nc.vector.scalar_tensor_tensor(
                    out=amix[:, l, :, s_], in0=mod[:, l, 8:16, s_], scalar=1.0, in1=gm[:, l, :], op0=ALU.add, op1=ALU.mult),
                    reads=[d_mod, dbm], writes=[d_mod])
                k.op(k.dve, lambda s_=s_: nc.vector.scalar_tensor_tensor(
                    out=affn[:, l, :, s_], in0=mod[:, l, 32:40, s_], scalar=1.0, in1=gf[:, l, :], op0=ALU.add, op1=ALU.mult),
                    reads=[d_mod, dbm], writes=[d_mod])

        def step(self, n):
            self.compute()
            self.load(n)

        def finish(self):
            while self.bc < self.nblk:
                self.load(self.nbuf)
                self.compute()

    def phase_adaln0():
        with ExitStack() as es:
            cur[0] = es
            bg = AdaBg([0], 3, [0])
            bg.finish()
            k.barrier()
        cur[0] = None

    class NormBufs:
        def __init__(self):
            self.xt = sb("xt", [128, NCH, 512], F32); self.dxt = Dep()
            self.hT = sb("hT", [128, NCH, 512], BF16); self.dhT = Dep()
            self.sq = [sb("sq", [128, 512], BF16) for _ in range(2)]; self.dsq = [Dep(), Dep()]
            self.tmp = [sb("tmp", [128, 512], F32) for _ in range(2)]; self.dtmp = [Dep(), Dep()]
            self.rs = sb("rs", [128, 512], F32); self.drs = Dep()

    def rstd(nb, x_of, dx, nfe, n, ps_i=7):
        ps = psum[ps_i]; dp = dps[ps_i]
        for c in range(nfe):
            s = c % 2
            k.op(k.act, lambda c=c, s=s: nc.scalar.activation(out=nb.sq[s][:, :n], in_=x_of(c), func=AF.Square),
                 reads=[dx], writes=[nb.dsq[s]])
            mm(ps[:, :n], ones_bf[:], nb.sq[s][:, :n], c == 0, c == nfe - 1, [d_const, nb.dsq[s]], [dp], last=True)
        k.op(k.act, lambda: nc.scalar.activation(out=nb.rs[:, :n], in_=ps[:, :n], func=AF.Ln, scale=1.0 / (nfe * 128), bias=eps_t[:, 0:1]),
             reads=[dp, d_const], writes=[nb.drs])
        k.op(k.act, lambda: nc.scalar.activation(out=nb.rs[:, :n], in_=nb.rs[:, :n], func=AF.Exp, scale=-0.5),
             reads=[nb.drs], writes=[nb.drs])

    def apply_norm(nb, x_of, dx, out_of, dout_, nfe, n, scale_ap, shift_ap, extra_reads=()):
        for c in range(nfe):
            s = c % 2
            k.op(k.dve, lambda c=c, s=s: nc.vector.tensor_tensor(out=nb.tmp[s][:, :n], in0=x_of(c), in1=nb.rs[:, :n], op=ALU.mult),
                 reads=[dx, nb.drs], writes=[nb.dtmp[s]])
            if shift_ap is not None:
                k.op(k.pool, lambda c=c, s=s: nc.gpsimd.tensor_scalar(out=out_of(c), in0=nb.tmp[s][:, :n], scalar1=scale_ap(c),
                                                                   scalar2=shift_ap(c), op0=ALU.mult, op1=ALU.add),
                     reads=[nb.dtmp[s], d_mod] + list(extra_reads), writes=[dout_])
            else:
                k.op(k.pool, lambda c=c, s=s: nc.gpsimd.tensor_scalar(out=out_of(c), in0=nb.tmp[s][:, :n], scalar1=scale_ap(c),
                                                                   scalar2=1.0, op0=ALU.mult, op1=ALU.mult),
                     reads=[nb.dtmp[s], d_mod] + list(extra_reads), writes=[dout_])

    eps_t = sb("eps_t", [128, 1], F32)
    k.op(k.pool, lambda: nc.gpsimd.memset(eps_t[:], EPS), writes=[d_const])

    def xsrc(first_layer):
        return I["xT"] if first_layer else xres

    def load_x_and_norm(nb, ti, l, first_layer, a_tile, shift_j):
        col0, n, seq = tiles[ti]
        k.dma(k.sp, nb.xt[:, :, :n], xsrc(first_layer)[:, col0:col0 + n].rearrange("(c p) t -> p c t", p=128),
              reads=[] if first_layer else [dxres[ti]], writes=[nb.dxt])
        rstd(nb, lambda c: nb.xt[:, c, :n], nb.dxt, NCH, n)
        apply_norm(nb, lambda c: nb.xt[:, c, :n], nb.dxt, lambda c: nb.hT[:, c, :n], nb.dhT, NCH, n,
                   lambda c: a_tile[:, l, c, seq:seq + 1], lambda c: mod[:, l, shift_j * 8 + c, seq:seq + 1])

    def phase_oproj(l, w_o_dram, first_layer, win, dwin):
        prev = cur[0]
        with ExitStack() as es:
            cur[0] = es
            wo = sb("wo", [128, NCH, D], BF16); dwo = Dep()
            stg, dstg = mkstg(2)
            load_w(stg, dstg, wo, dwo, w_o_dram, NCH, D)
            gen = load_w_gen(stg, dstg, win, dwin, I["w_ffn_in"][l], NCH, 2 * HID)
            at = [sb("oat", [128, NCH, 512], BF16) for i in range(2)]
            dat = [Dep() for _ in range(2)]
            xt = [sb("oxt", [128, NCH, 512], F32) for i in range(2)]
            dx = [Dep() for _ in range(2)]
            def ld_tile(ti):
                col0, n, seq = tiles[ti]; b = ti % 2
                k.dma(k.sp, at[b][:, :, :n], attnT[:, col0:col0 + n].rearrange("(c p) t -> p c t", p=128),
                      reads=[dattn], writes=[dat[b]])
                k.dma(k.sp, xt[b][:, :, :n], xsrc(first_layer)[:, col0:col0 + n].rearrange("(c p) t -> p c t", p=128),
                      reads=[] if first_layer else [dxres[ti]], writes=[dx[b]])

            ld_tile(0)
            for ti, (col0, n, seq) in enumerate(tiles):
                b = ti % 2
                if ti + 1 < len(tiles):
                    ld_tile(ti + 1)
                for m in range(NCH):
                    pi = m % 4; ps = psum[pi]; dp = dps[pi]
                    mmk(ps, dp, wo, m * 128, 128, lambda kk: at[b][:, kk, :n], NCH, n, [dwo, dat[b]])
                    k.op(k.dve, lambda m=m, ps=ps: nc.vector.scalar_tensor_tensor(
                        out=xt[b][:, m, :n], in0=ps[:, :n], scalar=mod[:, l, 16 + m, seq:seq + 1], in1=xt[b][:, m, :n],
                        op0=ALU.mult, op1=ALU.add), reads=[dp, d_mod, dx[b]], writes=[dx[b]])
                k.dma(STQ[0], xres[:, col0:col0 + n].rearrange("(c p) t -> p c t", p=128), xt[b][:, :, :n],
                      reads=[dx[b]], writes=[dxres[ti]])
                next(gen, None); next(gen, None)
            for _ in gen:
                pass
            k.barrier()
        cur[0] = prev

    def phase_ffn(l, final, win, dwin):
        prev = cur[0]
        with ExitStack() as es:
            cur[0] = es
            wout = sb("wout", [128, HCH, D], BF16); dwout = Dep()
            with ExitStack() as es2:
                cur[0] = es2
                stg, dstg = mkstg(2)
                load_w(stg, dstg, wout, dwout, I["w_ffn_out"][l], HCH, D)
                k.barrier()
            cur[0] = es
            hTs = [sb("hTf", [128, NCH, 512], BF16) for _ in range(2)]; dhTs = [Dep(), Dep()]
            sq = [sb("sqf", [128, 512], BF16) for _ in range(2)]; dsq = [Dep(), Dep()]
            tmp = [sb("tmpf", [128, 512], F32) for _ in range(2)]; dtmp = [Dep(), Dep()]
            rs = sb("rsf", [128, 512], F32); drs = Dep()
            NX = 4
            xc = [sb("xcf", [128, 512], F32) for _ in range(NX)]; dxc = [Dep() for _ in range(NX)]
            actT = sb("actT", [128, HCH, 512], BF16); dact = Dep()
            sg = [sb("sg", [128, 512], F32) for i in range(2)]
            dsg = [Dep() for _ in range(2)]
            xi = [0]

            def ldx(ti, c):
                col0, n, seq = tiles[ti]
                i = xi[0] % NX; xi[0] += 1
                k.dma(k.sp, xc[i][:, :n], xres[c * 128:(c + 1) * 128, col0:col0 + n], reads=[dxres[ti]], writes=[dxc[i]])
                return i

            def norm_tile(ti):
                col0, n, seq = tiles[ti]
                hT = hTs[ti % 2]; dhT = dhTs[ti % 2]
                ps = psum[7]; dp = dps[7]
                for c in range(NCH):
                    i = ldx(ti, c); s_ = c % 2
                    k.op(k.act, lambda i=i, s_=s_: nc.scalar.activation(out=sq[s_][:, :n], in_=xc[i][:, :n], func=AF.Square),
                         reads=[dxc[i]], writes=[dsq[s_]])
                    mm(ps[:, :n], ones_bf[:], sq[s_][:, :n], c == 0, c == NCH - 1, [d_const, dsq[s_]], [dp], last=True)
                k.op(k.act, lambda: nc.scalar.activation(out=rs[:, :n], in_=ps[:, :n], func=AF.Ln, scale=1.0 / D, bias=eps_t[:, 0:1]),
                     reads=[dp, d_const], writes=[drs])
                k.op(k.act, lambda: nc.scalar.activation(out=rs[:, :n], in_=rs[:, :n], func=AF.Exp, scale=-0.5), reads=[drs], writes=[drs])
                for c in range(NCH):
                    i = ldx(ti, c); s_ = c % 2
                    k.op(k.dve, lambda i=i, s_=s_: nc.vector.tensor_tensor(out=tmp[s_][:, :n], in0=xc[i][:, :n], in1=rs[:, :n], op=ALU.mult),
                         reads=[dxc[i], drs], writes=[dtmp[s_]])
                    k.op(k.pool, lambda c=c, s_=s_: nc.gpsimd.tensor_scalar(out=hT[:, c, :n], in0=tmp[s_][:, :n], scalar1=affn[:, l, c, seq:seq + 1],
                                                                     scalar2=mod[:, l, 24 + c, seq:seq + 1], op0=ALU.mult, op1=ALU.add),
                         reads=[dtmp[s_], d_mod], writes=[dhT])

            norm_tile(0)
            for ti, (col0, n, seq) in enumerate(tiles):
                hT = hTs[ti % 2]; dhT = dhTs[ti % 2]
                if ti + 1 < len(tiles):
                    norm_tile(ti + 1)
                for j in range(HCH):
                    pg = psum[(2 * j) % 6]; dg = dps[(2 * j) % 6]
                    pu = psum[(2 * j + 1) % 6]; du = dps[(2 * j + 1) % 6]
                    mmk(pg, dg, win, j * 128, 128, lambda kk: hT[:, kk, :n], NCH, n, [dwin, dhT])
                    mmk(pu, du, win, HID + j * 128, 128, lambda kk: hT[:, kk, :n], NCH, n, [dwin, dhT])
                    s = j % 2
                    k.op(k.act, lambda s=s, pg=pg: nc.scalar.activation(out=sg[s][:, :n], in_=pg[:, :n], func=AF.Silu),
                         reads=[dg], writes=[dsg[s]])
                    k.op(k.dve, lambda s=s, pu=pu, j=j: nc.vector.tensor_tensor(out=actT[:, j, :n], in0=pu[:, :n], in1=sg[s][:, :n], op=ALU.mult),
                         reads=[du, dsg[s]], writes=[dact])
                wdeps = []
                for m in range(NCH):
                    pi = 6 + (m % 2); ps = psum[pi]; dp = dps[pi]
                    for j in range(HCH):
                        mm(ps[:, :n], wout[:, j, m * 128:(m + 1) * 128], actT[:, j, :n], j == 0, j == HCH - 1,
                           [dwout, dact], [dp], last=(j == HCH - 1))
                    i = ldx(ti, m)
                    k.op(k.dve, lambda m=m, ps=ps, i=i: nc.vector.scalar_tensor_tensor(
                        out=xc[i][:, :n], in0=ps[:, :n], scalar=mod[:, l, 40 + m, seq:seq + 1], in1=xc[i][:, :n],
                        op0=ALU.mult, op1=ALU.add), reads=[dp, d_mod, dxc[i]], writes=[dxc[i]])
                    wd = Dep(); wdeps.append(wd)
                    k.dma(STQ[0], xres[m * 128:(m + 1) * 128, col0:col0 + n], xc[i][:, :n], reads=[dxc[i], dxres[ti]], writes=[wd])
                merged = Dep()
                for wd in wdeps:
                    if wd.w is not None:
                        merged.r[wd.w[0]] = (wd.w[1], wd.w[2])
                dxres_w[ti] = merged
            k.barrier()
        cur[0] = prev

    dxres_w = {}

    def phase_final():
        with ExitStack() as es:
            cur[0] = es
            nbs = [NormBufs(), NormBufs()]
            for ti, (col0, n, seq) in enumerate(tiles):
                nb = nbs[ti % 2]
                k.dma(k.sp, nb.xt[:, :, :n], xres[:, col0:col0 + n].rearrange("(c p) t -> p c t", p=128), reads=[dxres[ti]], writes=[nb.dxt])
                rstd(nb, lambda c: nb.xt[:, c, :n], nb.dxt, NCH, n)
                for c in range(NCH):
                    k.op(k.dve, lambda c=c: nc.vector.scalar_tensor_tensor(
                        out=nb.xt[:, c, :n], in0=nb.xt[:, c, :n], scalar=gfin[:, c:c + 1], in1=nb.rs[:, :n],
                        op0=ALU.mult, op1=ALU.mult), reads=[nb.dxt, nb.drs, d_mod], writes=[nb.dxt])
                k.dma(STQ[0], O["yT"][:, col0:col0 + n].rearrange("(c p) t -> p c t", p=128), nb.xt[:, :, :n], reads=[nb.dxt])
            k.barrier()
        cur[0] = None

    tri_bf = sb("tri_bf", [128, 128], BF16)
    anti_bf = sb("anti_bf", [128, 128], BF16)
    ident_bf = sb("ident_bf", [128, 128], BF16)

    def load_const_bf(dst, src_ap, shape):
        with ExitStack() as es:
            cur[0] = es
            t = sb("cst", shape, F32); dt_ = Dep()
            k.dma(k.sp, t[:], src_ap, writes=[dt_])
            k.op(k.dve, lambda: nc.vector.tensor_copy(out=dst, in_=t[:]), reads=[dt_], writes=[d_const])
            k.barrier()
        cur[0] = None

    load_const_bf(tri_bf[:], I["tri"], [128, 128])
    load_const_bf(anti_bf[:], I["antiI"], [128, 128])
    load_const_bf(ident_bf[:], I["ident"], [128, 128])

    def cbase(b):
        return TT + b * P

    def ingest_cache(kc_ap, vc_ap, Lc, F, krow0=0):
        nf = (F + 127) // 128
        kin = [sb("kin", [128, F], F32) for _ in range(2)]; dkin = [Dep(), Dep()]
        kto = [sb("kto", [128, nf, 128], BF16) for _ in range(2)]; dkto = [Dep(), Dep()]
        vin = [sb("vin", [128, F], F32) for _ in range(2)]; dvin = [Dep(), Dep()]
        vbo = [sb("vbo", [128, F], BF16) for _ in range(2)]; dvbo = [Dep(), Dep()]
        it = 0
        for b in range(2):
            for t0 in range(0, Lc, 128):
                nt = min(128, Lc - t0)
                s = it % 2; it += 1
                k.dma(k.sp, kin[s][:nt, :], kc_ap[b, t0:t0 + nt, :], writes=[dkin[s]])
                for c in range(nf):
                    fw = min(128, F - c * 128)
                    pi = c % 4; ps = psum[pi]; dp = dps[pi]
                    mm(ps[:fw, :nt], kin[s][:nt, c * 128:c * 128 + fw], ident[:nt, :nt], True, True,
                       [dkin[s], d_const], [dp])
                    k.op(k.act, lambda c=c, ps=ps, fw=fw: nc.scalar.activation(out=kto[s][:fw, c, :nt], in_=ps[:fw, :nt], func=AF.Copy),
                         reads=[dp], writes=[dkto[s]])
                if F >= 128:
                    k.dma(STQ[0], kT_s[krow0:krow0 + F, cbase(b) + t0:cbase(b) + t0 + nt].rearrange("(c p) t -> p c t", p=128),
                          kto[s][:, :, :nt], reads=[dkto[s]], writes=[dkk])
                else:
                    k.dma(STQ[0], kT_s[krow0:krow0 + F, cbase(b) + t0:cbase(b) + t0 + nt], kto[s][:F, 0, :nt],
                          reads=[dkto[s]], writes=[dkk])
                if vc_ap is not None:
                    k.dma(k.sp, vin[s][:nt, :], vc_ap[b, t0:t0 + nt, :], writes=[dvin[s]])
                    cast(vbo[s][:nt, :], vin[s][:nt, :], [dvin[s]], [dvbo[s]])
                    k.dma(STQ[0], v_s[cbase(b) + t0:cbase(b) + t0 + nt, 0:F], vbo[s][:nt, :], reads=[dvbo[s]], writes=[dv])

    def phase_qkv(l, mix, first_layer):
        wq = {1: "w_b_qkv", 2: "w_c_qkv", 3: "w_d_qkv"}[mix]
        NQ = 1024
        NKF = 256 if mix == 2 else 1024
        NW = NQ + 2 * NKF
        nkc = NKF // 128
        Lc = LC[mix]
        with ExitStack() as es:
            cur[0] = es
            w = sb("wqkv", [128, NCH, NW], BF16); dw = Dep()
            with ExitStack() as es2:
                cur[0] = es2
                stg, dstg = mkstg(2)
                load_w(stg, dstg, w, dw, I[wq], NCH, NW)
                kc = {1: "cache_b_k", 2: "cache_c_k", 3: "cache_d_k"}[mix]
                vc = {1: "cache_b_v", 2: "cache_c_v", 3: "cache_d_v"}[mix]
                ingest_cache(I[kc], I[vc], Lc, NKF)
                k.barrier()
            cur[0] = es
            if mix == 2:
                wsw = sb("wsw", [128, NCH, NQ + NKF], BF16)
                wv = w[:, :, 0:NQ + NKF].rearrange("p k (h d) -> p k h d", d=64)
                sv = wsw[:, :, :].rearrange("p k (h d) -> p k h d", d=64)
                for kk in range(NCH):
                    k.op(k.pool, lambda kk=kk: nc.gpsimd.tensor_copy(out=sv[:, kk, :, 16:64], in_=wv[:, kk, :, 16:64]), reads=[dw], writes=[dw])
                    k.op(k.pool, lambda kk=kk: nc.gpsimd.tensor_copy(out=sv[:, kk, :, 0:8], in_=wv[:, kk, :, 8:16]), reads=[dw], writes=[dw])
                    k.op(k.pool, lambda kk=kk: nc.gpsimd.tensor_copy(out=sv[:, kk, :, 8:16], in_=wv[:, kk, :, 0:8]), reads=[dw], writes=[dw])
                bc = sb("bc", [128, 12], F32); bcs = sb("bcs", [128, 12], F32)
                brow = sb("brow", [128, 256], F32)
                rc = sb("rc", [128, TT], F32); rsn = sb("rsn", [128, TT], F32)
                k.dma(k.sp, bc[:], I["bcT"], writes=[dw])
                k.dma(k.sp, bcs[:], I["bcswT"], writes=[dw])
                k.dma(k.sp, brow[:], I["bc_row"][0:1, 1280:1536].partition_broadcast(128), writes=[dw])
                k.dma(k.sp, rc[:], I["ropeC_cos"], writes=[dw])
                k.dma(k.sp, rsn[:], I["ropeC_sin"], writes=[dw])
            nbs = [NormBufs(), NormBufs()]
            qo = sb("qo", [128, NCH, 512], BF16); dqo = Dep()
            ko = sb("ko", [128, nkc, 512], BF16); dko = Dep()
            kf = sb("kf", [128, nkc, 512], F32); dkf = Dep()
            t1 = [sb("t1", [128, 512], F32) for _ in range(2)]; dt1 = [Dep(), Dep()]
            t2 = [sb("t2", [128, 512], F32) for _ in range(2)]; dt2 = [Dep(), Dep()]
            ktm = [sb("ktm", [128, NKF], F32) for _ in range(2)]; dktm = [Dep(), Dep()]
            vtm = [sb("vtm", [128, NKF], F32) for _ in range(2)]; dvtm = [Dep(), Dep()]
            vtb = [sb("vtb", [128, NKF], BF16) for _ in range(2)]; dvtb = [Dep(), Dep()]
            keep = {1: S, 2: min(128, S), 3: min(512, S)}[mix]
            ko_name = {1: "b_k", 2: "c_k", 3: "d_k"}[mix]
            vo_name = {1: "b_v", 2: "c_v", 3: "d_v"}[mix]
            it = 0
            pr = 0
            load_x_and_norm(nbs[0], 0, l, first_layer, amix, 0)
            for ti, (col0, n, seq) in enumerate(tiles):
                nb = nbs[ti % 2]
                if ti + 1 < len(tiles):
                    load_x_and_norm(nbs[(ti + 1) % 2], ti + 1, l, first_layer, amix, 0)
                hrhs = lambda kk, nb=nb, n=n: nb.hT[:, kk, :n]
                for m in range(NCH + nkc):
                    isq = m < NCH
                    wcol = m * 128 if isq else NQ + (m - NCH) * 128
                    mi = m if isq else m - NCH
                    pi = pr % 6; pr += 1; ps = psum[pi]; dp = dps[pi]
                    mmk(ps, dp, w, wcol, 128, hrhs, NCH, n, [dw, nb.dhT])
                    dst_bf = qo[:, mi, :n] if isq else ko[:, mi, :n]
                    ddst = dqo if isq else dko
                    if mix != 2:
                        k.op(k.act, lambda ps=ps, dst_bf=dst_bf: nc.scalar.activation(out=dst_bf, in_=ps[:, :n], func=AF.Copy),
                             reads=[dp], writes=[ddst])
                        if not isq:
                            k.op(k.act, lambda ps=ps, mi=mi: nc.scalar.activation(out=kf[:, mi, :n], in_=ps[:, :n], func=AF.Copy),
                                 reads=[dp], writes=[dkf])
                    else:
                        pi2 = pr % 6; pr += 1; ps2 = psum[pi2]; dp2 = dps[pi2]
                        wc2 = m * 128
                        mmk(ps2, dp2, wsw, wc2, 128, hrhs, NCH, n, [dw, nb.dhT])
                        bcol = m
                        s_ = it % 2; it += 1
                        k.op(k.dve, lambda ps=ps, s_=s_, bcol=bcol: nc.vector.scalar_tensor_tensor(
                            out=t1[s_][:, :n], in0=ps[:, :n], scalar=bc[:, bcol:bcol + 1], in1=rc[:, col0:col0 + n],
                            op0=ALU.add, op1=ALU.mult), reads=[dp, dw], writes=[dt1[s_]])
                        k.op(k.dve, lambda ps2=ps2, s_=s_, bcol=bcol: nc.vector.scalar_tensor_tensor(
                            out=t2[s_][:, :n], in0=ps2[:, :n], scalar=bcs[:, bcol:bcol + 1], in1=rsn[:, col0:col0 + n],
                            op0=ALU.add, op1=ALU.mult), reads=[dp2, dw], writes=[dt2[s_]])
                        if isq:
                            k.op(k.pool, lambda s_=s_, dst_bf=dst_bf: nc.gpsimd.tensor_tensor(out=dst_bf, in0=t1[s_][:, :n], in1=t2[s_][:, :n], op=ALU.add),
                                 reads=[dt1[s_], dt2[s_]], writes=[ddst])
                        else:
                            k.op(k.pool, lambda s_=s_, mi=mi: nc.gpsimd.tensor_tensor(out=kf[:, mi, :n], in0=t1[s_][:, :n], in1=t2[s_][:, :n], op=ALU.add),
                                 reads=[dt1[s_], dt2[s_]], writes=[dkf])
                            k.op(k.pool, lambda mi=mi, dst_bf=dst_bf: nc.gpsimd.tensor_copy(out=dst_bf, in_=kf[:, mi, :n]),
                                 reads=[dkf], writes=[ddst])
                k.dma(STQ[0], qT_s[0:NQ, col0:col0 + n].rearrange("(c p) t -> p c t", p=128), qo[:, :, :n], reads=[dqo], writes=[dq])
                k.dma(STQ[0], kT_s[0:NKF, col0:col0 + n].rearrange("(c p) t -> p c t", p=128), ko[:, :, :n], reads=[dko], writes=[dkk])
                for s0 in range(0, n, 128):
                    nt = min(128, n - s0)
                    tok0 = col0 + s0
                    b_ = it % 2; it += 1
                    if seq == 0:
                        lo = S - keep
                        want = tok0 >= lo
                    else:
                        want = True
                    if want:
                        for c in range(nkc):
                            pi = pr % 6; pr += 1; ps = psum[pi]; dp = dps[pi]
                            mm(ps[:nt, :128], kf[:, c, s0:s0 + nt], ident[:, :], True, True, [dkf, d_const], [dp])
                            k.op(k.act, lambda ps=ps, c=c, b_=b_: nc.scalar.activation(out=ktm[b_][:nt, c * 128:(c + 1) * 128], in_=ps[:nt, :128], func=AF.Copy),
                                 reads=[dp], writes=[dktm[b_]])
                    for fb in range(0, NKF, 512):
                        fw = min(512, NKF - fb)
                        pi = pr % 6; pr += 1; ps = psum[pi]; dp = dps[pi]
                        for kk in range(NCH):
                            mm(ps[:nt, :fw], nb.hT[:, kk, s0:s0 + nt], w[:, kk, NQ + NKF + fb:NQ + NKF + fb + fw],
                               kk == 0, kk == NCH - 1, [dw, nb.dhT], [dp], last=(kk == NCH - 1))
                        if mix == 2:
                            k.op(k.dve, lambda ps=ps, b_=b_, fb=fb, fw=fw: nc.vector.tensor_tensor(out=vtm[b_][:nt, fb:fb + fw], in0=ps[:nt, :fw], in1=brow[:nt, fb:fb + fw], op=ALU.add),
                                 reads=[dp, dw], writes=[dvtm[b_]])
                        else:
                            k.op(k.act, lambda ps=ps, b_=b_, fb=fb, fw=fw: nc.scalar.activation(out=vtm[b_][:nt, fb:fb + fw], in_=ps[:nt, :fw], func=AF.Copy),
                                 reads=[dp], writes=[dvtm[b_]])
                    k.op(k.dve, lambda b_=b_: nc.vector.tensor_copy(out=vtb[b_][:nt, :], in_=vtm[b_][:nt, :]), reads=[dvtm[b_]], writes=[dvtb[b_]])
                    k.dma(STQ[0], v_s[tok0:tok0 + nt, 0:NKF], vtb[b_][:nt, :], reads=[dvtb[b_]], writes=[dv])
                    if want:
                        if mix == 1:
                            ka = O["b_k"][tok0:tok0 + nt, :]; va = O["b_v"][tok0:tok0 + nt, :]
                        elif seq == 0:
                            ka = O[ko_name + "_p"][tok0 - lo:tok0 - lo + nt, :]; va = O[vo_name + "_p"][tok0 - lo:tok0 - lo + nt, :]
                        else:
                            ka = O[ko_name + "_s"][seq - 1, Lc - T_S:Lc, :]; va = O[vo_name + "_s"][seq - 1, Lc - T_S:Lc, :]
                        k.dma(STQ[0], ka, ktm[b_][:nt, :], reads=[dktm[b_]])
                        k.dma(STQ[0], va, vtm[b_][:nt, :], reads=[dvtm[b_]])
            if mix != 1:
                for b in range(2):
                    k.dma(k.sp, O[ko_name + "_s"][b, 0:Lc - T_S, :], I[kc][b, T_S:Lc, :])
                    k.dma(k.sp, O[vo_name + "_s"][b, 0:Lc - T_S, :], I[vc][b, T_S:Lc, :])
            k.barrier()
        cur[0] = None

    class AttnBufs:
        def __init__(self, Lc, vw, nh=2):
            self.Lc = Lc
            self.ncs = (Lc + 127) // 128
            self.nslots = S // 128 + 2 * (self.ncs + 1)
            self.Kf = sb("Kf", [128, TT + 2 * Lc], BF16); self.dK = Dep()
            self.Qf = sb("Qf", [128, TT], BF16); self.dQ = Dep()
            self.Qz = [self.Qf, sb("Qf1", [128, TT], BF16)] if nh == 2 else [self.Qf]
            self.Va = sb("Va", [128, self.nslots, nh, vw], BF16); self.dV = Dep()

        def zero_q(self):
            k.op(k.pool, lambda: nc.gpsimd.memset(self.Qz[0][64:128, :], 0.0), writes=[self.dQ])
            k.op(k.pool, lambda: nc.gpsimd.memset(self.Qz[1][0:64, :], 0.0), writes=[self.dQ])

        def load_q_pair(self, u):
            k.dma(k.sp, self.Qz[0][0:64, :], qT_s[u * 128:u * 128 + 64, 0:TT], reads=[dq], writes=[self.dQ])
            k.dma(k.sp, self.Qz[1][64:128, :], qT_s[u * 128 + 64:(u + 1) * 128, 0:TT], reads=[dq], writes=[self.dQ])

        def kcol(self, b, j=0):
            return TT + b * self.Lc + j

        def slot_cache(self, b, t):
            return S // 128 + b * (self.ncs + 1) + t

        def slot_new(self, b):
            return S // 128 + b * (self.ncs + 1) + self.ncs

    def load_K(ab, rows_dst, krow0, nrows):
        Lc = ab.Lc
        k.dma(k.sp, ab.Kf[rows_dst:rows_dst + nrows, 0:TT], kT_s[krow0:krow0 + nrows, 0:TT], reads=[dkk], writes=[ab.dK])
        for b in range(2):
            k.dma(k.sp, ab.Kf[rows_dst:rows_dst + nrows, ab.kcol(b):ab.kcol(b) + Lc],
                  kT_s[krow0:krow0 + nrows, cbase(b):cbase(b) + Lc], reads=[dkk], writes=[ab.dK])

    def load_V(ab, hsel, vcol0):
        Lc = ab.Lc
        for t0 in range(0, S // 128, 8):
            t1_ = min(S // 128, t0 + 8)
            k.dma(k.sp, ab.Va[:, t0:t1_, hsel, 0:64], v_s[t0 * 128:t1_ * 128, vcol0:vcol0 + 64].rearrange("(t p) d -> p t d", p=128),
                  reads=[dv], writes=[ab.dV])
        for b in range(2):
            for t0 in range(0, ab.ncs, 8):
                t1_ = min(ab.ncs, t0 + 8)
                k.dma(k.sp, ab.Va[:, ab.slot_cache(b, t0):ab.slot_cache(b, t0) + (t1_ - t0), hsel, 0:64],
                      v_s[cbase(b) + t0 * 128:cbase(b) + t1_ * 128, vcol0:vcol0 + 64].rearrange("(t p) d -> p t d", p=128), reads=[dv], writes=[ab.dV])
            k.dma(k.sp, ab.Va[0:T_S, ab.slot_new(b), hsel, 0:64], v_s[S + b * T_S:S + (b + 1) * T_S, vcol0:vcol0 + 64],
                  reads=[dv], writes=[ab.dV])

    def load_masks(name, nm):
        mt = sb("mask", [128, nm, 512], BF16); dm = Dep()
        with ExitStack() as es2:
            old = cur[0]; cur[0] = es2
            st = sb("mstg", [128, 512], F32); dst_ = Dep()
            for i in range(nm):
                k.dma(k.sp, st[:], I[name][i], writes=[dst_])
                k.op(k.dve, lambda i=i: nc.vector.tensor_copy(out=mt[:, i, :], in_=st[:]), reads=[dst_], writes=[dm])
            k.barrier()
            cur[0] = old
        return mt, dm

    cnt = {"s": 0, "o": 0, "p": 0}

    def phase_attn_softmax(mix, lnext):
        Lc = LC[mix]
        kdim = 96 if mix == 0 else 64
        scale = float(kdim) ** -0.5
        OFFE = 511
        STQ[0] = k.sp
        with ExitStack() as es:
            cur[0] = es
            if mix == 2:
                mt, dm = None, Dep()
            else:
                mname, nm = {0: ("maskA", 4), 3: ("maskD", 8)}[mix]
                mt, dm = load_masks(mname, nm)
            abs_ = [AttnBufs(Lc, 128, 1 if mix == 0 else 2) for _ in range(2)]
            for ab in abs_:
                k.op(k.pool, lambda ab=ab: nc.gpsimd.memset(ab.Va[:, :, :, 64:128], 1.0), writes=[ab.dV])
                if mix == 3:
                    ab.zero_q()
            if mix == 2:
                with ExitStack() as es2:
                    cur[0] = es2
                    su = sb("stgU", [128, TT + 2 * Lc], F32); dsu = Dep()
                    sw = sb("stgW", [128, TT], F32); dsw = Dep()
                    k.dma(k.sp, su[64:128, :], I["chunkU"], writes=[dsu])
                    k.dma(k.sp, sw[64:128, :], I["chunkW"], writes=[dsw])
                    for ab in abs_:
                        k.op(k.dve, lambda ab=ab: nc.vector.tensor_copy(out=ab.Kf[64:128, :], in_=su[64:128, :]), reads=[dsu], writes=[ab.dK])
                        for qz in ab.Qz:
                            k.op(k.dve, lambda qz=qz: nc.vector.tensor_copy(out=qz[64:128, :], in_=sw[64:128, :]), reads=[dsw], writes=[ab.dQ])
                    k.barrier()
                cur[0] = es
            pt = [sb("pt", [128, 512], BF16) for _ in range(6)]; dpt = [Dep() for _ in range(6)]
            if mix == 2:
                esink = sb("esink", [128, 16], F32); des = Dep()
                k.dma(k.sp, esink[:], I["sink_c"][0:1, :].partition_broadcast(128), writes=[des])
                k.op(k.act, lambda: nc.scalar.activation(out=esink[:], in_=esink[:], func=AF.Exp), reads=[des], writes=[des])
            if mix == 3:
                et = sb("ext", [16, 1536], F32); det = Dep()
                k.op(k.pool, lambda: nc.gpsimd.memset(et[:], 0.0), writes=[det])
                k.dma(k.sp, et[:, OFFE - 128:OFFE + 129], I["rel_bias_d"], writes=[det])
                k.op(k.dve, lambda: nc.vector.tensor_scalar(out=et[:, 0:OFFE - 128], in0=et[:, 0:OFFE - 128], scalar1=et[:, OFFE - 128:OFFE - 127],
                                                          scalar2=None, op0=ALU.add), reads=[det], writes=[det])
                k.op(k.dve, lambda: nc.vector.tensor_scalar(out=et[:, OFFE + 129:1536], in0=et[:, OFFE + 129:1536], scalar1=et[:, OFFE + 128:OFFE + 129],
                                                          scalar2=None, op0=ALU.add), reads=[det], writes=[det])
                dext = Dep()
                k.dma(STQ[0], ext_s, et[:], reads=[det], writes=[dext])
                Hst = [sb("Hst", [128, 512], F32) for _ in range(4)]; dHst = [Dep() for _ in range(4)]
                Hb = [sb("Hb", [128, 8 + Lc // 128 + 1, 512], BF16) for _ in range(2)]; dHb = [Dep(), Dep()]
                anti32 = sb("anti32", [32, 32], BF16)
                a32s = sb("a32s", [32, 32], F32); da32 = Dep()
                k.dma(k.sp, a32s[:], I["antiI"][96:128, 0:32], writes=[da32])
                k.op(k.dve, lambda: nc.vector.tensor_copy(out=anti32[:], in_=a32s[:]), reads=[da32], writes=[d_const])
            units = 16 if mix == 0 else 8
            hst_i = [0]

            def load_unit(u, ab):
                if mix == 0:
                    load_K(ab, 0, u * 64, 64)
                    load_K(ab, 64, D, 32)
                    k.dma(k.sp, ab.Qf[0:96, :], qT_s[u * 96:(u + 1) * 96, 0:TT], reads=[dq], writes=[ab.dQ])
                    load_V(ab, 0, u * 64)
                else:
                    if mix == 2:
                        g = u // 2
                        load_K(ab, 0, g * 64, 64)
                        load_V(ab, 0, g * 64)
                        k.dma(k.sp, ab.Qz[0][0:64, :], qT_s[u * 128:u * 128 + 64, 0:TT], reads=[dq], writes=[ab.dQ])
                        k.dma(k.sp, ab.Qz[1][0:64, :], qT_s[u * 128 + 64:(u + 1) * 128, 0:TT], reads=[dq], writes=[ab.dQ])
                    else:
                        load_K(ab, 0, u * 128, 128)
                        load_V(ab, 0, u * 128)
                        load_V(ab, 1, u * 128 + 64)
                        ab.load_q_pair(u)

            def load_H(h, hb_i):
                specs = [(8 + 0, 0, 0, 0)]
                specs = []
                for r in range(-4, 4):
                    specs.append((r + 4, OFFE - 127 - 128 * r, 128, 512))
                for kt in range(Lc // 128):
                    specs.append((8 + kt, OFFE + Lc - 128 * kt - 127, 128, T_S))
                specs.append((8 + Lc // 128, OFFE - 31, 32, T_S))
                for (slot, base, nr, ncol) in specs:
                    s_ = hst_i[0] % 4; hst_i[0] += 1
                    src = bass.AP(ext_s.tensor, h * 1536 + base, [[1, nr], [1, ncol]])
                    k.dma(k.sp, Hst[s_][:nr, :ncol], src, reads=[dext], writes=[dHst[s_]])
                    if slot < 8:
                        k.op(k.dve, lambda s_=s_, slot=slot, nr=nr, ncol=ncol: nc.vector.scalar_tensor_tensor(
                            out=Hb[hb_i][:nr, slot, :ncol], in0=Hst[s_][:nr, :ncol], scalar=8.0, in1=mt[:nr, slot, :ncol], op0=ALU.mult, op1=ALU.add),
                            reads=[dHst[s_], dm], writes=[dHb[hb_i]])
                    else:
                        k.op(k.dve, lambda s_=s_, slot=slot, nr=nr, ncol=ncol: nc.vector.tensor_scalar(
                            out=Hb[hb_i][:nr, slot, :ncol], in0=Hst[s_][:nr, :ncol], scalar1=8.0, scalar2=None, op0=ALU.mult),
                            reads=[dHst[s_]], writes=[dHb[hb_i]])

            obanks = {0: [4, 5], 2: [4, 5, 7], 3: [4, 5, 6, 7]}[mix]
            NO = len(obanks)
            rden = [sb("rden", [128, 512], F32) for _ in range(NO)]; drd = [Dep() for _ in range(NO)]
            ot = [sb("ot", [128, 512], BF16) for _ in range(NO)]; dot_ = [Dep() for _ in range(NO)]
            pend = []
            SK = 3

            def push(fA, fP):
                fA()
                pend.append(fP)
                if len(pend) > SK:
                    pend.pop(0)()

            pstore = []

            def flush():
                while pend:
                    pend.pop(0)()
                while pstore:
                    pstore.pop(0)()

            def run_q(ab, pbase, hsel, h, qcol0, nq, blocks, hb_i):
                oc = cnt["o"]; cnt["o"] += 1
                ri = oc % NO
                psO = psum[obanks[ri]]; dO = dps[obanks[ri]]
                nb_ = len(blocks)

                def finalize():
                    if mix == 2:
                        k.op(k.act, lambda: nc.scalar.activation(out=rden[ri][64:128, :nq], in_=psO[64:128, :nq], func=AF.Ln, bias=esink[64:128, h:h + 1]),
                             reads=[dO, des], writes=[drd[ri]])
                        k.op(k.act, lambda: nc.scalar.activation(out=rden[ri][64:128, :nq], in_=rden[ri][64:128, :nq], func=AF.Exp, scale=-1.0),
                             reads=[drd[ri]], writes=[drd[ri]])
                    elif mix == 3:
                        k.op(k.act, lambda: nc.scalar.activation(out=rden[ri][64:128, :nq], in_=psO[64:128, :nq], func=AF.Ln),
                             reads=[dO], writes=[drd[ri]])
                        k.op(k.act, lambda: nc.scalar.activation(out=rden[ri][64:128, :nq], in_=rden[ri][64:128, :nq], func=AF.Exp, scale=-1.0),
                             reads=[drd[ri]], writes=[drd[ri]])
                    else:
                        k.op(k.dve, lambda: nc.vector.reciprocal(out=rden[ri][64:128, :nq], in_=psO[64:128, :nq]), reads=[dO], writes=[drd[ri]])
                    k.op(k.dve, lambda: nc.vector.tensor_tensor(out=ot[ri][0:64, :nq], in0=psO[0:64, :nq], in1=rden[ri][64:128, :nq], op=ALU.mult),
                         reads=[dO, drd[ri]], writes=[dot_[ri]])
                    while pstore:
                        pstore.pop(0)()
                    pstore.append(lambda: k.dma(STQ[0], attnT[h * 64:(h + 1) * 64, qcol0:qcol0 + nq], ot[ri][0:64, :nq], reads=[dot_[ri]], writes=[dattn]))

                for j, (kcol, nk, slot, mask, hslot) in enumerate(blocks):
                    st = {}

                    def fA(kcol=kcol, nk=nk, mask=mask, hslot=hslot, st=st):
                        si = cnt["s"] % 4; cnt["s"] += 1
                        psS = psum[si]; dS = dps[si]
                        more = (hslot is not None) or (mask is not None)
                        if mix == 0:
                            mm(psS[:nk, :nq], ab.Kf[0:96, kcol:kcol + nk], ab.Qf[0:96, qcol0:qcol0 + nq],
                               True, not more, [ab.dK, ab.dQ], [dS], last=(not more))
                        else:
                            mm(psS[:nk, :nq], ab.Kf[:, kcol:kcol + nk], ab.Qz[pbase // 64][:, qcol0:qcol0 + nq],
                               True, not more, [ab.dK, ab.dQ], [dS], last=(not more))
                        if hslot is not None:
                            al = anti_bf[:, :] if nk == 128 else anti32[:, :]
                            mm(psS[:nk, :nq], al, Hb[hb_i][:nk, hslot, :nq], False, True, [d_const, dHb[hb_i]], [dS])
                        elif mask is not None:
                            mm(psS[:nk, :nq], ident_bf[:nk, :nk], mt[:nk, mask, :nq], False, True, [d_const, dm], [dS])
                        pi = cnt["p"] % 6; cnt["p"] += 1
                        k.op(k.act, lambda: nc.scalar.activation(out=pt[pi][:nk, :nq], in_=psS[:nk, :nq], func=AF.Exp, scale=scale),
                             reads=[dS], writes=[dpt[pi]])
                        st["pi"] = pi

                    def fP(j=j, nk=nk, slot=slot, st=st):
                        pi = st["pi"]
                        mm(psO[:, :nq], ab.Va[:nk, slot, hsel, :], pt[pi][:nk, :nq], j == 0, j == nb_ - 1, [ab.dV, dpt[pi]], [dO],
                           last=(j == nb_ - 1))
                        if j == nb_ - 1:
                            finalize()

                    push(fA, fP)

            nper = 2
            bg = AdaBg({0: [1, 2], 2: [3]}.get(mix, []), 4, [6, 7])
            if mix == 3:
                load_H(0, 0)
            load_unit(0, abs_[0])
            bg.load(nper)
            for u in range(units):
                ab = abs_[u % 2]
                if u + 1 < units:
                    flush()
                    load_unit(u + 1, abs_[(u + 1) % 2])
                bg.step(nper)
                heads = [(0, 0, u)] if mix == 0 else [(0, 0, 2 * u), (64, 0 if mix == 2 else 1, 2 * u + 1)]
                for (pbase, hsel, h) in heads:
                    hb_i = h % 2
                    if mix == 3 and h + 1 < 16:
                        load_H(h + 1, (h + 1) % 2)
                    for qi in range(NPT):
                        blocks = []
                        if mix == 0:
                            for kt in range(0, 4 * qi + 4):
                                blocks.append((kt * 128, 128, kt, (kt - 4 * qi) if kt >= 4 * qi else None, None))
                        elif mix == 2:
                            for kt in range(max(0, 4 * qi - 1), 4 * qi + 4):
                                blocks.append((kt * 128, 128, kt, None, None))
                        else:
                            for kt in range(max(0, 4 * qi - 4), 4 * qi + 4):
                                blocks.append((kt * 128, 128, kt, None, kt - 4 * qi + 4))
                        run_q(ab, pbase, hsel, h, qi * 512, 512, blocks, hb_i)
                    for b in range(2):
                        blocks = []
                        for t in range(ab.ncs):
                            blocks.append((ab.kcol(b, t * 128), min(128, Lc - t * 128), ab.slot_cache(b, t), None, (8 + t) if mix == 3 else None))
                        blocks.append((S + b * T_S, T_S, ab.slot_new(b), None, (8 + Lc // 128) if mix == 3 else None))
                        run_q(ab, pbase, hsel, h, S + b * T_S, T_S, blocks, hb_i)
            flush()
            bg.finish()
            k.barrier()
        cur[0] = None
        STQ[0] = k.act

    def phase_proj_a(l, first_layer):
        with ExitStack() as es:
            cur[0] = es
            wdn = sb("wdn", [128, NCH, 672], BF16); dw = Dep()
            wkr = sb("wkr", [128, NCH, 96], BF16)
            wkrs = sb("wkrs", [128, NCH, 96], BF16)
            wuq = sb("wuq", [128, 3, 1536], BF16)
            wuqs = sb("wuqs", [128, 3, 1536], BF16)
            wuk = sb("wuk", [128, 2, D], BF16)
            wuv = sb("wuv", [128, 2, D], BF16)
            gq = sb("gq", [128, 3], F32); gkv = sb("gkv", [128, 2], F32)
            rc = sb("rc", [128, TT], F32); rsn = sb("rsn", [128, TT], F32)
            k.dma(k.sp, gq[:], I["gaqT"], writes=[d_mod])
            k.dma(k.sp, gkv[:], I["gakvT"], writes=[d_mod])
            k.dma(k.sp, rc[:], I["ropeA_cos"], writes=[dw])
            k.dma(k.sp, rsn[:], I["ropeA_sin"], writes=[dw])
            nb = NormBufs()
            cknb_c = [sb("cknc", [128, 2, 128], BF16) for _ in range(2)]; dcknc = [Dep(), Dep()]
            with ExitStack() as es2:
                cur[0] = es2
                stg, dstg = mkstg(2)
                load_w(stg, dstg, wdn, dw, I["w_a_down"], NCH, 672)
                load_w(stg, dstg, wuq, dw, I["w_a_uq"], 3, 1536)
                load_w(stg, dstg, wuk, dw, I["w_a_uk"], 2, D)
                load_w(stg, dstg, wuv, dw, I["w_a_uv"], 2, D)
                k.op(k.pool, lambda: nc.gpsimd.memset(wkr[:], 0.0), writes=[dw])
                k.op(k.pool, lambda: nc.gpsimd.memset(wkrs[:], 0.0), writes=[dw])
                k.op(k.pool, lambda: nc.gpsimd.tensor_copy(out=wkr[:, :, 64:96], in_=wdn[:, :, 640:672]), reads=[dw], writes=[dw])
                k.op(k.pool, lambda: nc.gpsimd.tensor_copy(out=wkrs[:, :, 64:80], in_=wdn[:, :, 656:672]), reads=[dw], writes=[dw])
                k.op(k.pool, lambda: nc.gpsimd.tensor_copy(out=wkrs[:, :, 80:96], in_=wdn[:, :, 640:656]), reads=[dw], writes=[dw])
                qv = wuq[:, :, :].rearrange("p k (h d) -> p k h d", d=96)
                sv = wuqs[:, :, :].rearrange("p k (h d) -> p k h d", d=96)
                for kk in range(3):
                    k.op(k.pool, lambda kk=kk: nc.gpsimd.tensor_copy(out=sv[:, kk, :, 0:64], in_=qv[:, kk, :, 0:64]), reads=[dw], writes=[dw])
                    k.op(k.pool, lambda kk=kk: nc.gpsimd.tensor_copy(out=sv[:, kk, :, 64:80], in_=qv[:, kk, :, 80:96]), reads=[dw], writes=[dw])
                    k.op(k.pool, lambda kk=kk: nc.gpsimd.tensor_copy(out=sv[:, kk, :, 80:96], in_=qv[:, kk, :, 64:80]), reads=[dw], writes=[dw])
                ingest_cache(I["cache_a_kr"], None, P, 32, krow0=D)
                cin = [sb("cin", [128, 256], F32) for _ in range(2)]; dcin = [Dep(), Dep()]
                kto = [sb("ktoa", [128, NCH, 128], BF16) for _ in range(2)]; dkto = [Dep(), Dep()]
                vbo = [sb("vboa", [128, D], BF16) for _ in range(2)]; dvbo = [Dep(), Dep()]
                it = 0
                for b in range(2):
                    for t0 in range(0, P, 128):
                        s = it % 2; it += 1
                        k.dma(k.sp, cin[s][:, :], I["cache_a_ckv"][b, t0:t0 + 128, :], writes=[dcin[s]])
                        for c in range(2):
                            ps = psum[c]; dp = dps[c]
                            mm(ps[:, :128], cin[s][:, c * 128:(c + 1) * 128], ident[:, :], True, True, [dcin[s], d_const], [dp])
                            k.op(k.act, lambda c=c, ps=ps, s=s: nc.scalar.activation(out=cknb_c[s][:, c, :], in_=ps[:, :128], func=AF.Copy),
                                 reads=[dp], writes=[dcknc[s]])
                        for m in range(NCH):
                            pi = 2 + m % 2; ps = psum[pi]; dp = dps[pi]
                            mmk(ps, dp, wuk, m * 128, 128, lambda kk: cknb_c[s][:, kk, :], 2, 128, [dw, dcknc[s]])
                            k.op(k.act, lambda m=m, ps=ps, s=s: nc.scalar.activation(out=kto[s][:, m, :], in_=ps[:, :128], func=AF.Copy),
                                 reads=[dp], writes=[dkto[s]])
                        k.dma(STQ[0], kT_s[0:D, cbase(b) + t0:cbase(b) + t0 + 128].rearrange("(c p) t -> p c t", p=128), kto[s][:, :, :],
                              reads=[dkto[s]], writes=[dkk])
                        for fb in range(2):
                            pi = 4 + fb; ps = psum[pi]; dp = dps[pi]
                            for c in range(2):
                                mm(ps[:, :512], cknb_c[s][:, c, :], wuv[:, c, fb * 512:(fb + 1) * 512], c == 0, c == 1, [dw, dcknc[s]], [dp], last=(c == 1))
                            k.op(k.dve, lambda fb=fb, ps=ps, s=s: nc.vector.tensor_copy(out=vbo[s][:, fb * 512:(fb + 1) * 512], in_=ps[:, :512]),
                                 reads=[dp], writes=[dvbo[s]])
                        k.dma(STQ[0], v_s[cbase(b) + t0:cbase(b) + t0 + 128, :], vbo[s][:, :], reads=[dvbo[s]], writes=[dv])
                k.barrier()
            cur[0] = es
            cqf = sb("cqf", [128, 3, 512], F32); dcqf = Dep()
            cqn = sb("cqn", [128, 3, 512], BF16); dcqn = Dep()
            ckf = sb("ckf", [128, 2, 512], F32); dckf = Dep()
            ckn32 = sb("ckn32", [128, 2, 512], F32); dckn32 = Dep()
            cknb = sb("cknb", [128, 2, 512], BF16); dcknb = Dep()
            t1 = [sb("t1", [128, 512], F32) for _ in range(2)]; dt1 = [Dep(), Dep()]
            t2 = [sb("t2", [128, 512], F32) for _ in range(2)]; dt2 = [Dep(), Dep()]
            krf = sb("krf", [128, 512], F32); dkrf = Dep()
            krb = sb("krb", [128, 512], BF16); dkrb = Dep()
            qall = sb("qall", [128, 16, 512], BF16); dqall = Dep()
            ko = sb("koa", [128, NCH, 512], BF16); dko = Dep()
            ctm = [sb("ctm", [128, 256], F32) for _ in range(2)]; dctm = [Dep(), Dep()]
            krtm = [sb("krtm", [128, 32], F32) for _ in range(2)]; dkrtm = [Dep(), Dep()]
            vtb = [sb("vtba", [128, D], BF16) for _ in range(2)]; dvtb = [Dep(), Dep()]
            it = 0
            pr = 0
            for ti, (col0, n, seq) in enumerate(tiles):
                load_x_and_norm(nb, ti, l, first_layer, amix, 0)
                hrhs = lambda kk: nb.hT[:, kk, :n]
                for j in range(3):
                    pi = pr % 6; pr += 1; ps = psum[pi]; dp = dps[pi]
                    mmk(ps, dp, wdn, j * 128, 128, hrhs, NCH, n, [dw, nb.dhT])
                    k.op(k.act, lambda j=j, ps=ps: nc.scalar.activation(out=cqf[:, j, :n], in_=ps[:, :n], func=AF.Copy), reads=[dp], writes=[dcqf])
                for j in range(2):
                    pi = pr % 6; pr += 1; ps = psum[pi]; dp = dps[pi]
                    mmk(ps, dp, wdn, 384 + j * 128, 128, hrhs, NCH, n, [dw, nb.dhT])
                    k.op(k.act, lambda j=j, ps=ps: nc.scalar.activation(out=ckf[:, j, :n], in_=ps[:, :n], func=AF.Copy), reads=[dp], writes=[dckf])
                pi = pr % 6; pr += 1; ps1 = psum[pi]; dp1 = dps[pi]
                mmk(ps1, dp1, wkr, 0, 96, hrhs, NCH, n, [dw, nb.dhT])
                pi = pr % 6; pr += 1; ps2 = psum[pi]; dp2 = dps[pi]
                mmk(ps2, dp2, wkrs, 0, 96, hrhs, NCH, n, [dw, nb.dhT])
                s_ = it % 2; it += 1
                k.op(k.dve, lambda: nc.vector.tensor_tensor(out=t1[s_][64:96, :n], in0=ps1[64:96, :n], in1=rc[64:96, col0:col0 + n], op=ALU.mult),
                     reads=[dp1, dw], writes=[dt1[s_]])
                k.op(k.dve, lambda: nc.vector.tensor_tensor(out=t2[s_][64:96, :n], in0=ps2[64:96, :n], in1=rsn[64:96, col0:col0 + n], op=ALU.mult),
                     reads=[dp2, dw], writes=[dt2[s_]])
                k.op(k.pool, lambda: nc.gpsimd.tensor_tensor(out=krf[64:96, :n], in0=t1[s_][64:96, :n], in1=t2[s_][64:96, :n], op=ALU.add),
                     reads=[dt1[s_], dt2[s_]], writes=[dkrf])
                k.op(k.pool, lambda: nc.gpsimd.tensor_copy(out=krb[64:96, :n], in_=krf[64:96, :n]), reads=[dkrf], writes=[dkrb])
                k.dma(STQ[0], kT_s[D:D + 32, col0:col0 + n], krb[64:96, :n], reads=[dkrb], writes=[dkk])
                rstd(nb, lambda c: cqf[:, c, :n], dcqf, 3, n)
                apply_norm(nb, lambda c: cqf[:, c, :n], dcqf, lambda c: cqn[:, c, :n], dcqn, 3, n, lambda c: gq[:, c:c + 1], None)
                rstd(nb, lambda c: ckf[:, c, :n], dckf, 2, n)
                apply_norm(nb, lambda c: ckf[:, c, :n], dckf, lambda c: ckn32[:, c, :n], dckn32, 2, n, lambda c: gkv[:, c:c + 1], None)
                k.op(k.pool, lambda: nc.gpsimd.tensor_copy(out=cknb[:, :, :n], in_=ckn32[:, :, :n]), reads=[dckn32], writes=[dcknb])
                for h in range(16):
                    pi = pr % 6; pr += 1; ps1 = psum[pi]; dp1 = dps[pi]
                    mmk(ps1, dp1, wuq, h * 96, 96, lambda kk: cqn[:, kk, :n], 3, n, [dw, dcqn])
                    pi = pr % 6; pr += 1; ps2 = psum[pi]; dp2 = dps[pi]
                    mmk(ps2, dp2, wuqs, h * 96, 96, lambda kk: cqn[:, kk, :n], 3, n, [dw, dcqn])
                    s_ = it % 2; it += 1
                    k.op(k.act, lambda h=h, ps1=ps1: nc.scalar.activation(out=qall[0:64, h, :n], in_=ps1[0:64, :n], func=AF.Copy), reads=[dp1], writes=[dqall])
                    k.op(k.dve, lambda s_=s_, ps1=ps1: nc.vector.tensor_tensor(out=t1[s_][64:96, :n], in0=ps1[64:96, :n], in1=rc[64:96, col0:col0 + n], op=ALU.mult),
                         reads=[dp1, dw], writes=[dt1[s_]])
                    k.op(k.dve, lambda s_=s_, ps2=ps2: nc.vector.tensor_tensor(out=t2[s_][64:96, :n], in0=ps2[64:96, :n], in1=rsn[64:96, col0:col0 + n], op=ALU.mult),
                         reads=[dp2, dw], writes=[dt2[s_]])
                    k.op(k.pool, lambda s_=s_, h=h: nc.gpsimd.tensor_tensor(out=qall[64:96, h, :n], in0=t1[s_][64:96, :n], in1=t2[s_][64:96, :n], op=ALU.add),
                         reads=[dt1[s_], dt2[s_]], writes=[dqall])
                k.dma(STQ[0], qT_s[:, col0:col0 + n].rearrange("(h r) t -> r h t", r=96), qall[0:96, :, :n], reads=[dqall], writes=[dq])
                for m in range(NCH):
                    pi = pr % 6; pr += 1; ps = psum[pi]; dp = dps[pi]
                    mmk(ps, dp, wuk, m * 128, 128, lambda kk: cknb[:, kk, :n], 2, n, [dw, dcknb])
                    k.op(k.act, lambda m=m, ps=ps: nc.scalar.activation(out=ko[:, m, :n], in_=ps[:, :n], func=AF.Copy), reads=[dp], writes=[dko])
                k.dma(STQ[0], kT_s[0:D, col0:col0 + n].rearrange("(c p) t -> p c t", p=128), ko[:, :, :n], reads=[dko], writes=[dkk])
                for s0 in range(0, n, 128):
                    nt = min(128, n - s0)
                    tok0 = col0 + s0
                    b_ = it % 2; it += 1
                    for fb in range(2):
                        pi = pr % 6; pr += 1; ps = psum[pi]; dp = dps[pi]
                        for c in range(2):
                            mm(ps[:nt, :512], cknb[:, c, s0:s0 + nt], wuv[:, c, fb * 512:(fb + 1) * 512], c == 0, c == 1, [dw, dcknb], [dp], last=(c == 1))
                        k.op(k.dve, lambda fb=fb, ps=ps: nc.vector.tensor_copy(out=vtb[b_][:nt, fb * 512:(fb + 1) * 512], in_=ps[:nt, :512]),
                             reads=[dp], writes=[dvtb[b_]])
                    k.dma(STQ[0], v_s[tok0:tok0 + nt, :], vtb[b_][:nt, :], reads=[dvtb[b_]], writes=[dv])
                    for c in range(2):
                        pi = pr % 6; pr += 1; ps = psum[pi]; dp = dps[pi]
                        mm(ps[:nt, :128], ckn32[:, c, s0:s0 + nt], ident[:, :], True, True, [dckn32, d_const], [dp])
                        k.op(k.act, lambda c=c, ps=ps: nc.scalar.activation(out=ctm[b_][:nt, c * 128:(c + 1) * 128], in_=ps[:nt, :128], func=AF.Copy),
                             reads=[dp], writes=[dctm[b_]])
                    k.dma(STQ[0], O["a_ckv"][tok0:tok0 + nt, :], ctm[b_][:nt, :], reads=[dctm[b_]])
                    pi = pr % 6; pr += 1; ps = psum[pi]; dp = dps[pi]
                    mm(ps[:nt, :32], krf[64:96, s0:s0 + nt], ident[64:96, 64:96], True, True, [dkrf, d_const], [dp])
                    k.op(k.act, lambda ps=ps: nc.scalar.activation(out=krtm[b_][:nt, :], in_=ps[:nt, :32], func=AF.Copy), reads=[dp], writes=[dkrtm[b_]])
                    k.dma(STQ[0], O["a_kr"][tok0:tok0 + nt, :], krtm[b_][:nt, :], reads=[dkrtm[b_]])
            k.barrier()
        cur[0] = None

    def phase_attn_b(lnext):
        Lc = P
        STQ[0] = k.sp
        with ExitStack() as es:
            cur[0] = es
            mt, dm = load_masks("maskB", 4)
            mtp, dmp = load_masks("maskBp", 4)
            ms = sb("maskBs", [32, 2, 512], BF16)
            with ExitStack() as es2:
                cur[0] = es2
                st = sb("mstg2", [32, 2, 512], F32); dst_ = Dep()
                k.dma(k.sp, st[:, 0, :], I["maskBs"], writes=[dst_])
                k.dma(k.sp, st[:, 1, :], I["maskBsp"], writes=[dst_])
                k.op(k.dve, lambda: nc.vector.tensor_copy(out=ms[:], in_=st[:]), reads=[dst_], writes=[dm])
                k.barrier()
            cur[0] = es
            abs_ = [AttnBufs(Lc, 64) for _ in range(2)]
            for ab in abs_:
                ab.zero_q()
            Kn = [sb("Kn", [128, TT + 2 * Lc], BF16) for _ in range(2)]; dKn = [Dep(), Dep()]
            NE, NL, NA = 3, 4, 3
            et = [sb("et", [128, 512], F32) for _ in range(NE)]; det = [Dep() for _ in range(NE)]
            Lt = [sb("Lt", [128, 512], BF16) for _ in range(NL)]; dLt = [Dep() for _ in range(NL)]
            at = [sb("at", [128, 512], BF16) for _ in range(NA)]; dat = [Dep() for _ in range(NA)]
            Ra = [sb("Ra", [128, 512], BF16) for _ in range(2)]; dRa = [Dep(), Dep()]
            ot = [sb("otb", [128, 512], BF16) for _ in range(2)]; dot_ = [Dep(), Dep()]
            c = {"s": 0, "c": 0, "o": 0, "l": 0, "e": 0, "a": 0}

            seq = []
            tstep = [0]

            def advance():
                t = tstep[0]; tstep[0] += 1
                if t < len(seq):
                    seq[t][0]()
                if 0 <= t - 2 < len(seq):
                    seq[t - 2][2]()
                if t < len(seq):
                    seq[t][1]()
                if 0 <= t - 3 < len(seq):
                    seq[t - 3][3]()

            pstore = []

            def flush():
                while tstep[0] < len(seq) + 3:
                    advance()
                seq.clear(); tstep[0] = 0
                while pstore:
                    pstore.pop(0)()

            def run_q(ab, Knb, dKnb, pbase, hsel, h, qcol0, nq, blocks):
                oc = c["o"]; c["o"] += 1
                ri = oc % 2
                psO = psum[6 + ri]; dO = dps[6 + ri]
                nb_ = len(blocks)
                for j, (kcol, nk, slot, mneg, mpos) in enumerate(blocks):
                    st = {}

                    def fA(kcol=kcol, nk=nk, mneg=mneg, st=st):
                        si = c["s"] % 3; c["s"] += 1
                        psS = psum[si]; dS = dps[si]
                        mm(psS[:nk, :nq], ab.Kf[:, kcol:kcol + nk], ab.Qz[pbase // 64][:, qcol0:qcol0 + nq],
                           True, mneg is None, [ab.dK, ab.dQ], [dS], last=(mneg is None))
                        if mneg is not None:
                            mm(psS[:nk, :nq], ident_bf[:nk, :nk], mneg[:nk, :nq], False, True, [d_const, dm], [dS])
                        ei = c["e"] % NE; c["e"] += 1
                        li = c["l"] % NL; c["l"] += 1
                        k.op(k.act, lambda: nc.scalar.activation(out=et[ei][:nk, :nq], in_=psS[:nk, :nq], func=AF.Exp, scale=0.125),
                             reads=[dS], writes=[det[ei]])
                        st["ei"] = ei; st["li"] = li

                    def fL(nk=nk, st=st):
                        ei = st["ei"]; li = st["li"]
                        k.op(k.act, lambda: nc.scalar.activation(out=Lt[li][:nk, :nq], in_=et[ei][:nk, :nq], func=AF.Ln, bias=1.0),
                             reads=[det[ei]], writes=[dLt[li]])

                    def f2(j=j, kcol=kcol, nk=nk, mpos=mpos, st=st):
                        li = st["li"]
                        ci = 3 + c["c"] % 3; c["c"] += 1
                        psC = psum[ci]; dC = dps[ci]
                        mm(psC[:nk, :nq], tri_bf[:nk, :nk], Lt[li][:nk, :nq], True, False, [d_const, dLt[li]], [dC], last=False)
                        if j > 0:
                            mm(psC[:nk, :nq], ones_bf[:, :nk], Ra[j % 2][:, :nq], False, False, [d_const, dRa[j % 2]], [dC], last=False)
                        if mpos is not None:
                            mm(psC[:nk, :nq], ident_bf[:nk, :nk], mpos[:nk, :nq], False, False, [d_const, dm, dmp], [dC], last=False)
                        mm(psC[:nk, :nq], Knb[:, kcol:kcol + nk], ab.Qz[pbase // 64][:, qcol0:qcol0 + nq], False, True,
                           [dKnb, ab.dQ], [dC])
                        ai = c["a"] % NA; c["a"] += 1
                        k.op(k.act, lambda: nc.scalar.activation(out=at[ai][:nk, :nq], in_=psC[:nk, :nq], func=AF.Exp, scale=-1.0),
                             reads=[dC], writes=[dat[ai]])
                        st["ai"] = ai
                        rn = (j + 1) % 2
                        if j + 1 < nb_:
                            if j == 0:
                                if nk < 128:
                                    k.op(k.dve, lambda: nc.vector.memset(Ra[rn][:, :nq], 0.0), writes=[dRa[rn]])
                                k.op(k.dve, lambda: nc.vector.tensor_copy(out=Ra[rn][:nk, :nq], in_=Lt[li][:nk, :nq]), reads=[dLt[li]], writes=[dRa[rn]])
                            else:
                                k.op(k.dve, lambda: nc.vector.tensor_tensor(out=Ra[rn][:, :nq], in0=Ra[j % 2][:, :nq], in1=Lt[li][:, :nq], op=ALU.add),
                                     reads=[dRa[j % 2], dLt[li]], writes=[dRa[rn]])

                    def f3(j=j, nk=nk, slot=slot, st=st):
                        ai = st["ai"]
                        mm(psO[:, :nq], ab.Va[:nk, slot, :, :], at[ai][:nk, :nq], j == 0, j == nb_ - 1, [ab.dV, dat[ai]], [dO], last=(j == nb_ - 1))
                        if j == nb_ - 1:
                            k.op(k.dve, lambda: nc.vector.tensor_copy(out=ot[ri][pbase:pbase + 64, :nq], in_=psO[pbase:pbase + 64, :nq]), reads=[dO], writes=[dot_[ri]])
                            while pstore:
                                pstore.pop(0)()
                            pstore.append(lambda: k.dma(STQ[0], attnT[h * 64:(h + 1) * 64, qcol0:qcol0 + nq], ot[ri][pbase:pbase + 64, :nq], reads=[dot_[ri]], writes=[dattn]))

                    seq.append((fA, fL, f2, f3))
                    advance()

            def load_unit(u):
                ab = abs_[u % 2]
                load_K(ab, 0, u * 128, 128)
                ab.load_q_pair(u)
                load_V(ab, 0, u * 128)
                load_V(ab, 1, u * 128 + 64)
                k.op(k.pool, lambda u=u, ab=ab: nc.gpsimd.tensor_scalar(out=Kn[u % 2][:, :], in0=ab.Kf[:, :], scalar1=-0.125, scalar2=1.0, op0=ALU.mult, op1=ALU.mult),
                     reads=[ab.dK], writes=[dKn[u % 2]])

            bg = AdaBg([], 4, [7])
            load_unit(0)
            bg.load(2)
            for u in range(8):
                ab = abs_[u % 2]
                if u + 1 < 8:
                    flush()
                    load_unit(u + 1)
                bg.step(2)
                for (pbase, hsel, h) in [(0, 0, 2 * u), (64, 1, 2 * u + 1)]:
                    for qi in range(NPT):
                        blocks = []
                        for kt in range(4 * qi + 3, -1, -1):
                            dg = kt >= 4 * qi
                            blocks.append((kt * 128, 128, kt, mt[:, kt - 4 * qi, :] if dg else None, mtp[:, kt - 4 * qi, :] if dg else None))
                        run_q(ab, Kn[u % 2], dKn[u % 2], pbase, hsel, h, qi * 512, 512, blocks)
                    for b in range(2):
                        blocks = [(S + b * T_S, T_S, ab.slot_new(b), ms[:, 0, :], ms[:, 1, :])]
                        for t in range(ab.ncs - 1, -1, -1):
                            blocks.append((ab.kcol(b, t * 128), 128, ab.slot_cache(b, t), None, None))
                        run_q(ab, Kn[u % 2], dKn[u % 2], pbase, hsel, h, S + b * T_S, T_S, blocks)
            flush()
            bg.finish()
            k.barrier()
        cur[0] = None
        STQ[0] = k.act

    one_t = sb("one_t", [128, 1], F32)
    k.op(k.pool, lambda: nc.gpsimd.memset(one_t[:], 1.0), writes=[d_const])

    phase_adaln0()
    wo_names = ["w_a_o", "w_b_o", "w_c_o", "w_d_o"]
    for l in range(depth):
        mix = l % 4
        first = (l == 0)
        if mix == 0:
            phase_proj_a(l, first)
            phase_attn_softmax(0, l + 1)
        elif mix == 1:
            phase_qkv(l, 1, first)
            phase_attn_b(l + 1)
        else:
            phase_qkv(l, mix, first)
            phase_attn_softmax(mix, l + 1)
        with ExitStack() as eL:
            cur[0] = eL
            win = sb("win", [128, NCH, 2 * HID], BF16); dwin = Dep()
            phase_oproj(l, I[wo_names[mix]], first, win, dwin)
            phase_ffn(l, l == depth - 1, win, dwin)
        cur[0] = None
    phase_final()
    k.barrier()
    return nc, k


ROPE_THETA = 500000.0


def _consts(S, P):
    TT = S + 64
    c = {}
    c["ident"] = np.eye(128, dtype=np.float32)
    j = np.arange(128)[:, None]; kk = np.arange(128)[None, :]
    c["tri"] = (j >= kk).astype(np.float32)
    c["antiI"] = (j + kk == 127).astype(np.float32)
    p = np.arange(128)[:, None]; ql = np.arange(512)[None, :]
    qc = ql // 64

    def kc_of(r):
        return np.floor_divide(128 * r + p, 64)
    c["maskA"] = (np.stack([(kc_of(r) <= qc) for r in range(4)]).astype(np.float32) - 1.0) * 30000.0
    mB = np.stack([((128 * r + p) < ql) for r in range(4)]).astype(np.float32)
    c["maskB"] = (mB - 1.0) * 30000.0
    c["maskBp"] = (1.0 - mB) * 30000.0
    mb = np.zeros((32, 512), np.float32)
    mb[:, :32] = (np.arange(32)[:, None] < np.arange(32)[None, :])
    c["maskBs"] = (mb - 1.0) * 30000.0
    c["maskBsp"] = (1.0 - mb) * 30000.0
    c["maskC"] = (np.stack([((kc_of(r) <= qc) & (kc_of(r) >= qc - 2)) for r in range(-1, 4)]).astype(np.float32) - 1.0) * 30000.0
    c["maskD"] = (np.stack([((kc_of(r) <= qc) & (kc_of(r) >= qc - 8)) for r in range(-4, 4)]).astype(np.float32) - 1.0) * 30000.0
    c["maskD"] = np.ascontiguousarray(c["maskD"][:, ::-1, :])
    pos = np.concatenate([np.arange(S), P + np.arange(32), P + np.arange(32)]).astype(np.float32)
    ca = np.zeros((128, TT), np.float32); sa = np.zeros((128, TT), np.float32)
    inv = (ROPE_THETA ** (-np.arange(16, dtype=np.float32) * 2.0 / 32)).astype(np.float32)
    ang = pos[None, :] * inv[:, None]
    ca[64:80] = np.cos(ang); ca[80:96] = np.cos(ang)
    sa[64:80] = -np.sin(ang); sa[80:96] = np.sin(ang)
    c["ropeA_cos"] = ca; c["ropeA_sin"] = sa
    cc = np.ones((128, TT), np.float32); sc = np.zeros((128, TT), np.float32)
    inv = (ROPE_THETA ** (-np.arange(8, dtype=np.float32) * 2.0 / 16)).astype(np.float32)
    ang = pos[None, :] * inv[:, None]
    for hb in (0, 64):
        cc[hb:hb + 8] = np.cos(ang); cc[hb + 8:hb + 16] = np.cos(ang)
        sc[hb:hb + 8] = -np.sin(ang); sc[hb + 8:hb + 16] = np.sin(ang)
    c["ropeC_cos"] = cc; c["ropeC_sin"] = sc
    lcc = min(128, P)
    U = np.zeros((64, TT + 2 * lcc), np.float32)
    W = np.zeros((64, TT), np.float32)
    cid = np.arange(64)[:, None]
    kpos = np.arange(S)[None, :]
    if S // 64 <= 64:
        U[:, :S] = (kpos // 64 == cid)
        qc_ = kpos // 64
        W[:, :S] = np.where((cid <= qc_) & (cid >= qc_ - 2), 0.0, -30000.0)
    c["chunkU"] = U; c["chunkW"] = W
    return c


def _prep(inp, i, S, P, consts):
    f = np.ascontiguousarray
    m = dict(consts)
    xs = inp["x_sample"]
    m["xT"] = f(np.concatenate([inp["x_prompt"][i].T, xs[2 * i].T, xs[2 * i + 1].T], axis=1))
    cv = np.stack([inp["c_prompt"][i], inp["c_sample"][2 * i], inp["c_sample"][2 * i + 1]])
    m["cT"] = f(cv.reshape(3, 8, 128).transpose(2, 1, 0))
    m["w_mod"] = inp["w_mod"]
    m["bmodT"] = f(inp["b_mod"].reshape(4, 48, 128).transpose(2, 0, 1))
    m["gmixT"] = f(inp["g_mix"].reshape(4, 8, 128).transpose(2, 0, 1))
    m["gffnT"] = f(inp["g_ffn"].reshape(4, 8, 128).transpose(2, 0, 1))
    m["gfinT"] = f(inp["g_final"].reshape(8, 128).T)
    m["w_ffn_in"] = inp["w_ffn_in"]; m["w_ffn_out"] = inp["w_ffn_out"]
    m["w_a_down"] = inp["w_a_down"][0]
    m["gaqT"] = f(inp["g_a_q"][0].reshape(3, 128).T); m["gakvT"] = f(inp["g_a_kv"][0].reshape(2, 128).T)
    m["w_a_uq"] = inp["w_a_uq"][0]
    m["w_a_uk"] = inp["w_a_uk"][0].reshape(256, 1024); m["w_a_uv"] = inp["w_a_uv"][0].reshape(256, 1024)
    m["w_a_o"] = inp["w_a_o"][0]
    sl = slice(2 * i, 2 * i + 2)
    m["cache_a_ckv"] = f(inp["cache_a_ckv"][0, sl]); m["cache_a_kr"] = f(inp["cache_a_krope"][0, sl])
    m["w_b_qkv"] = inp["w_b_qkv"][0]; m["w_b_o"] = inp["w_b_o"][0]
    m["cache_b_k"] = f(inp["cache_b_k"][0, sl]).reshape(2, P, 1024); m["cache_b_v"] = f(inp["cache_b_v"][0, sl]).reshape(2, P, 1024)
    m["w_c_qkv"] = inp["w_c_qkv"][0]
    b = inp["b_c_qkv"][0]
    m["bcT"] = f(b.reshape(12, 128).T)
    idx = np.arange(1536); d = idx % 64
    src = np.where((idx < 1280) & (d < 8), idx + 8, np.where((idx < 1280) & (d >= 8) & (d < 16), idx - 8, idx))
    m["bcswT"] = f(b[src].reshape(12, 128).T)
    m["bc_row"] = f(b.reshape(1, 1536))
    m["sink_c"] = f(inp["sink_c"].reshape(1, 16))
    m["w_c_o"] = inp["w_c_o"][0]
    lc = inp["cache_c_k"].shape[2]
    m["cache_c_k"] = f(inp["cache_c_k"][0, sl]).reshape(2, lc, 256); m["cache_c_v"] = f(inp["cache_c_v"][0, sl]).reshape(2, lc, 256)
    m["w_d_qkv"] = inp["w_d_qkv"][0]; m["rel_bias_d"] = inp["rel_bias_d"][0]; m["w_d_o"] = inp["w_d_o"][0]
    ld = inp["cache_d_k"].shape[2]
    m["cache_d_k"] = f(inp["cache_d_k"][0, sl]).reshape(2, ld, 1024); m["cache_d_v"] = f(inp["cache_d_v"][0, sl]).reshape(2, ld, 1024)
    return {k_: np.ascontiguousarray(v, dtype=np.float32) for k_, v in m.items()}


def _assemble(res, S, P, lc, ld):
    n = len(res)
    f32 = np.float32
    yp = np.stack([r["yT"][:, :S].T for r in res]).astype(f32)
    ys = np.stack([r["yT"][:, S + 32 * b:S + 32 * (b + 1)].T for r in res for b in range(2)]).astype(f32)

    def pp(name, shp):
        return np.stack([r[name][:S].reshape((S,) + shp) for r in res])[None].astype(f32)

    def ss(name, shp):
        return np.stack([r[name][S + 32 * b:S + 32 * (b + 1)].reshape((32,) + shp) for r in res for b in range(2)])[None].astype(f32)

    def p2(name, shp):
        return np.stack([r[name].reshape(shp) for r in res])[None].astype(f32)

    def s2(name, shp):
        return np.stack([r[name][b].reshape(shp) for r in res for b in range(2)])[None].astype(f32)

    kp_c = min(128, S); kp_d = min(512, S)
    return (yp, ys, pp("a_ckv", (256,)), pp("a_kr", (32,)), ss("a_ckv", (256,)), ss("a_kr", (32,)),
            pp("b_k", (16, 64)), pp("b_v", (16, 64)), ss("b_k", (16, 64)), ss("b_v", (16, 64)),
            p2("c_k_p", (kp_c, 4, 64)), p2("c_v_p", (kp_c, 4, 64)), s2("c_k_s", (lc, 4, 64)), s2("c_v_s", (lc, 4, 64)),
            p2("d_k_p", (kp_d, 16, 64)), p2("d_v_p", (kp_d, 16, 64)), s2("d_k_s", (ld, 16, 64)), s2("d_v_s", (ld, 16, 64)))


def kernel(**inputs):
    inp = {k_: np.asarray(v) for k_, v in inputs.items()}
    nb, S = inp["x_prompt"].shape[0], inp["x_prompt"].shape[1]
    P = inp["cache_a_ckv"].shape[2]
    nc, _ = build(S, P)
    consts = _consts(S, P)
    in_maps = [_prep(inp, i, S, P, consts) for i in range(nb)]
    res = run_bass_kernel_spmd(nc, in_maps, core_ids=list(range(nb)))
    return _assemble(res.results, S, P, inp["cache_c_k"].shape[2], inp["cache_d_k"].shape[2])
```
